# Optimizing a Trainium2 kernel written in Bass

```python
import math
import jax, jax.numpy as jnp
from jax import lax
import numpy as np

D_MODEL = 1024
BATCH = 4
SEQ = 4096
DEPTH = 2

D_MIX = 1024
MLA_HEADS = 6
MLA_Q_RANK = 256
MLA_KV_RANK = 128
MLA_NOPE = 64
MLA_ROPE = 32
MLA_V = 64
ROPE_THETA = 10000.0
POOL_WINDOWS = (2, 4, 8, 16)
POOL_GROUP = 64
POOL_WIDTH = POOL_GROUP * len(POOL_WINDOWS)
FOX_HEADS = 6
FOX_HEAD_DIM = 64
FOX_GATE_BIAS_INIT = 2.0
BLOCK_Q = 128
D_FF = 2816
EPS = 1e-6
IN_Q_A = MLA_Q_RANK
IN_KV_A = MLA_KV_RANK
IN_K_ROPE = MLA_ROPE
IN_POOL = POOL_WIDTH
IN_FOX_QKV = 3 * FOX_HEADS * FOX_HEAD_DIM
IN_FOX_F = FOX_HEADS
N_IN = IN_Q_A + IN_KV_A + IN_K_ROPE + IN_POOL + IN_FOX_QKV + IN_FOX_F

kernel_name = "hybrid_mla_pool_fox_macaron"


def rmsnorm(x, g):
    xf = x.astype(jnp.float32)
    y = xf * lax.rsqrt(jnp.mean(xf * xf, axis=-1, keepdims=True) + EPS)
    return y.astype(x.dtype) * g


def rope(x, pos):
    r = x.shape[-1]
    inv_freq = ROPE_THETA ** (-jnp.arange(0, r, 2, dtype=jnp.float32) / r)
    ang = pos.astype(jnp.float32)[:, None] * inv_freq[None, :]
    cos = jnp.cos(ang).astype(x.dtype)
    sin = jnp.sin(ang).astype(x.dtype)
    x1, x2 = x[..., : r // 2], x[..., r // 2:]
    return jnp.concatenate([x1 * cos - x2 * sin, x2 * cos + x1 * sin], axis=-1)


def causal_block_attention(q, k, v, scale, log_decay_cum=None):
    b, h, s, dk = q.shape
    dv = v.shape[-1]
    nb = s // BLOCK_Q
    qb = q.reshape(b, h, nb, BLOCK_Q, dk).transpose(2, 0, 1, 3, 4)
    kpos = jnp.arange(s)
    xs = (jnp.arange(nb), qb)
    if log_decay_cum is not None:
        xs = xs + (log_decay_cum.reshape(b, h, nb, BLOCK_Q).transpose(2, 0, 1, 3),)

    def one_block(args):
        i, q_blk = args[0], args[1]
        sc = jnp.einsum('bhqd,bhkd->bhqk', q_blk, k, preferred_element_type=jnp.float32) * scale
        if log_decay_cum is not None:
            c_blk = args[2]
            sc = sc + c_blk[..., :, None] - log_decay_cum[..., None, :].astype(jnp.float32)
        qpos = i * BLOCK_Q + jnp.arange(BLOCK_Q)
        sc = jnp.where(kpos[None, :] <= qpos[:, None], sc, -jnp.inf)
        p = jax.nn.softmax(sc, axis=-1).astype(v.dtype)
        return jnp.einsum('bhqk,bhkd->bhqd', p, v)

    out = lax.map(one_block, xs)
    return out.transpose(1, 2, 0, 3, 4).reshape(b, h, s, dv)


def mla_mixer(q_a, kv_a, k_rope, q_a_norm, w_q_b, kv_a_norm, w_kv_b, pos):
    b, s, _ = q_a.shape
    q = (rmsnorm(q_a, q_a_norm) @ w_q_b).reshape(b, s, MLA_HEADS, MLA_NOPE + MLA_ROPE).transpose(0, 2, 1, 3)
    q_nope, q_pe = q[..., :MLA_NOPE], rope(q[..., MLA_NOPE:], pos)
    kv = (rmsnorm(kv_a, kv_a_norm) @ w_kv_b).reshape(b, s, MLA_HEADS, MLA_NOPE + MLA_V).transpose(0, 2, 1, 3)
    k_nope, v = kv[..., :MLA_NOPE], kv[..., MLA_NOPE:]
    k_pe = jnp.broadcast_to(rope(k_rope, pos)[:, None], (b, MLA_HEADS, s, MLA_ROPE))
    qf = jnp.concatenate([q_nope, q_pe], axis=-1)
    kf = jnp.concatenate([k_nope, k_pe], axis=-1)
    o = causal_block_attention(qf, kf, v, 1.0 / math.sqrt(MLA_NOPE + MLA_ROPE))
    return o.transpose(0, 2, 1, 3).reshape(b, s, MLA_HEADS * MLA_V)


def pool_mixer(u, pool_w, pool_scale):
    b, s, _ = u.shape
    ng = len(POOL_WINDOWS)
    ug = u.reshape(b, s, ng, POOL_GROUP)
    cs = jnp.cumsum(ug.astype(jnp.float32), axis=1)
    count = jnp.arange(1, s + 1, dtype=jnp.float32)
    means = []
    for g, w in enumerate(POOL_WINDOWS):
        c = cs[:, :, g]
        prev = jnp.pad(c[:, : s - w], ((0, 0), (w, 0), (0, 0)))
        means.append((c - prev) / jnp.minimum(count, float(w))[None, :, None])
    pooled = jnp.stack(means, axis=2).astype(u.dtype) - ug
    y = jnp.einsum('bsgc,gcd->bsgd', pooled, pool_w)
    return y.reshape(b, s, POOL_WIDTH) * pool_scale


def fox_mixer(qkv, f_logit, fox_b_f):
    b, s, _ = qkv.shape
    qkv = qkv.reshape(b, s, 3, FOX_HEADS, FOX_HEAD_DIM).transpose(2, 0, 3, 1, 4)
    q, k, v = qkv[0], qkv[1], qkv[2]
    log_f = jax.nn.log_sigmoid((f_logit + fox_b_f).astype(jnp.float32))
    cum = jnp.cumsum(log_f, axis=1).transpose(0, 2, 1)
    o = causal_block_attention(q, k, v, 1.0 / math.sqrt(FOX_HEAD_DIM), cum)
    return o.transpose(0, 2, 1, 3).reshape(b, s, FOX_HEADS * FOX_HEAD_DIM)


def swiglu(h, w_gu, w_down):
    gu = h @ w_gu
    g, u = gu[..., :D_FF], gu[..., D_FF:]
    return (jax.nn.silu(g) * u) @ w_down


def hybrid_mixing(h, w_in, q_a_norm, w_q_b, kv_a_norm, w_kv_b, pool_w, pool_scale, fox_b_f, w_out, pos):
    z = h @ w_in
    o0 = 0
    o1 = o0 + IN_Q_A
    o2 = o1 + IN_KV_A
    o3 = o2 + IN_K_ROPE
    o4 = o3 + IN_POOL
    o5 = o4 + IN_FOX_QKV
    o6 = o5 + IN_FOX_F
    ya = mla_mixer(z[..., o0:o1], z[..., o1:o2], z[..., o2:o3], q_a_norm, w_q_b, kv_a_norm, w_kv_b, pos)
    yb = pool_mixer(z[..., o3:o4], pool_w, pool_scale)
    yc = fox_mixer(z[..., o4:o5], z[..., o5:o6], fox_b_f)
    return jnp.concatenate([ya, yb, yc], axis=-1) @ w_out


def setup_inputs(seed: int = 0) -> dict:
    key = jax.random.key(seed)
    ks = jax.random.split(key, 24)
    L, D, F = DEPTH, D_MODEL, D_FF
    f32 = jnp.float32

    def nrm(k, shape, fan_in):
        return jax.random.normal(k, shape, f32) * (fan_in ** -0.5)

    def gain(k, shape):
        return 1.0 + 0.02 * jax.random.normal(k, shape, f32)

    return {
        "x": jax.random.normal(ks[0], (BATCH, SEQ, D), f32),
        "ffn1_norm": gain(ks[1], (L, D)),
        "ffn1_w_gu": nrm(ks[2], (L, D, 2 * F), D),
        "ffn1_w_down": nrm(ks[3], (L, F, D), F),
        "mix_norm": gain(ks[4], (L, D)),
        "w_in": nrm(ks[5], (L, D, N_IN), D),
        "q_a_norm": gain(ks[6], (L, MLA_Q_RANK)),
        "w_q_b": nrm(ks[7], (L, MLA_Q_RANK, MLA_HEADS * (MLA_NOPE + MLA_ROPE)), MLA_Q_RANK),
        "kv_a_norm": gain(ks[8], (L, MLA_KV_RANK)),
        "w_kv_b": nrm(ks[9], (L, MLA_KV_RANK, MLA_HEADS * (MLA_NOPE + MLA_V)), MLA_KV_RANK),
        "pool_w": nrm(ks[10], (L, len(POOL_WINDOWS), POOL_GROUP, POOL_GROUP), POOL_GROUP),
        "pool_scale": gain(ks[11], (L, POOL_WIDTH)),
        "fox_b_f": FOX_GATE_BIAS_INIT + 0.5 * jax.random.normal(ks[12], (L, FOX_HEADS), f32),
        "w_out": nrm(ks[13], (L, D_MIX, D), D_MIX),
        "ffn2_norm": gain(ks[14], (L, D)),
        "ffn2_w_gu": nrm(ks[15], (L, D, 2 * F), D),
        "ffn2_w_down": nrm(ks[16], (L, F, D), F),
        "final_norm": gain(ks[17], (D,)),
    }


def reference(x, ffn1_norm, ffn1_w_gu, ffn1_w_down, mix_norm, w_in, q_a_norm, w_q_b, kv_a_norm, w_kv_b,
              pool_w, pool_scale, fox_b_f, w_out, ffn2_norm, ffn2_w_gu, ffn2_w_down, final_norm):
    pos = jnp.arange(x.shape[1], dtype=jnp.int32)
    for l in range(DEPTH):
        x = x + 0.5 * swiglu(rmsnorm(x, ffn1_norm[l]), ffn1_w_gu[l], ffn1_w_down[l])
        x = x + hybrid_mixing(rmsnorm(x, mix_norm[l]), w_in[l], q_a_norm[l], w_q_b[l], kv_a_norm[l], w_kv_b[l],
                              pool_w[l], pool_scale[l], fox_b_f[l], w_out[l], pos)
        x = x + 0.5 * swiglu(rmsnorm(x, ffn2_norm[l]), ffn2_w_gu[l], ffn2_w_down[l])
    return rmsnorm(x, final_norm)
```

```python
from contextlib import ExitStack
import numpy as np
import concourse.bass as bass
import concourse.mybir as mybir
from concourse.bass_utils import run_bass_kernel_spmd

F32 = mybir.dt.float32
BF16 = mybir.dt.bfloat16
AF = mybir.ActivationFunctionType
ALU = mybir.AluOpType

D = 1024
KC = D // 128
DEPTH = 2
B, S = 4, 4096
NCORES = 8
T = S // 2
TS = 512
NTS = T // TS
D_FF = 2816
FC = D_FF // 128
FGROUPS = 2
EPS = 1e-6
NB = T // 128
NH = 6
N_IN = 1830
O_QA, O_KVA, O_KR, O_POOL, O_FQ, O_FK, O_FV, O_FF = 0, 256, 384, 416, 672, 1056, 1440, 1824
NEG = -30000.0
MLA_SCALE = 1.0 / float(np.sqrt(96.0))

ENGINES = ("pe", "act", "dve", "pool", "sp")


class Prog:
    def __init__(self, nc):
        self.nc = nc
        self.ops = []
        self.last_w = {}
        self.readers = {}
        self.fence_op = None
        self.fence_start = 0

    def fence(self, scratch):
        deps = set()
        last = {}
        for i in range(self.fence_start, len(self.ops)):
            o = self.ops[i]
            if o["dma"]:
                if o["inc"] != 1:
                    deps.add(i)
            else:
                last[o["engine"]] = i
        deps.update(last.values())
        oid = self._add("dve", lambda e: e.memset(scratch, 0.0), (), ())
        self.ops[oid]["deps"].update(deps)
        self.fence_op = oid
        self.fence_start = oid
        return oid

    def _add(self, engine, fn, r, w, dma=False, semkey=None, inc=16):
        oid = len(self.ops)
        deps = set()
        if self.fence_op is not None:
            deps.add(self.fence_op)
        for k in r:
            if k in self.last_w:
                deps.add(self.last_w[k])
        for k in w:
            if k in self.last_w:
                deps.add(self.last_w[k])
            deps.update(self.readers.get(k, ()))
        for k in r:
            self.readers.setdefault(k, []).append(oid)
        for k in w:
            self.last_w[k] = oid
            self.readers[k] = []
        self.ops.append(dict(engine=engine, fn=fn, deps=deps, dma=dma, semkey=semkey, inc=inc))
        return oid

    def op(self, engine, fn, r=(), w=()):
        return self._add(engine, fn, tuple(r), tuple(w))

    def dma(self, engine, out, in_, semkey, r=(), w=()):
        return self._add(engine, lambda e: e.dma_start(out=out, in_=in_), tuple(r), tuple(w),
                         dma=True, semkey=semkey)

    def cc(self, fn, semkey, r=(), w=()):
        return self._add("pool", fn, tuple(r), tuple(w), dma=True, semkey=semkey, inc=1)

    def emit(self, final_wait_ops=()):
        nc, ops = self.nc, self.ops
        signalled = set()
        for o in ops:
            for d in o["deps"]:
                do = ops[d]
                if not do["dma"] and (o["dma"] or do["engine"] != o["engine"] or o["engine"] != "pe"):
                    signalled.add(d)
        for d in final_wait_ops:
            if not ops[d]["dma"]:
                signalled.add(d)
        eng_count = {e: 0 for e in ENGINES}
        sig_val, dma_count = {}, {}
        for i, o in enumerate(ops):
            if o["dma"]:
                k = o["semkey"]
                dma_count[k] = dma_count.get(k, 0) + o["inc"]
                sig_val[i] = dma_count[k]
            elif i in signalled:
                eng_count[o["engine"]] += 1
                sig_val[i] = eng_count[o["engine"]]
        semkeys = sorted(set(o["semkey"] for o in ops if o["dma"]), key=str)
        self.n_sems = len(semkeys) + len(ENGINES)
        with ExitStack() as st:
            esem = {e: st.enter_context(nc.semaphore("e_" + e)) for e in ENGINES}
            dsem = {k: st.enter_context(nc.semaphore("d_%d" % j)) for j, k in enumerate(semkeys)}
            block = st.enter_context(nc.Block())
            streams = {e: [i for i, o in enumerate(ops) if o["engine"] == e] for e in ENGINES}

            def run(e, eng):
                waited = {}
                for i in streams[e]:
                    o = ops[i]
                    need = {}
                    for d in o["deps"]:
                        do = ops[d]
                        if do["dma"]:
                            key, sem = ("d", do["semkey"]), dsem[do["semkey"]]
                        else:
                            if do["engine"] == e and not o["dma"] and e == "pe":
                                continue
                            key, sem = ("e", do["engine"]), esem[do["engine"]]
                        if need.get(key, (None, 0))[1] < sig_val[d]:
                            need[key] = (sem, sig_val[d])
                    for key, (sem, v) in need.items():
                        if waited.get(key, 0) >= v:
                            continue
                        waited[key] = v
                        eng.wait_ge(sem, v)
                    ins = o["fn"](eng)
                    if o["dma"]:
                        ins.then_inc(dsem[o["semkey"]], o["inc"])
                    elif i in signalled:
                        ins.then_inc(esem[e], 1)
                if e == "sp":
                    for d in final_wait_ops:
                        do = ops[d]
                        eng.wait_ge(dsem[do["semkey"]] if do["dma"] else esem[do["engine"]], sig_val[d])

            @block.tensor
            def _(eng):
                run("pe", eng)

            @block.scalar
            def _(eng):
                run("act", eng)

            @block.vector
            def _(eng):
                run("dve", eng)

            @block.gpsimd
            def _(eng):
                run("pool", eng)

            @block.sync
            def _(eng):
                run("sp", eng)


def build_nc(stage="full"):
    nc = bass.Bass("TRN2", target_bir_lowering=False)
    xT_d = nc.dram_tensor("xT", [D, T], F32, kind="ExternalInput").ap()
    outT_d = nc.dram_tensor("outT", [D, T], F32, kind="ExternalOutput").ap()
    NG = 3 * DEPTH + 1
    gains_d = nc.dram_tensor("gains", [128, NG, KC], F32, kind="ExternalInput").ap()
    wgu_d = [nc.dram_tensor("ffn%d_w_gu" % i, [DEPTH, D, 2 * D_FF], F32, kind="ExternalInput").ap() for i in (1, 2)]
    wdn_d = [nc.dram_tensor("ffn%d_w_down" % i, [DEPTH, D_FF, D], F32, kind="ExternalInput").ap() for i in (1, 2)]
    FG = FC // FGROUPS
    NW = 3
    din = lambda n, shp: nc.dram_tensor(n, shp, F32, kind="ExternalInput").ap()
    w_in_d = din("w_in", [DEPTH, D, N_IN])
    w_krot_d = din("w_in_krot", [DEPTH, D, 32])
    w_qb_d = din("w_q_b", [DEPTH, 256, 576])
    w_qbrot_d = din("w_q_b_rot", [DEPTH, 256, 192])
    w_kvb_d = din("w_kv_b", [DEPTH, 128, 768])
    poolw_d = din("pool_w", [DEPTH, 256, 64])
    w_out_d = din("w_out", [DEPTH, D, D])
    mvec_d = din("mvec", [128, DEPTH, 8])
    foxb_d = din("fox_b", [128, DEPTH, NH])
    rope_d = din("rope", [128, 2, T])
    mask_d = din("maskT", [128, 4, TS])
    ident_d = din("ident", [128, 128])
    triu_d = din("triu", [128, 128])
    core_d = din("corev", [128, 4])
    pcnt_d = din("pcnt", [128, 2, 16])
    invw_d = din("invw", [128, 2])
    dsc = lambda n, shp, dt: nc.dram_tensor(n, shp, dt, kind="Internal").ap()
    mlaK_in = [[dsc("mlaK_in%d_%d" % (l, g), [3 * 96, T], BF16) for g in range(2)] for l in range(DEPTH)]
    mlaK_out = [[dsc("mlaK_out%d_%d" % (l, g), [2 * 3 * 96, T], BF16) for g in range(2)] for l in range(DEPTH)]
    foxK_in = [dsc("foxK_in%d" % l, [NH * 64, T], BF16) for l in range(DEPTH)]
    foxK_out = [dsc("foxK_out%d" % l, [2 * NH * 64, T], BF16) for l in range(DEPTH)]
    mlaV_in = [dsc("mlaV_in%d" % l, [NH * T, 64], BF16) for l in range(DEPTH)]
    mlaV_out = [dsc("mlaV_out%d" % l, [2 * NH * T, 64], BF16) for l in range(DEPTH)]
    foxV_in = [dsc("foxV_in%d" % l, [NH * T, 64], BF16) for l in range(DEPTH)]
    foxV_out = [dsc("foxV_out%d" % l, [2 * NH * T, 64], BF16) for l in range(DEPTH)]
    misc_in = [dsc("misc_in%d" % l, [128, 128], F32) for l in range(DEPTH)]
    misc_out = [dsc("misc_out%d" % l, [256, 128], F32) for l in range(DEPTH)]
    mlaQ_d = [[dsc("mlaQ%d_%d" % (l, g), [3 * 96, T], BF16) for g in range(2)] for l in range(DEPTH)]
    mlaQ_out = [[dsc("mlaQ_out%d_%d" % (l, g), [2 * 3 * 96, T], BF16) for g in range(2)] for l in range(DEPTH)]
    foxQ_d = [dsc("foxQ%d" % l, [NH * 65, T], BF16) for l in range(DEPTH)]
    foxQ_out = [dsc("foxQ_out%d" % l, [2 * NH * 65, T], BF16) for l in range(DEPTH)]
    rout_in = [[dsc("rout_in%d_%d" % (l, g), [256, T], F32) for g in range(3)] for l in range(DEPTH)]
    rout_out = [[dsc("rout_out%d_%d" % (l, g), [512, T], F32) for g in range(3)] for l in range(DEPTH)]
    PAIRS = [[0, 1], [2, 3], [4, 5], [6, 7]]

    with ExitStack() as st:
        uniq = {"n": 0}

        def sb(name, shape, dt, stack=st):
            uniq["n"] += 1
            return stack.enter_context(nc.sbuf_tensor("%s_%d" % (name, uniq["n"]), shape, dt))

        def ps(name):
            return st.enter_context(nc.psum_tensor(name, [128, TS], F32))

        P = Prog(nc)
        xT = sb("xT_sb", [128, KC, T], F32)
        hT = sb("hT_sb", [128, KC, T], BF16)
        gains = sb("gains_sb", [128, NG, KC], F32)
        ones = sb("ones_sb", [128, 128], BF16)
        scratch = sb("scratch_sb", [128, 8], F32)
        epsb = sb("epsb_sb", [128, 1], F32)
        sq = [sb("sq%d" % i, [128, TS], BF16) for i in range(2)]
        rstd = [sb("rstd%d" % i, [128, TS], F32) for i in range(2)]
        pg = [ps("pg%d" % i) for i in range(2)]
        pu = [ps("pu%d" % i) for i in range(2)]
        py = [ps("py%d" % i) for i in range(2)]
        pn = [ps("pn%d" % i) for i in range(2)]
        cnt = {"n": 0, "w": 0, "g": 0, "y": 0}

        for t in range(NTS):
            P.dma("sp", xT[:, :, t * TS:(t + 1) * TS], xT_d.rearrange("(k p) t -> p k t", p=128)[:, :, t * TS:(t + 1) * TS], ("x", t),
                  w=[("x", c, t) for c in range(KC)])
        P.dma("sp", gains[:], gains_d, "gains", w=["gains"])
        P.op("dve", lambda e: e.memset(ones[:], 1.0 / D), w=["ones"])
        P.op("dve", lambda e: e.memset(epsb[:], EPS), w=["epsb"])

        def rmsnorm(gi, out_fp32=None):
            for t in range(NTS):
                rmsnorm_tile(gi, t, out_fp32)

        def rmsnorm_tile(gi, t, out_fp32=None):
            if True:
                tsl = slice(t * TS, (t + 1) * TS)
                j = cnt["n"] % 2
                cnt["n"] += 1
                for c in range(KC):
                    q = sq[c % 2]
                    P.op("act", lambda e, q=q, c=c, tsl=tsl: e.activation(out=q[:], in_=xT[:, c, tsl], func=AF.Square),
                         r=[("x", c, t)], w=[("sq", c % 2)])
                    P.op("pe", lambda e, q=q, c=c, j=j: e.matmul(pn[j][:], lhsT=ones[:], rhs=q[:], start=(c == 0), stop=(c == KC - 1)),
                         r=[("sq", c % 2), "ones"], w=[("bk", 6 + j)])
                P.op("act", lambda e, j=j: e.activation(out=rstd[j][:], in_=pn[j][:], func=AF.Ln, bias=epsb[:], scale=1.0),
                     r=[("bk", 6 + j), "epsb"], w=[("rstd", j)])
                P.op("act", lambda e, j=j: e.activation(out=rstd[j][:], in_=rstd[j][:], func=AF.Exp, scale=-0.5),
                     r=[("rstd", j)], w=[("rstd", j)])
                for c in range(KC):
                    if out_fp32 is None:
                        P.op("dve", lambda e, c=c, j=j, tsl=tsl: e.scalar_tensor_tensor(
                            out=hT[:, c, tsl], in0=xT[:, c, tsl], scalar=gains[:, gi, c:c + 1], in1=rstd[j][:], op0=ALU.mult, op1=ALU.mult),
                            r=[("x", c, t), ("rstd", j), "gains"], w=[("h", c, t)])
                    else:
                        P.op("dve", lambda e, c=c, j=j, tsl=tsl: e.scalar_tensor_tensor(
                            out=xT[:, c, tsl], in0=xT[:, c, tsl], scalar=gains[:, gi, c:c + 1], in1=rstd[j][:], op0=ALU.mult, op1=ALU.mult),
                            r=[("x", c, t), ("rstd", j), "gains"], w=[("x", c, t)])

        def ffn(l, which, bufs, after_tile=None):
            act, sil, wgu, wdn = bufs
            wg_v = wgu_d[which][l].rearrange("(k p) n -> p k n", p=128)
            wd_d = wdn_d[which][l]
            for fg in range(FGROUPS):
                slot_of = {}

                def load_gu(fi):
                    f = fg * FG + fi
                    s = cnt["w"] % NW
                    cnt["w"] += 1
                    slot_of[fi] = s
                    P.dma("pool", wgu[s][:, :, 0:128], wg_v[:, :, f * 128:(f + 1) * 128], ("wg", s), w=[("wgu", s)])
                    P.dma("pool", wgu[s][:, :, 128:256], wg_v[:, :, D_FF + f * 128:D_FF + (f + 1) * 128], ("wg", s), w=[("wgu", s)])

                for fi in range(NW):
                    load_gu(fi)
                P.dma("pool", wdn[:], wd_d[fg * FG * 128:(fg + 1) * FG * 128, :].rearrange("(f p) d -> p f d", p=128), "wd",
                      w=[("wdn", fi) for fi in range(FG)])
                for fi in range(FG):
                    f = fg * FG + fi
                    if fi + NW - 1 < FG and fi > 0:
                        load_gu(fi + NW - 1)
                    s = slot_of[fi]
                    for t in range(NTS):
                        tsl = slice(t * TS, (t + 1) * TS)
                        j = cnt["g"] % 2
                        cnt["g"] += 1
                        for c in range(KC):
                            P.op("pe", lambda e, c=c, s=s, j=j, tsl=tsl: e.matmul(pg[j][:], lhsT=wgu[s][:, c, 0:128], rhs=hT[:, c, tsl], start=(c == 0), stop=(c == KC - 1)),
                                 r=[("wgu", s), ("h", c, t)], w=[("pg", j)])
                        for c in range(KC):
                            P.op("pe", lambda e, c=c, s=s, j=j, tsl=tsl: e.matmul(pu[j][:], lhsT=wgu[s][:, c, 128:256], rhs=hT[:, c, tsl], start=(c == 0), stop=(c == KC - 1)),
                                 r=[("wgu", s), ("h", c, t)], w=[("pu", j)])
                        P.op("act", lambda e, j=j: e.activation(out=sil[j][:], in_=pg[j][:], func=AF.Silu),
                             r=[("pg", j)], w=[("sil", j)])
                        P.op("dve", lambda e, j=j, fi=fi, tsl=tsl: e.tensor_tensor(out=act[:, fi, tsl], in0=pu[j][:], in1=sil[j][:], op=ALU.mult),
                             r=[("pu", j), ("sil", j)], w=[("act", fi, t)])
                for t in range(NTS):
                    tsl = slice(t * TS, (t + 1) * TS)
                    for dc in range(KC):
                        j = cnt["y"] % 2
                        cnt["y"] += 1
                        for fi in range(FG):
                            P.op("pe", lambda e, fi=fi, dc=dc, j=j, tsl=tsl: e.matmul(py[j][:], lhsT=wdn[:, fi, dc * 128:(dc + 1) * 128], rhs=act[:, fi, tsl], start=(fi == 0), stop=(fi == FG - 1)),
                                 r=[("wdn", fi), ("act", fi, t)], w=[("py", j)])
                        P.op("dve", lambda e, dc=dc, j=j, tsl=tsl: e.scalar_tensor_tensor(
                            out=xT[:, dc, tsl], in0=py[j][:], scalar=0.5, in1=xT[:, dc, tsl], op0=ALU.mult, op1=ALU.add),
                            r=[("py", j), ("x", dc, t)], w=[("x", dc, t)])
                    if after_tile is not None and fg == FGROUPS - 1:
                        after_tile(t)

        def ffn_phase(l, which, gi, pre_normed=False, after_tile=None):
            with ExitStack() as ph:
                act = sb("act_sb", [128, FG, T], BF16, ph)
                sil = [sb("sil%d" % i, [128, TS], F32, ph) for i in range(2)]
                wgu = [sb("wgu%d" % i, [128, KC, 256], BF16, ph) for i in range(NW)]
                wdn = sb("wdn_sb", [128, FG, D], BF16, ph)
                if not pre_normed:
                    rmsnorm(gi)
                ffn(l, which, (act, sil, wgu, wdn), after_tile)
                P.fence(scratch[:, 0:1])

        ident = sb("ident_sb", [128, 128], BF16)
        triu = sb("triu_sb", [128, 128], F32)
        onesf = sb("onesf_sb", [128, 128], F32)
        maskb = sb("mask_sb", [128, 4, TS], BF16)
        corev = sb("corev_sb", [128, 4], F32)
        mvec = sb("mvec_sb", [128, DEPTH, 8], F32)
        foxb = sb("foxb_sb", [128, DEPTH, NH], F32)
        pcnt = sb("pcnt_sb", [128, 2, 16], F32)
        invw = sb("invw_sb", [128, 2], F32)
        oneb = sb("oneb_sb", [128, 1], F32)
        zerob = sb("zerob_sb", [128, 1], F32)
        P.dma("pool", ident[:], ident_d, "c_ident", w=["ident"])
        P.dma("pool", maskb[:], mask_d, "c_mask", w=["maskb"])
        P.dma("sp", triu[:], triu_d, "c_triu", w=["triu"])
        P.dma("sp", corev[:], core_d, "c_core", w=["corev"])
        P.dma("sp", mvec[:], mvec_d, "c_mvec", w=["mvec"])
        P.dma("sp", foxb[:], foxb_d, "c_foxb", w=["foxb"])
        P.dma("sp", pcnt[:], pcnt_d, "c_pcnt", w=["pcnt"])
        P.dma("sp", invw[:], invw_d, "c_invw", w=["invw"])
        P.op("dve", lambda e: e.reciprocal(out=pcnt[:], in_=pcnt[:]), r=["pcnt"], w=["pcnt"])
        P.op("dve", lambda e: e.memset(onesf[:], 1.0), w=["onesf"])
        P.op("dve", lambda e: e.memset(oneb[:], 1.0), w=["oneb"])
        P.op("dve", lambda e: e.memset(zerob[:], 0.0), w=["zerob"])
        banks = pg + pu + py + pn
        bk = {"i": 0}

        def bank():
            i = bk["i"] % 8
            bk["i"] += 1
            return i

        def mm(i, out, lhsT, rhs, start, stop, r):
            P.op("pe", lambda e: e.matmul(out, lhsT=lhsT, rhs=rhs, start=start, stop=stop), r=list(r), w=[("bk", i)])

        def mix_phase(l, gi, parts=("mla", "pool", "fox"), pre_normed=False, after_tile=None):
            if not pre_normed:
                rmsnorm(gi)
            tsls = [slice(t * TS, (t + 1) * TS) for t in range(NTS)]
            with ExitStack() as mx:
                u32 = sb("u32_sb", [128, 2, 16 + T], F32, mx)
                csall = sb("cs_sb", [128, NB, NH], F32, mx)
                def pool_mixer(pp):
                    poolw = sb("poolw_sb", [128, 2, 64], BF16, pp)
                    sA = sb("sA_sb", [128, 16 + TS], F32, pp)
                    sB = sb("sB_sb", [128, 16 + TS], F32, pp)
                    pl = [sb("pl%d" % i, [128, TS], BF16, pp) for i in range(2)]
                    fx = sb("fx_sb", [128, 16], F32, pp)
                    P.dma("pool", poolw[:], poolw_d[l].rearrange("(m p) d -> p m d", p=128), "poolw", w=["poolw"])
                    for m in range(2):
                        P.dma("sp", u32[:, m, 0:16], misc_out[l][0:128, 96 + m * 16:96 + (m + 1) * 16], ("halo", m), r=[("cc", "misc")], w=[("halo", m)])
                        P.op("dve", lambda e, m=m: e.tensor_scalar(out=u32[:, m, 0:16], in0=u32[:, m, 0:16], scalar1=corev[:, 1:2], scalar2=None, op0=ALU.mult),
                             r=[("halo", m), "corev"], w=[("halo", m)])
                    def pm_tile(t):
                        for m in range(2):
                            a = u32[:, m, t * TS:t * TS + 16 + TS]
                            W = 16 + TS
                            rd = [("u32", m, t), ("halo", m)] + ([("u32", m, t - 1)] if t else [])
                            P.op("dve", lambda e, a=a: e.tensor_tensor(out=sA[:, 1:W], in0=a[:, 1:W], in1=a[:, 0:W - 1], op=ALU.add), r=rd, w=["sA"])
                            plm = pl[m]
                            pk = ("pl", m)
                            def fin(src, rows, skey, a=a, m=m, t=t, plm=plm, pk=pk):
                                P.op("dve", lambda e: e.scalar_tensor_tensor(out=plm[rows, :], in0=src[rows, 16:W], scalar=invw[rows, m:m + 1], in1=a[rows, 16:W],
                                                                           op0=ALU.mult, op1=ALU.subtract), r=[skey, "invw"], w=[pk])
                                if t == 0:
                                    P.op("dve", lambda e: e.tensor_tensor(out=fx[rows, :], in0=src[rows, 16:32], in1=pcnt[rows, m, :], op=ALU.mult),
                                         r=[skey, "pcnt"], w=["fx"])
                                    P.op("dve", lambda e: e.tensor_tensor(out=plm[rows, 0:16], in0=fx[rows, :], in1=a[rows, 16:32], op=ALU.subtract),
                                         r=["fx"], w=[pk])
                            if m == 0:
                                fin(sA, slice(0, 64), "sA")
                            P.op("dve", lambda e: e.tensor_tensor(out=sB[:, 3:W], in0=sA[:, 3:W], in1=sA[:, 1:W - 2], op=ALU.add), r=["sA"], w=["sB"])
                            if m == 0:
                                fin(sB, slice(64, 128), "sB")
                            else:
                                P.op("dve", lambda e: e.tensor_tensor(out=sA[:, 7:W], in0=sB[:, 7:W], in1=sB[:, 3:W - 4], op=ALU.add), r=["sB"], w=["sA"])
                                fin(sA, slice(0, 64), "sA")
                                P.op("dve", lambda e: e.tensor_tensor(out=sB[:, 15:W], in0=sA[:, 15:W], in1=sA[:, 7:W - 8], op=ALU.add), r=["sA"], w=["sB"])
                                fin(sB, slice(64, 128), "sB")
                            for gg in range(2):
                                g = m * 2 + gg
                                rows = slice(gg * 64, gg * 64 + 64)
                                i = bank()
                                mm(i, banks[i][0:64, :], poolw[rows, m, :], plm[rows, :], True, True, ["poolw", pk])
                                P.op("dve", lambda e, i=i, g=g, rows=rows, m=m, t=t: e.tensor_scalar(out=hT[rows, 3 + m, tsls[t]], in0=banks[i][0:64, :], scalar1=mvec[0:64, l, 3 + g:4 + g],
                                                                                              scalar2=None, op0=ALU.mult),
                                     r=[("bk", i), "mvec"], w=[("mix", 3 + m, t, gg * 64), ("h", 3 + m, t)])
                    return pm_tile

                with ExitStack() as pa:
                    win = sb("win_sb", [128, KC, N_IN], BF16, pa)
                    wkrot = sb("wkrot_sb", [128, KC, 32], BF16, pa)
                    wqb = sb("wqb_sb", [128, 2, 576], BF16, pa)
                    wqbrot = sb("wqbrot_sb", [128, 2, 192], BF16, pa)
                    wkvb = sb("wkvb_sb", [128, 768], BF16, pa)
                    ropet = [sb("rope%d" % i, [128, 2, TS], F32, pa) for i in range(2)]
                    qn = [sb("qn%d" % i, [128, 2, TS], BF16, pa) for i in range(2)]
                    kvn = [sb("kvn%d" % i, [128, TS], BF16, pa) for i in range(2)]
                    sqm = [sb("sqm%d" % i, [128, TS], BF16, pa) for i in range(2)]
                    rsm = [sb("rsm%d" % i, [128, TS], F32, pa) for i in range(2)]
                    r1 = sb("r1_sb", [128, TS], F32, pa)
                    r2 = sb("r2_sb", [128, TS], F32, pa)
                    NQS, NFS = 4, 2
                    qst = [sb("qst%d" % i, [96, TS], BF16, pa) for i in range(NQS)]
                    kst = [sb("kst%d" % i, [96, TS], BF16, pa) for i in range(NQS)]
                    fqst = [sb("fqst%d" % i, [128, TS], BF16, pa) for i in range(NFS)]
                    fkst = [sb("fkst%d" % i, [128, TS], BF16, pa) for i in range(NFS)]
                    drow = [sb("drow%d" % i, [8, TS], BF16, pa) for i in range(2)]
                    vst = [sb("vst%d" % i, [128, NH, 64], BF16, pa) for i in range(4)]
                    fb = sb("fb_sb", [128, NH], F32, pa)
                    spb = [sb("sp%d" % i, [128, NH], F32, pa) for i in range(2)]
                    spacc = sb("spacc_sb", [128, NH], F32, pa)
                    misc = sb("misc_sb", [128, 128], F32, pa)
                    P.dma("pool", win[:], w_in_d[l].rearrange("(k p) n -> p k n", p=128), "win", w=[("win", c) for c in range(KC)])
                    P.dma("pool", wkrot[:], w_krot_d[l].rearrange("(k p) n -> p k n", p=128), "wkrot", w=["wkrot"])
                    P.dma("pool", wqb[:], w_qb_d[l].rearrange("(k p) n -> p k n", p=128), "wqb", w=["wqb"])
                    P.dma("pool", wqbrot[:], w_qbrot_d[l].rearrange("(k p) n -> p k n", p=128), "wqbrot", w=["wqbrot"])
                    P.dma("pool", wkvb[:], w_kvb_d[l], "wkvb", w=["wkvb"])
                    P.op("pool", lambda e: e.memset(spacc[:], 0.0), w=["spacc"])
                    WIN = [("win", c) for c in range(KC)]
                    cn = {"q": 0, "k": 0, "fq": 0, "fk": 0, "v": 0, "sp": 0}

                    def proj(i, out, col0, ncol, t, wt=None):
                        for c in range(KC):
                            lh = (win if wt is None else wt)[:, c, col0:col0 + ncol]
                            mm(i, out, lh, hT[:, c, tsls[t]], c == 0, c == KC - 1, WIN + ["wkrot", ("h", c, t)])

                    def subnorm(t, srcs, mean_scale, gcols, dst, dkeys):
                        j = t % 2
                        ib = bank()
                        for m, (i, ap) in enumerate(srcs):
                            P.op("act", lambda e, ap=ap, m=m: e.activation(out=sqm[m % 2][:], in_=ap, func=AF.Square),
                                 r=[("bk", i)], w=[("sqm", m % 2)])
                            mm(ib, banks[ib][:], ones[:], sqm[m % 2][:], m == 0, m == len(srcs) - 1, [("sqm", m % 2), "ones"])
                        P.op("act", lambda e: e.activation(out=rsm[j][:], in_=banks[ib][:], func=AF.Ln, bias=epsb[:], scale=mean_scale),
                             r=[("bk", ib), "epsb"], w=[("rsm", j)])
                        P.op("act", lambda e: e.activation(out=rsm[j][:], in_=rsm[j][:], func=AF.Exp, scale=-0.5),
                             r=[("rsm", j)], w=[("rsm", j)])
                        for m, (i, ap) in enumerate(srcs):
                            P.op("dve", lambda e, ap=ap, m=m: e.scalar_tensor_tensor(out=dst[m], in0=ap, scalar=mvec[:, l, gcols[m]:gcols[m] + 1],
                                                                                  in1=rsm[j][:], op0=ALU.mult, op1=ALU.mult),
                                 r=[("bk", i), ("rsm", j), "mvec"], w=[dkeys[m]])

                    def rope_rows(ia, ib_, rp, out_list, okeys, t1, t2):
                        P.op("dve", lambda e: e.tensor_tensor(out=t1[64:96, :], in0=banks[ia][64:96, :], in1=rp[64:96, 0, :], op=ALU.mult),
                             r=[("bk", ia), "ropet"], w=[("t1", id(t1))])
                        P.op("dve", lambda e: e.tensor_tensor(out=t2[64:96, :], in0=banks[ib_][64:96, :], in1=rp[64:96, 1, :], op=ALU.mult),
                             r=[("bk", ib_), "ropet"], w=[("t2", id(t2))])
                        for o, ok in zip(out_list, okeys):
                            P.op("dve", lambda e, o=o: e.tensor_tensor(out=o, in0=t1[64:96, :], in1=t2[64:96, :], op=ALU.add),
                                 r=[("t1", id(t1)), ("t2", id(t2))], w=[ok])

                    m8all = sb("m8all_sb", [128, NB, NH], BF16, pa)

                    def evac_v(i, dst_d, blk):
                        vt = vst[cn["v"] % 4]
                        vk = ("vst", cn["v"] % 4)
                        cn["v"] += 1
                        P.op("act", lambda e: e.activation(out=vt[:].rearrange("p h d -> p (h d)"), in_=banks[i][:, 0:NH * 64], func=AF.Copy),
                             r=[("bk", i)], w=[vk])
                        name = "foxV_in" if dst_d is foxV_in else "mlaV_in"
                        P.dma("pool", dst_d[l].rearrange("(h t) d -> t h d", h=NH)[blk * 128:(blk + 1) * 128], vt[:], vk, r=[vk], w=[(name, blk)])

                    def tile_kv(t):
                        tsl = tsls[t]
                        rp = ropet[t % 2]
                        P.dma("sp", rp[:], rope_d[:, :, tsl], ("rope", t % 2), w=["ropet"])
                        ic = bank()
                        proj(ic, banks[ic][:], O_KVA, 128, t)
                        kj = kvn[t % 2]
                        subnorm(t, [(ic, banks[ic][:])], 8.0, [2], [kj[:]], [("kvn", t % 2)])
                        for bi in range(4):
                            blk = t * 4 + bi
                            bsl = slice(blk * 128, (blk + 1) * 128)
                            i = bank()
                            for c in range(KC):
                                mm(i, banks[i][:, 0:NH], hT[:, c, bsl], win[:, c, O_FF:O_FF + NH], c == 0, c == KC - 1, WIN + [("h", c, t)])
                            sp_ = spb[cn["sp"] % 2]
                            spk = ("sp", cn["sp"] % 2)
                            cn["sp"] += 1
                            P.op("dve", lambda e, i=i: e.tensor_tensor(out=fb[:], in0=banks[i][:, 0:NH], in1=foxb[:, l, :], op=ALU.add),
                                 r=[("bk", i), "foxb"], w=["fb"])
                            P.op("act", lambda e: e.activation(out=fb[:], in_=fb[:], func=AF.Exp, scale=-1.0), r=["fb"], w=["fb"])
                            P.op("act", lambda e, sp_=sp_: e.activation(out=sp_[:], in_=fb[:], func=AF.Ln, bias=oneb[:], scale=1.0),
                                 r=["fb", "oneb"], w=[spk])
                            i = bank()
                            for c in range(KC):
                                mm(i, banks[i][:, 0:NH * 64], hT[:, c, bsl], win[:, c, O_FV:O_FV + NH * 64], c == 0, c == KC - 1, WIN + [("h", c, t)])
                            evac_v(i, foxV_in, blk)
                            i2 = bank()
                            mm(i2, banks[i2][:, 0:NH], triu[:], sp_[:], True, False, ["triu", spk])
                            mm(i2, banks[i2][:, 0:NH], onesf[:], spacc[:], False, True, ["onesf", "spacc"])
                            P.op("act", lambda e, i2=i2, blk=blk: e.activation(out=csall[:, blk, :], in_=banks[i2][:, 0:NH], func=AF.Copy),
                                 r=[("bk", i2)], w=[("cs", blk)])
                            P.op("pool", lambda e, sp_=sp_: e.tensor_tensor(out=spacc[:], in0=spacc[:], in1=sp_[:], op=ALU.add),
                                 r=[spk, "spacc"], w=["spacc"])
                            P.op("dve", lambda e, blk=blk: e.tensor_scalar(out=m8all[:, blk, :], in0=csall[:, blk, :], scalar1=-8.0, scalar2=None, op0=ALU.mult),
                                 r=[("cs", blk)], w=[("m8", blk)])
                        for bi in range(4):
                            i = bank()
                            for h in range(NH):
                                mm(i, banks[i][:, h * 64:(h + 1) * 64], kj[:, bi * 128:(bi + 1) * 128], wkvb[:, h * 128 + 64:h * 128 + 128], True, True, [("kvn", t % 2), "wkvb"])
                            evac_v(i, mlaV_in, t * 4 + bi)
                        ika, ikb = bank(), bank()
                        proj(ika, banks[ika][64:96, :], O_KR, 32, t)
                        proj(ikb, banks[ikb][64:96, :], 0, 32, t, wt=wkrot)
                        K1, K2 = ("t1", id(r1)), ("t2", id(r2))
                        P.op("dve", lambda e: e.tensor_tensor(out=r1[64:96, :], in0=banks[ika][64:96, :], in1=rp[64:96, 0, :], op=ALU.mult),
                             r=[("bk", ika), "ropet"], w=[K1])
                        P.op("dve", lambda e: e.tensor_tensor(out=r2[64:96, :], in0=banks[ikb][64:96, :], in1=rp[64:96, 1, :], op=ALU.mult),
                             r=[("bk", ikb), "ropet"], w=[K2])
                        for h in range(NH):
                            kt = kst[cn["k"] % NQS]
                            kk = ("kst", cn["k"] % NQS)
                            cn["k"] += 1
                            i = bank()
                            mm(i, banks[i][0:64, :], wkvb[:, h * 128:h * 128 + 64], kj[:], True, True, [("kvn", t % 2), "wkvb"])
                            P.op("act", lambda e, i=i, kt=kt: e.activation(out=kt[0:64, :], in_=banks[i][0:64, :], func=AF.Copy), r=[("bk", i)], w=[kk])
                            P.op("dve", lambda e, kt=kt: e.tensor_tensor(out=kt[64:96, :], in0=r1[64:96, :], in1=r2[64:96, :], op=ALU.add), r=[K1, K2], w=[kk])
                            P.dma(("sp" if h % 2 == 0 else "pool"), mlaK_in[l][h // 3][(h % 3) * 96:(h % 3 + 1) * 96, tsl], kt[:], kk, r=[kk], w=[("mlaK_in", h, t)])
                            if h % 2 == 1:
                                hp = h // 2
                                ft = fkst[cn["fk"] % NFS]
                                fk = ("fkst", cn["fk"] % NFS)
                                cn["fk"] += 1
                                i = bank()
                                proj(i, banks[i][:], O_FK + hp * 128, 128, t)
                                P.op("act", lambda e, i=i, ft=ft: e.activation(out=ft[:], in_=banks[i][:], func=AF.Copy), r=[("bk", i)], w=[fk])
                                P.dma(("sp" if hp % 2 == 0 else "pool"), foxK_in[l][hp * 128:(hp + 1) * 128, tsl], ft[:], fk, r=[fk],
                                      w=[("foxK_in", h - 1, t), ("foxK_in", h, t)])
                        for m in range(2):
                            i = bank()
                            proj(i, banks[i][:], O_POOL + m * 128, 128, t)
                            P.op("act", lambda e, i=i, m=m: e.activation(out=u32[:, m, 16 + t * TS:16 + (t + 1) * TS], in_=banks[i][:], func=AF.Copy),
                                 r=[("bk", i)], w=[("u32", m, t)])

                    def qa_norm(t):
                        ia, ib2 = bank(), bank()
                        proj(ia, banks[ia][:], O_QA, 128, t)
                        proj(ib2, banks[ib2][:], O_QA + 128, 128, t)
                        qj = qn[t % 2]
                        subnorm(t, [(ia, banks[ia][:]), (ib2, banks[ib2][:])], 4.0, [0, 1], [qj[:, 0, :], qj[:, 1, :]], [("qn", t % 2, 0), ("qn", t % 2, 1)])

                    def tile_q_mla(t):
                        tsl = tsls[t]
                        rp = ropet[t % 2]
                        P.dma("sp", rp[:], rope_d[:, :, tsl], ("rope", t % 2), w=["ropet"])
                        if t + 1 < NTS:
                            qa_norm(t + 1)
                        qj = qn[t % 2]
                        QN = [("qn", t % 2, 0), ("qn", t % 2, 1)]
                        for h in range(NH):
                            qt_ = qst[cn["q"] % NQS]
                            qk = ("qst", cn["q"] % NQS)
                            cn["q"] += 1
                            i, ir = bank(), bank()
                            for m in range(2):
                                mm(i, banks[i][0:96, :], wqb[:, m, h * 96:(h + 1) * 96], qj[:, m, :], m == 0, m == 1, QN + ["wqb"])
                            for m in range(2):
                                mm(ir, banks[ir][64:96, :], wqbrot[:, m, h * 32:(h + 1) * 32], qj[:, m, :], m == 0, m == 1, QN + ["wqbrot"])
                            P.op("act", lambda e, i=i, qt_=qt_: e.activation(out=qt_[0:64, :], in_=banks[i][0:64, :], func=AF.Copy), r=[("bk", i)], w=[qk])
                            rope_rows(i, ir, rp, [qt_[64:96, :]], [qk], r1, r2)
                            P.dma(("sp" if h % 2 == 0 else "pool"), mlaQ_d[l][h // 3][(h % 3) * 96:(h % 3 + 1) * 96, tsl], qt_[:], qk, r=[qk], w=[("mlaQ", h, t)])

                    def tile_q_fox(t):
                        tsl = tsls[t]
                        i2 = bank()
                        for bi in range(4):
                            blk = t * 4 + bi
                            mm(i2, banks[i2][0:NH, bi * 128:(bi + 1) * 128], m8all[:, blk, :], ident[:], True, True, [("m8", blk), "ident"])
                        dr = drow[t % 2]
                        dk = ("drow", t % 2)
                        P.op("act", lambda e: e.activation(out=dr[0:NH, :], in_=banks[i2][0:NH, :], func=AF.Copy), r=[("bk", i2)], w=[dk])
                        P.dma("pool", foxQ_d[l].rearrange("(h r) t -> r h t", r=65)[64, :, tsl], dr[0:NH, :], dk, r=[dk], w=[("foxQd", t)])
                        for hp in range(NH // 2):
                            fq = fqst[cn["fq"] % NFS]
                            fqk = ("fqst", cn["fq"] % NFS)
                            cn["fq"] += 1
                            i = bank()
                            proj(i, banks[i][:], O_FQ + hp * 128, 128, t)
                            P.op("act", lambda e, i=i, fq=fq: e.activation(out=fq[:], in_=banks[i][:], func=AF.Copy), r=[("bk", i)], w=[fqk])
                            for k_ in range(2):
                                h = 2 * hp + k_
                                P.dma(("sp" if k_ == 0 else "pool"), foxQ_d[l][h * 65:h * 65 + 64, tsl], fq[k_ * 64:(k_ + 1) * 64, :], fqk, r=[fqk], w=[("foxQ", h, t)])

                    for t in range(NTS):
                        tile_kv(t)
                    it = bank()
                    mm(it, banks[it][:, 0:NH], onesf[:], spacc[:], True, True, ["onesf", "spacc"])
                    for blk in range(NB):
                        P.op("dve", lambda e, blk=blk: e.tensor_tensor(out=misc[:, blk * NH:(blk + 1) * NH], in0=csall[:, blk, :], in1=banks[it][:, 0:NH], op=ALU.subtract),
                             r=[("cs", blk), ("bk", it)], w=["misc"])
                    for m in range(2):
                        P.op("dve", lambda e, m=m: e.tensor_copy(out=misc[:, 96 + m * 16:96 + (m + 1) * 16], in_=u32[:, m, T:T + 16]),
                             r=[("u32", m, NTS - 1)], w=["misc"])
                    P.dma("sp", misc_in[l], misc[:], "misc", r=["misc"], w=["misc_in"])
                    def gather(name, src, dst, rkeys):
                        P.cc(lambda g: g.collective_compute("AllGather", ALU.bypass, replica_groups=PAIRS, ins=[src], outs=[dst]),
                             ("cc", name, l), r=rkeys, w=[("cc", name)])
                    HT = [(h, t) for h in range(NH) for t in range(NTS)]
                    gather("misc", misc_in[l], misc_out[l], ["misc_in"])
                    gather("mlaK0", mlaK_in[l][0], mlaK_out[l][0], [("mlaK_in", h, t) for h, t in HT if h < 3])
                    gather("mlaK1", mlaK_in[l][1], mlaK_out[l][1], [("mlaK_in", h, t) for h, t in HT if h >= 3])
                    gather("mlaV", mlaV_in[l], mlaV_out[l], [("mlaV_in", b_) for b_ in range(NB)])
                    gather("foxK", foxK_in[l], foxK_out[l], [("foxK_in", h, t) for h, t in HT])
                    gather("foxV", foxV_in[l], foxV_out[l], [("foxV_in", b_) for b_ in range(NB)])
                    pm_tile = pool_mixer(pa) if "pool" in parts else (lambda t: None)
                    qa_norm(0)
                    for t in range(NTS):
                        tile_q_mla(t)
                    gather("mlaQ0", mlaQ_d[l][0], mlaQ_out[l][0], [("mlaQ", h, t) for h, t in HT if h < 3])
                    gather("mlaQ1", mlaQ_d[l][1], mlaQ_out[l][1], [("mlaQ", h, t) for h, t in HT if h >= 3])
                    for t in range(NTS):
                        tile_q_fox(t)
                        pm_tile(t)
                    gather("foxQ", foxQ_d[l], foxQ_out[l], [("foxQ", h, t) for h, t in HT] + [("foxQd", t) for t in range(NTS)])
                    P.fence(scratch[:, 1:2])
                with ExitStack() as pb:
                    Kh = [sb("Kh%d" % i, [96, T], BF16, pb) for i in range(2)]
                    Vh = [sb("Vh%d" % i, [128, NB, 128], BF16, pb) for i in range(2)]
                    Qh = [sb("Qh%d" % i, [96, T], BF16, pb) for i in range(2)]
                    Kc = [sb("Kc%d" % i, [96, T], BF16, pb) for i in range(2)]
                    Vc = [sb("Vc%d" % i, [128, NB, 64], BF16, pb) for i in range(2)]
                    Qc = [sb("Qc%d" % i, [96, T], BF16, pb) for i in range(2)]
                    pt = [sb("pt%d" % i, [128, TS], BF16, pb) for i in range(3)]
                    rct = sb("rct_sb", [128, TS], F32, pb)
                    comb = sb("comb_sb", [128, TS], F32, pb)
                    part = [sb("part%d" % i, [128, TS], F32, pb) for i in range(2)]
                    rst = [sb("rst%d" % i, [128, TS], F32, pb) for i in range(2)]
                    negraw = sb("negraw_sb", [128, NB, NH], F32, pb)
                    negsel = sb("negsel_sb", [128, NB, 3], F32, pb)
                    fA, fB = corev[:, 2:3], corev[:, 1:2]
                    for i in range(2):
                        P.op("pool", lambda e, i=i: e.memset(Vh[i][:, :, 64:128], 1.0), w=[("Vh", i)])
                        P.op("pool", lambda e, i=i: e.memset(Kh[i][64:65, :], 1.0), w=[("Kh", i)])
                    P.dma("sp", negraw[:].rearrange("p b h -> p (b h)"), misc_out[l][0:128, 0:NB * NH], "negraw", r=[("cc", "misc")], w=["negraw"])
                    P.op("dve", lambda e: e.tensor_scalar(out=negsel[:], in0=negraw[:, :, 0:3], scalar1=fA, scalar2=None, op0=ALU.mult), r=["negraw", "corev"], w=["negsel"])
                    P.op("dve", lambda e: e.scalar_tensor_tensor(out=negsel[:], in0=negraw[:, :, 3:6], scalar=fB, in1=negsel[:], op0=ALU.mult, op1=ALU.add),
                         r=["negraw", "corev", "negsel"], w=["negsel"])
                    ptn = {"i": 0}
                    pon = {"i": 0}
                    last_fam = {0: None, 1: None}

                    def blend(dst, ca, cb, srca, srcb, skey, rkeys, dkey, np_):
                        fa, fb = corev[0:np_, 2:3], corev[0:np_, 1:2]
                        P.dma("sp", ca, srca, (skey, "a"), r=rkeys, w=[(skey, "a")])
                        P.dma("sp", cb, srcb, (skey, "b"), r=rkeys, w=[(skey, "b")])
                        P.op("act", lambda e: e.activation(out=dst, in_=ca, func=AF.Copy, scale=fa), r=[(skey, "a"), "corev"], w=[dkey])
                        P.op("dve", lambda e: e.scalar_tensor_tensor(out=dst, in0=cb, scalar=fb, in1=dst, op0=ALU.mult, op1=ALU.add), r=[(skey, "b"), "corev", dkey], w=[dkey])

                    def fix_k_row(sl, fam):
                        if fam == "fox" and last_fam[sl] == "mla":
                            P.op("pool", lambda e: e.memset(Kh[sl][64:65, :], 1.0), r=[("Kh", sl)], w=[("Kh", sl)])
                        last_fam[sl] = fam

                    def attend(sl, fam, kdq, qt, blocks, bias_of, ipo):
                        tsl = tsls[qt]
                        sc = MLA_SCALE if fam == "mla" else 0.125
                        pend = []

                        def score(kb, diag):
                            i = bk["i"] % 6
                            bk["i"] += 1
                            q0 = max(diag, 0) * 128
                            ksl = slice(kb * 128, (kb + 1) * 128)
                            qsl = slice(tsl.start + q0, tsl.stop)
                            P.op("pe", lambda e: e.matmul(banks[i][:, q0:], lhsT=Kh[sl][0:kdq, ksl], rhs=Qh[sl][0:kdq, qsl], start=True, stop=(diag < 0)),
                                 r=[("Kh", sl), ("Qh", sl)], w=[("bk", i)])
                            if diag >= 0:
                                P.op("pe", lambda e: e.matmul(banks[i][:, q0:], lhsT=ident[:], rhs=maskb[:, diag, q0:], start=False, stop=True),
                                     r=["ident", "maskb"], w=[("bk", i)])
                            p_ = ptn["i"] % 3
                            ptn["i"] += 1
                            bias, bkeys = bias_of(kb)
                            P.op("act", lambda e: e.activation(out=pt[p_][:, q0:], in_=banks[i][:, q0:], func=AF.Exp, bias=bias, scale=sc),
                                 r=[("bk", i)] + bkeys, w=[("pt", p_)])
                            return kb, p_, q0

                        def pv(kb, p_, q0, first, last):
                            P.op("pe", lambda e: e.matmul(banks[ipo][:, q0:], lhsT=Vh[sl][:, kb, :], rhs=pt[p_][:, q0:], start=first, stop=last),
                                 r=[("Vh", sl), ("pt", p_)], w=[("bk", ipo)])

                        first_kb = blocks[0][0]
                        for kb, diag in blocks:
                            pend.append(score(kb, diag))
                            if len(pend) > 2:
                                k0, p0, c0 = pend.pop(0)
                                pv(k0, p0, c0, k0 == first_kb, False)
                        while pend:
                            k0, p0, c0 = pend.pop(0)
                            pv(k0, p0, c0, k0 == first_kb, not pend)

                    items = []
                    rect = ([("mla", j) for j in range(3)] if "mla" in parts else []) + ([("fox", j) for j in range(3)] if "fox" in parts else [])
                    heads = ([("mla", h) for h in range(NH)] if "mla" in parts else []) + ([("fox", h) for h in range(NH)] if "fox" in parts else [])
                    pn_ = {"i": 0}

                    def make_rect(fam, j, sl):
                        jj = j if fam == "mla" else 3 + j
                        kd, kdq = (96, 96) if fam == "mla" else (64, 65)
                        if fam == "mla":
                            Ka, Kb_ = mlaK_out[l][0][j * 96:(j + 1) * 96, :], mlaK_out[l][1][j * 96:(j + 1) * 96, :]
                            Qa, Qb_ = mlaQ_out[l][0][288 + j * 96:288 + (j + 1) * 96, :], mlaQ_out[l][1][288 + j * 96:288 + (j + 1) * 96, :]
                            Vv = mlaV_out[l].rearrange("(h b p) d -> p h b d", h=2 * NH, b=NB)
                            rk, rq, rv = [("cc", "mlaK0"), ("cc", "mlaK1")], [("cc", "mlaQ0"), ("cc", "mlaQ1")], [("cc", "mlaV")]
                        else:
                            Ka, Kb_ = foxK_out[l][j * 64:(j + 1) * 64, :], foxK_out[l][(j + 3) * 64:(j + 4) * 64, :]
                            Qa, Qb_ = foxQ_out[l][390 + j * 65:390 + (j + 1) * 65, :], foxQ_out[l][390 + (j + 3) * 65:390 + (j + 4) * 65, :]
                            Vv = foxV_out[l].rearrange("(h b p) d -> p h b d", h=2 * NH, b=NB)
                            rk, rq, rv = [("cc", "foxK")], [("cc", "foxQ")], [("cc", "foxV")]
                        specs = [(Kh[sl][0:kd, :], Kc[0][0:kd, :], Kc[1][0:kd, :], Ka, Kb_, "Kc", rk, ("Kh", sl), kd),
                                 (Vh[sl][:, :, 0:64], Vc[0][:], Vc[1][:], Vv[:, j], Vv[:, j + 3], "Vc", rv, ("Vh", sl), 128),
                                 (Qh[sl][0:kdq, :], Qc[0][0:kdq, :], Qc[1][0:kdq, :], Qa, Qb_, "Qc", rq, ("Qh", sl), kdq)]

                        def loads():
                            for dst, ca, cb, srca, srcb, skey, rkeys, dkey, np_ in specs:
                                P.dma("sp", ca, srca, (skey, "a"), r=rkeys, w=[(skey, "a")])
                                P.dma("sp", cb, srcb, (skey, "b"), r=rkeys, w=[(skey, "b")])

                        def prep():
                            for n_, (dst, ca, cb, srca, srcb, skey, rkeys, dkey, np_) in enumerate(specs):
                                fa, fb = corev[0:np_, 2:3], corev[0:np_, 1:2]
                                P.op("dve", lambda e, dst=dst, ca=ca, fa=fa: e.tensor_scalar(out=dst, in0=ca, scalar1=fa, scalar2=None, op0=ALU.mult), r=[(skey, "a"), "corev"], w=[dkey])
                                P.op("dve", lambda e, dst=dst, cb=cb, fb=fb: e.scalar_tensor_tensor(out=dst, in0=cb, scalar=fb, in1=dst, op0=ALU.mult, op1=ALU.add),
                                     r=[(skey, "b"), "corev", dkey], w=[dkey])
                                if n_ == 0:
                                    fix_k_row(sl, fam)

                        def run(mid):
                            for qt in range(NTS):
                                ipo = 6 + pon["i"] % 2
                                pon["i"] += 1
                                if fam == "mla":
                                    bias_of = lambda kb: (zerob[:], ["zerob"])
                                else:
                                    bias_of = lambda kb: (negsel[:, kb, j:j + 1], ["negsel"])
                                attend(sl, fam, kdq, qt, [(kb, -1) for kb in range(NB)], bias_of, ipo)
                                r_ = rst[pon["i"] % 2]
                                rkey = ("rst", pon["i"] % 2)
                                P.op("dve", lambda e, ipo=ipo, r_=r_: e.tensor_copy(out=r_[:], in_=banks[ipo][:]), r=[("bk", ipo)], w=[rkey])
                                P.dma("pool", rout_in[l][jj // 2][(jj % 2) * 128:(jj % 2 + 1) * 128, tsls[qt]], r_[:], rkey, r=[rkey], w=[("rout_in", jj, qt)])
                                if qt == 1:
                                    mid()
                            if jj % 2 == 1:
                                g_ = jj // 2
                                P.cc(lambda g: g.collective_compute("AllGather", ALU.bypass, replica_groups=PAIRS, ins=[rout_in[l][g_]], outs=[rout_out[l][g_]]),
                                     ("cc", "rout%d" % g_, l), r=[("rout_in", x, qt) for x in (jj - 1, jj) for qt in range(NTS)], w=[("cc", "rout", g_)])
                        return loads, prep, run

                    def make_tri(fam, h, sl):
                        kd, kdq = (96, 96) if fam == "mla" else (64, 65)
                        jj = (h % 3) if fam == "mla" else 3 + (h % 3)
                        slot = h // 3
                        if fam == "mla":
                            Ko, Qo, Vo = mlaK_in[l][h // 3][(h % 3) * 96:(h % 3 + 1) * 96, :], mlaQ_d[l][h // 3][(h % 3) * 96:(h % 3 + 1) * 96, :], mlaV_in[l]
                            kn, vn, qn_ = "mlaK_in", "mlaV_in", "mlaQ"
                        else:
                            Ko, Qo, Vo = foxK_in[l][h * 64:(h + 1) * 64, :], foxQ_d[l][h * 65:(h + 1) * 65, :], foxV_in[l]
                            kn, vn, qn_ = "foxK_in", "foxV_in", "foxQ"
                        chunk = (h // 2) if fam == "mla" else (5 + h // 2)
                        prow = (h % 2) * 64

                        def loads():
                            P.dma("sp", Kh[sl][0:kd, :], Ko, ("Kho", sl), r=[(kn, h, t) for t in range(NTS)], w=[("Kh", sl)])
                            fix_k_row(sl, fam)
                            P.dma("sp", Vh[sl][:, :, 0:64], Vo.rearrange("(h b p) d -> p h b d", h=NH, b=NB)[:, h], ("Vho", sl), r=[(vn, b_) for b_ in range(NB)], w=[("Vh", sl)])
                            P.dma("sp", Qh[sl][0:kdq, :], Qo, ("Qho", sl), r=[(qn_, h, t) for t in range(NTS)] + ([("foxQd", t) for t in range(NTS)] if fam == "fox" else []), w=[("Qh", sl)])

                        def prep():
                            pass

                        def run(mid):
                            for qt in range(NTS):
                                ipo = 6 + pon["i"] % 2
                                pon["i"] += 1
                                pa_ = part[pn_["i"] % 2]
                                pkey = ("part", pn_["i"] % 2)
                                pn_["i"] += 1
                                P.dma("sp", pa_[:], rout_out[l][jj // 2][slot * 256 + (jj % 2) * 128:slot * 256 + (jj % 2 + 1) * 128, tsls[qt]], pkey,
                                      r=[("cc", "rout", jj // 2)], w=[pkey])
                                if fam == "mla":
                                    bias_of = lambda kb: (zerob[:], ["zerob"])
                                else:
                                    bias_of = lambda kb: (csall[:, kb, h:h + 1], [("cs", kb)])
                                attend(sl, fam, kdq, qt, [(kb, kb - 4 * qt) for kb in range(4 * (qt + 1))], bias_of, ipo)
                                tsl = tsls[qt]
                                P.op("dve", lambda e, ipo=ipo, pa_=pa_: e.scalar_tensor_tensor(out=comb[:], in0=pa_[:], scalar=fB, in1=banks[ipo][:], op0=ALU.mult, op1=ALU.add),
                                     r=[("bk", ipo), pkey, "corev"], w=["comb"])
                                P.op("dve", lambda e: e.reciprocal(out=rct[0:64, :], in_=comb[64:128, :]), r=["comb"], w=["rct"])
                                P.op("dve", lambda e, tsl=tsl: e.tensor_tensor(out=hT[prow:prow + 64, chunk, tsl], in0=comb[0:64, :], in1=rct[0:64, :], op=ALU.mult),
                                     r=["comb", "rct"], w=[("mix", chunk, qt, prow)])
                                if qt == 1:
                                    mid()
                        return loads, prep, run

                    for n_, (fam, j) in enumerate(rect):
                        items.append(make_rect(fam, j, n_ % 2))
                    for n_, (fam, h) in enumerate(heads):
                        items.append(make_tri(fam, h, (len(rect) + n_) % 2))
                    if items:
                        items[0][0]()
                        items[0][1]()
                    for n_, (loads, prep, run) in enumerate(items):
                        nxt = items[n_ + 1] if n_ + 1 < len(items) else None
                        if nxt:
                            nxt[0]()
                        run(nxt[1] if nxt else (lambda: None))
                    P.fence(scratch[:, 2:3])
                with ExitStack() as pc:
                    wout = sb("wout_sb", [128, KC, D], BF16, pc)
                    P.dma("pool", wout[:], w_out_d[l].rearrange("(k p) n -> p k n", p=128), "wout", w=[("wout", c) for c in range(KC)])
                    zero_chunks = ([] if "mla" in parts else [0, 1, 2]) + ([] if "pool" in parts else [3, 4]) + ([] if "fox" in parts else [5, 6, 7])
                    for c in zero_chunks:
                        P.op("pool", lambda e, c=c: e.memset(hT[:, c, :], 0.0), w=[("mix", c, t, pr) for t in range(NTS) for pr in (0, 64)])
                    for t in range(NTS):
                        for dc in range(KC):
                            i = bank()
                            for c in range(KC):
                                mm(i, banks[i][:], wout[:, c, dc * 128:(dc + 1) * 128], hT[:, c, tsls[t]], c == 0, c == KC - 1, [("wout", c), ("mix", c, t, 0), ("mix", c, t, 64)])
                            P.op("dve", lambda e, i=i, dc=dc, t=t: e.tensor_tensor(out=xT[:, dc, tsls[t]], in0=banks[i][:], in1=xT[:, dc, tsls[t]], op=ALU.add),
                                 r=[("bk", i), ("x", dc, t)], w=[("x", dc, t)])
                        if after_tile is not None:
                            after_tile(t)
                    P.fence(scratch[:, 3:4])

        fin = []
        if stage == "ffn1":
            ffn_phase(0, 0, 0)
        elif stage == "ffn1n":
            ffn_phase(0, 0, 0)
            rmsnorm(NG - 1, out_fp32=True)
        elif stage.startswith("mix_"):
            mix_phase(0, 1, tuple(stage[4:].split("+")))
        elif stage == "full":
            def store_tile(t):
                fin.append(P.dma("sp", outT_d.rearrange("(k p) t -> p k t", p=128)[:, :, t * TS:(t + 1) * TS], xT[:, :, t * TS:(t + 1) * TS], ("o", t),
                                 r=[("x", c, t) for c in range(KC)]))

            def final_tile(t):
                rmsnorm_tile(NG - 1, t, out_fp32=True)
                store_tile(t)

            for l in range(DEPTH):
                ffn_phase(l, 0, 3 * l, pre_normed=(l > 0), after_tile=lambda t, l=l: rmsnorm_tile(3 * l + 1, t))
                mix_phase(l, 3 * l + 1, pre_normed=True, after_tile=lambda t, l=l: rmsnorm_tile(3 * l + 2, t))
                ffn_phase(l, 1, 3 * l + 2, pre_normed=True,
                          after_tile=(lambda t, l=l: rmsnorm_tile(3 * (l + 1), t)) if l + 1 < DEPTH else final_tile)
        if not fin:
            fin = [P.dma("sp", outT_d.rearrange("(k p) t -> p k t", p=128)[:, :, t * TS:(t + 1) * TS], xT[:, :, t * TS:(t + 1) * TS], ("o", t),
                         r=[("x", c, t) for c in range(KC)]) for t in range(NTS)]
        P.emit(final_wait_ops=fin)
    return nc


def _gains(inputs):
    rows = []
    for l in range(DEPTH):
        rows += [inputs["ffn1_norm"][l], inputs["mix_norm"][l], inputs["ffn2_norm"][l]]
    rows.append(inputs["final_norm"])
    g = np.stack([np.asarray(r, np.float32) for r in rows])
    return np.ascontiguousarray(g.reshape(len(rows), KC, 128).transpose(2, 0, 1))


def _constants(half):
    c = {}
    pos = (half * T + np.arange(T)).astype(np.float32)
    inv_freq = (np.float32(10000.0) ** (-(np.arange(0, 32, 2, dtype=np.float32) / np.float32(32)))).astype(np.float32)
    ang = (pos[:, None] * inv_freq[None, :]).astype(np.float32)
    cos, sin = np.cos(ang).astype(np.float32).T, np.sin(ang).astype(np.float32).T
    rope = np.zeros((128, 2, T), np.float32)
    rope[64:80, 0], rope[80:96, 0] = cos, cos
    rope[64:80, 1], rope[80:96, 1] = -sin, sin
    c["rope"] = rope
    k = np.arange(128)[:, None, None]
    r = np.arange(4)[None, :, None]
    q = np.arange(TS)[None, None, :]
    c["maskT"] = np.where(q >= r * 128 + k, 0.0, NEG).astype(np.float32)
    c["ident"] = np.eye(128, dtype=np.float32)
    c["triu"] = np.triu(np.ones((128, 128), np.float32))
    corev = np.zeros((128, 4), np.float32)
    corev[:, 0] = 0.0 if half == 1 else NEG
    corev[:, 1] = 1.0 if half == 1 else 0.0
    corev[:, 2] = 1.0 if half == 0 else 0.0
    c["corev"] = corev
    wins = np.array([2.0, 4.0, 8.0, 16.0], np.float32)
    g_of = (np.arange(2)[None, :] * 2 + (np.arange(128)[:, None] // 64))
    w_of = wins[g_of]
    count = (half * T + np.arange(16) + 1).astype(np.float32)
    c["pcnt"] = np.minimum(count[None, None, :], w_of[:, :, None]).astype(np.float32)
    c["invw"] = (1.0 / w_of).astype(np.float32)
    return c


def _layouts(inputs):
    f = lambda k: np.ascontiguousarray(np.asarray(inputs[k], np.float32))
    m = {k: f(k) for k in ("ffn1_w_gu", "ffn1_w_down", "ffn2_w_gu", "ffn2_w_down", "w_in", "w_q_b", "w_kv_b", "w_out")}
    swap = np.concatenate([np.arange(16, 32), np.arange(0, 16)])
    m["w_in_krot"] = np.ascontiguousarray(m["w_in"][:, :, O_KR:O_KR + 32][:, :, swap])
    qb = m["w_q_b"].reshape(DEPTH, 256, NH, 96)[:, :, :, 64:96][:, :, :, swap]
    m["w_q_b_rot"] = np.ascontiguousarray(qb.reshape(DEPTH, 256, NH * 32))
    m["pool_w"] = np.ascontiguousarray(f("pool_w").reshape(DEPTH, 256, 64))
    mvec = np.zeros((128, DEPTH, 8), np.float32)
    qa, kva, psc, fbf = f("q_a_norm"), f("kv_a_norm"), f("pool_scale"), f("fox_b_f")
    for l in range(DEPTH):
        mvec[:, l, 0], mvec[:, l, 1], mvec[:, l, 2] = qa[l, 0:128], qa[l, 128:256], kva[l]
        for g in range(4):
            mvec[0:64, l, 3 + g] = psc[l, g * 64:(g + 1) * 64]
    m["mvec"] = mvec
    m["fox_b"] = np.ascontiguousarray(np.broadcast_to(fbf[None], (128, DEPTH, NH)))
    m["gains"] = _gains(inputs)
    return m


def kernel(_stage="full", **inputs):
    x = np.asarray(inputs["x"], np.float32)
    nc = build_nc(_stage)
    shared = _layouts(inputs)
    consts = [_constants(0), _constants(1)]
    in_maps = []
    for c in range(NCORES):
        b, h = c // 2, c % 2
        m = dict(shared)
        m.update(consts[h])
        m["xT"] = np.ascontiguousarray(x[b, h * T:(h + 1) * T, :].T)
        in_maps.append(m)
    res = run_bass_kernel_spmd(nc, in_maps, core_ids=list(range(NCORES)))
    out = np.empty((B, S, D), np.float32)
    for c in range(NCORES):
        b, h = c // 2, c % 2
        out[b, h * T:(h + 1) * T, :] = res.results[c]["outT"].T
    return out
```

```python
from contextlib import ExitStack
import numpy as np
import concourse.bass as bass
import concourse.mybir as mybir
from concourse.bass_utils import run_bass_kernel_spmd

F32 = mybir.dt.float32
BF16 = mybir.dt.bfloat16
AF = mybir.ActivationFunctionType
ALU = mybir.AluOpType

D = 1024
KC = D // 128
DEPTH = 2
B, S = 4, 4096
NCORES = 8
T = S // 2
TS = 512
NTS = T // TS
D_FF = 2816
FC = D_FF // 128
FGROUPS = 2
EPS = 1e-6
NB = T // 128
NH = 6
N_IN = 1830
O_QA, O_KVA, O_KR, O_POOL, O_FQ, O_FK, O_FV, O_FF = 0, 256, 384, 416, 672, 1056, 1440, 1824
NEG = -30000.0
MLA_SCALE = 1.0 / float(np.sqrt(96.0))

ENGINES = ("pe", "act", "dve", "pool", "sp")


class Prog:
    def __init__(self, nc):
        self.nc = nc
        self.ops = []
        self.last_w = {}
        self.readers = {}
        self.fence_op = None
        self.fence_start = 0

    def fence(self, scratch):
        deps = set()
        last = {}
        for i in range(self.fence_start, len(self.ops)):
            o = self.ops[i]
            if o["dma"]:
                if o["inc"] != 1:
                    deps.add(i)
            else:
                last[o["engine"]] = i
        deps.update(last.values())
        oid = self._add("dve", lambda e: e.memset(scratch, 0.0), (), ())
        self.ops[oid]["deps"].update(deps)
        self.fence_op = oid
        self.fence_start = oid
        return oid

    def _add(self, engine, fn, r, w, dma=False, semkey=None, inc=16):
        oid = len(self.ops)
        deps = set()
        if self.fence_op is not None:
            deps.add(self.fence_op)
        for k in r:
            if k in self.last_w:
                deps.add(self.last_w[k])
        for k in w:
            if k in self.last_w:
                deps.add(self.last_w[k])
            deps.update(self.readers.get(k, ()))
        for k in r:
            self.readers.setdefault(k, []).append(oid)
        for k in w:
            self.last_w[k] = oid
            self.readers[k] = []
        self.ops.append(dict(engine=engine, fn=fn, deps=deps, dma=dma, semkey=semkey, inc=inc))
        return oid

    def op(self, engine, fn, r=(), w=()):
        return self._add(engine, fn, tuple(r), tuple(w))

    def dma(self, engine, out, in_, semkey, r=(), w=()):
        return self._add(engine, lambda e: e.dma_start(out=out, in_=in_), tuple(r), tuple(w),
                         dma=True, semkey=semkey)

    def cc(self, fn, semkey, r=(), w=()):
        return self._add("pool", fn, tuple(r), tuple(w), dma=True, semkey=semkey, inc=1)

    def emit(self, final_wait_ops=()):
        nc, ops = self.nc, self.ops
        signalled = set()
        for o in ops:
            for d in o["deps"]:
                do = ops[d]
                if not do["dma"] and (o["dma"] or do["engine"] != o["engine"] or o["engine"] != "pe"):
                    signalled.add(d)
        for d in final_wait_ops:
            if not ops[d]["dma"]:
                signalled.add(d)
        eng_count = {e: 0 for e in ENGINES}
        sig_val, dma_count = {}, {}
        for i, o in enumerate(ops):
            if o["dma"]:
                k = o["semkey"]
                dma_count[k] = dma_count.get(k, 0) + o["inc"]
                sig_val[i] = dma_count[k]
            elif i in signalled:
                eng_count[o["engine"]] += 1
                sig_val[i] = eng_count[o["engine"]]
        semkeys = sorted(set(o["semkey"] for o in ops if o["dma"]), key=str)
        self.n_sems = len(semkeys) + len(ENGINES)
        with ExitStack() as st:
            esem = {e: st.enter_context(nc.semaphore("e_" + e)) for e in ENGINES}
            dsem = {k: st.enter_context(nc.semaphore("d_%d" % j)) for j, k in enumerate(semkeys)}
            block = st.enter_context(nc.Block())
            streams = {e: [i for i, o in enumerate(ops) if o["engine"] == e] for e in ENGINES}

            def run(e, eng):
                waited = {}
                for i in streams[e]:
                    o = ops[i]
                    need = {}
                    for d in o["deps"]:
                        do = ops[d]
                        if do["dma"]:
                            key, sem = ("d", do["semkey"]), dsem[do["semkey"]]
                        else:
                            if do["engine"] == e and not o["dma"] and e == "pe":
                                continue
                            key, sem = ("e", do["engine"]), esem[do["engine"]]
                        if need.get(key, (None, 0))[1] < sig_val[d]:
                            need[key] = (sem, sig_val[d])
                    for key, (sem, v) in need.items():
                        if waited.get(key, 0) >= v:
                            continue
                        waited[key] = v
                        eng.wait_ge(sem, v)
                    ins = o["fn"](eng)
                    if o["dma"]:
                        ins.then_inc(dsem[o["semkey"]], o["inc"])
                    elif i in signalled:
                        ins.then_inc(esem[e], 1)
                if e == "sp":
                    for d in final_wait_ops:
                        do = ops[d]
                        eng.wait_ge(dsem[do["semkey"]] if do["dma"] else esem[do["engine"]], sig_val[d])

            @block.tensor
            def _(eng):
                run("pe", eng)

            @block.scalar
            def _(eng):
                run("act", eng)

            @block.vector
            def _(eng):
                run("dve", eng)

            @block.gpsimd
            def _(eng):
                run("pool", eng)

            @block.sync
            def _(eng):
                run("sp", eng)


def build_nc(stage="full"):
    nc = bass.Bass("TRN2", target_bir_lowering=False)
    xT_d = nc.dram_tensor("xT", [D, T], F32, kind="ExternalInput").ap()
    outT_d = nc.dram_tensor("outT", [D, T], F32, kind="ExternalOutput").ap()
    NG = 3 * DEPTH + 1
    gains_d = nc.dram_tensor("gains", [128, NG, KC], F32, kind="ExternalInput").ap()
    wgu_d = [nc.dram_tensor("ffn%d_w_gu" % i, [DEPTH, D, 2 * D_FF], F32, kind="ExternalInput").ap() for i in (1, 2)]
    wdn_d = [nc.dram_tensor("ffn%d_w_down" % i, [DEPTH, D_FF, D], F32, kind="ExternalInput").ap() for i in (1, 2)]
    FG = FC // FGROUPS
    NW = 3
    din = lambda n, shp: nc.dram_tensor(n, shp, F32, kind="ExternalInput").ap()
    w_in_d = din("w_in", [DEPTH, D, N_IN])
    w_krot_d = din("w_in_krot", [DEPTH, D, 32])
    w_qb_d = din("w_q_b", [DEPTH, 256, 576])
    w_qbrot_d = din("w_q_b_rot", [DEPTH, 256, 192])
    w_kvb_d = din("w_kv_b", [DEPTH, 128, 768])
    poolw_d = din("pool_w", [DEPTH, 256, 64])
    w_out_d = din("w_out", [DEPTH, D, D])
    mvec_d = din("mvec", [128, DEPTH, 8])
    foxb_d = din("fox_b", [128, DEPTH, NH])
    rope_d = din("rope", [128, 2, T])
    mask_d = din("maskT", [128, 4, TS])
    ident_d = din("ident", [128, 128])
    triu_d = din("triu", [128, 128])
    core_d = din("corev", [128, 4])
    pcnt_d = din("pcnt", [128, 2, 16])
    invw_d = din("invw", [128, 2])
    dsc = lambda n, shp, dt: nc.dram_tensor(n, shp, dt, kind="Internal").ap()
    mlaK_in = [[dsc("mlaK_in%d_%d" % (l, g), [3 * 96, T], BF16) for g in range(2)] for l in range(DEPTH)]
    mlaK_out = [[dsc("mlaK_out%d_%d" % (l, g), [2 * 3 * 96, T], BF16) for g in range(2)] for l in range(DEPTH)]
    foxK_in = [dsc("foxK_in%d" % l, [NH * 64, T], BF16) for l in range(DEPTH)]
    foxK_out = [dsc("foxK_out%d" % l, [2 * NH * 64, T], BF16) for l in range(DEPTH)]
    mlaV_in = [dsc("mlaV_in%d" % l, [NH * T, 64], BF16) for l in range(DEPTH)]
    mlaV_out = [dsc("mlaV_out%d" % l, [2 * NH * T, 64], BF16) for l in range(DEPTH)]
    foxV_in = [dsc("foxV_in%d" % l, [NH * T, 64], BF16) for l in range(DEPTH)]
    foxV_out = [dsc("foxV_out%d" % l, [2 * NH * T, 64], BF16) for l in range(DEPTH)]
    misc_in = [dsc("misc_in%d" % l, [128, 128], F32) for l in range(DEPTH)]
    misc_out = [dsc("misc_out%d" % l, [256, 128], F32) for l in range(DEPTH)]
    mlaQ_d = [[dsc("mlaQ%d_%d" % (l, g), [3 * 96, T], BF16) for g in range(2)] for l in range(DEPTH)]
    mlaQ_out = [[dsc("mlaQ_out%d_%d" % (l, g), [2 * 3 * 96, T], BF16) for g in range(2)] for l in range(DEPTH)]
    foxQ_d = [dsc("foxQ%d" % l, [NH * 65, T], BF16) for l in range(DEPTH)]
    foxQ_out = [dsc("foxQ_out%d" % l, [2 * NH * 65, T], BF16) for l in range(DEPTH)]
    rout_in = [[dsc("rout_in%d_%d" % (l, g), [256, T], F32) for g in range(3)] for l in range(DEPTH)]
    rout_out = [[dsc("rout_out%d_%d" % (l, g), [512, T], F32) for g in range(3)] for l in range(DEPTH)]
    PAIRS = [[0, 1], [2, 3], [4, 5], [6, 7]]

    with ExitStack() as st:
        uniq = {"n": 0}

        def sb(name, shape, dt, stack=st):
            uniq["n"] += 1
            return stack.enter_context(nc.sbuf_tensor("%s_%d" % (name, uniq["n"]), shape, dt))

        def ps(name):
            return st.enter_context(nc.psum_tensor(name, [128, TS], F32))

        P = Prog(nc)
        xT = sb("xT_sb", [128, KC, T], F32)
        hT = sb("hT_sb", [128, KC, T], BF16)
        gains = sb("gains_sb", [128, NG, KC], F32)
        ones = sb("ones_sb", [128, 128], BF16)
        scratch = sb("scratch_sb", [128, 8], F32)
        epsb = sb("epsb_sb", [128, 1], F32)
        sq = [sb("sq%d" % i, [128, TS], BF16) for i in range(2)]
        rstd = [sb("rstd%d" % i, [128, TS], F32) for i in range(2)]
        pg = [ps("pg%d" % i) for i in range(2)]
        pu = [ps("pu%d" % i) for i in range(2)]
        py = [ps("py%d" % i) for i in range(2)]
        pn = [ps("pn%d" % i) for i in range(2)]
        cnt = {"n": 0, "w": 0, "g": 0, "y": 0}

        for t in range(NTS):
            P.dma("sp", xT[:, :, t * TS:(t + 1) * TS], xT_d.rearrange("(k p) t -> p k t", p=128)[:, :, t * TS:(t + 1) * TS], ("x", t),
                  w=[("x", c, t) for c in range(KC)])
        P.dma("sp", gains[:], gains_d, "gains", w=["gains"])
        P.op("dve", lambda e: e.memset(ones[:], 1.0 / D), w=["ones"])
        P.op("dve", lambda e: e.memset(epsb[:], EPS), w=["epsb"])

        def rmsnorm(gi, out_fp32=None):
            for t in range(NTS):
                rmsnorm_tile(gi, t, out_fp32)

        def rmsnorm_tile(gi, t, out_fp32=None):
            if True:
                tsl = slice(t * TS, (t + 1) * TS)
                j = cnt["n"] % 2
                cnt["n"] += 1
                for c in range(KC):
                    q = sq[c % 2]
                    P.op("act", lambda e, q=q, c=c, tsl=tsl: e.activation(out=q[:], in_=xT[:, c, tsl], func=AF.Square),
                         r=[("x", c, t)], w=[("sq", c % 2)])
                    P.op("pe", lambda e, q=q, c=c, j=j: e.matmul(pn[j][:], lhsT=ones[:], rhs=q[:], start=(c == 0), stop=(c == KC - 1)),
                         r=[("sq", c % 2), "ones"], w=[("bk", 6 + j)])
                P.op("act", lambda e, j=j: e.activation(out=rstd[j][:], in_=pn[j][:], func=AF.Ln, bias=epsb[:], scale=1.0),
                     r=[("bk", 6 + j), "epsb"], w=[("rstd", j)])
                P.op("act", lambda e, j=j: e.activation(out=rstd[j][:], in_=rstd[j][:], func=AF.Exp, scale=-0.5),
                     r=[("rstd", j)], w=[("rstd", j)])
                for c in range(KC):
                    if out_fp32 is None:
                        P.op("dve", lambda e, c=c, j=j, tsl=tsl: e.scalar_tensor_tensor(
                            out=hT[:, c, tsl], in0=xT[:, c, tsl], scalar=gains[:, gi, c:c + 1], in1=rstd[j][:], op0=ALU.mult, op1=ALU.mult),
                            r=[("x", c, t), ("rstd", j), "gains"], w=[("h", c, t)])
                    else:
                        P.op("dve", lambda e, c=c, j=j, tsl=tsl: e.scalar_tensor_tensor(
                            out=xT[:, c, tsl], in0=xT[:, c, tsl], scalar=gains[:, gi, c:c + 1], in1=rstd[j][:], op0=ALU.mult, op1=ALU.mult),
                            r=[("x", c, t), ("rstd", j), "gains"], w=[("x", c, t)])

        def ffn(l, which, bufs, after_tile=None):
            act, sil, wgu, wdn = bufs
            wg_v = wgu_d[which][l].rearrange("(k p) n -> p k n", p=128)
            wd_d = wdn_d[which][l]
            for fg in range(FGROUPS):
                slot_of = {}

                def load_gu(fi):
                    f = fg * FG + fi
                    s = cnt["w"] % NW
                    cnt["w"] += 1
                    slot_of[fi] = s
                    P.dma("pool", wgu[s][:, :, 0:128], wg_v[:, :, f * 128:(f + 1) * 128], ("wg", s), w=[("wgu", s)])
                    P.dma("pool", wgu[s][:, :, 128:256], wg_v[:, :, D_FF + f * 128:D_FF + (f + 1) * 128], ("wg", s), w=[("wgu", s)])

                for fi in range(NW):
                    load_gu(fi)
                P.dma("pool", wdn[:], wd_d[fg * FG * 128:(fg + 1) * FG * 128, :].rearrange("(f p) d -> p f d", p=128), "wd",
                      w=[("wdn", fi) for fi in range(FG)])
                for fi in range(FG):
                    f = fg * FG + fi
                    if fi + NW - 1 < FG and fi > 0:
                        load_gu(fi + NW - 1)
                    s = slot_of[fi]
                    for t in range(NTS):
                        tsl = slice(t * TS, (t + 1) * TS)
                        j = cnt["g"] % 2
                        cnt["g"] += 1
                        for c in range(KC):
                            P.op("pe", lambda e, c=c, s=s, j=j, tsl=tsl: e.matmul(pg[j][:], lhsT=wgu[s][:, c, 0:128], rhs=hT[:, c, tsl], start=(c == 0), stop=(c == KC - 1)),
                                 r=[("wgu", s), ("h", c, t)], w=[("pg", j)])
                        for c in range(KC):
                            P.op("pe", lambda e, c=c, s=s, j=j, tsl=tsl: e.matmul(pu[j][:], lhsT=wgu[s][:, c, 128:256], rhs=hT[:, c, tsl], start=(c == 0), stop=(c == KC - 1)),
                                 r=[("wgu", s), ("h", c, t)], w=[("pu", j)])
                        P.op("act", lambda e, j=j: e.activation(out=sil[j][:], in_=pg[j][:], func=AF.Silu),
                             r=[("pg", j)], w=[("sil", j)])
                        P.op("dve", lambda e, j=j, fi=fi, tsl=tsl: e.tensor_tensor(out=act[:, fi, tsl], in0=pu[j][:], in1=sil[j][:], op=ALU.mult),
                             r=[("pu", j), ("sil", j)], w=[("act", fi, t)])
                for t in range(NTS):
                    tsl = slice(t * TS, (t + 1) * TS)
                    for dc in range(KC):
                        j = cnt["y"] % 2
                        cnt["y"] += 1
                        for fi in range(FG):
                            P.op("pe", lambda e, fi=fi, dc=dc, j=j, tsl=tsl: e.matmul(py[j][:], lhsT=wdn[:, fi, dc * 128:(dc + 1) * 128], rhs=act[:, fi, tsl], start=(fi == 0), stop=(fi == FG - 1)),
                                 r=[("wdn", fi), ("act", fi, t)], w=[("py", j)])
                        P.op("dve", lambda e, dc=dc, j=j, tsl=tsl: e.scalar_tensor_tensor(
                            out=xT[:, dc, tsl], in0=py[j][:], scalar=0.5, in1=xT[:, dc, tsl], op0=ALU.mult, op1=ALU.add),
                            r=[("py", j), ("x", dc, t)], w=[("x", dc, t)])
                    if after_tile is not None and fg == FGROUPS - 1 and t > 0:
                        after_tile(t - 1)
                if after_tile is not None and fg == FGROUPS - 1:
                    after_tile(NTS - 1)

        def ffn_phase(l, which, gi, pre_normed=False, after_tile=None):
            with ExitStack() as ph:
                act = sb("act_sb", [128, FG, T], BF16, ph)
                sil = [sb("sil%d" % i, [128, TS], F32, ph) for i in range(2)]
                wgu = [sb("wgu%d" % i, [128, KC, 256], BF16, ph) for i in range(NW)]
                wdn = sb("wdn_sb", [128, FG, D], BF16, ph)
                if not pre_normed:
                    rmsnorm(gi)
                ffn(l, which, (act, sil, wgu, wdn), after_tile)
                P.fence(scratch[:, 0:1])

        ident = sb("ident_sb", [128, 128], BF16)
        triu = sb("triu_sb", [128, 128], F32)
        onesf = sb("onesf_sb", [128, 128], F32)
        maskb = sb("mask_sb", [128, 4, TS], BF16)
        corev = sb("corev_sb", [128, 4], F32)
        mvec = sb("mvec_sb", [128, DEPTH, 8], F32)
        foxb = sb("foxb_sb", [128, DEPTH, NH], F32)
        pcnt = sb("pcnt_sb", [128, 2, 16], F32)
        invw = sb("invw_sb", [128, 2], F32)
        oneb = sb("oneb_sb", [128, 1], F32)
        zerob = sb("zerob_sb", [128, 1], F32)
        P.dma("pool", ident[:], ident_d, "c_ident", w=["ident"])
        P.dma("pool", maskb[:], mask_d, "c_mask", w=["maskb"])
        P.dma("sp", triu[:], triu_d, "c_triu", w=["triu"])
        P.dma("sp", corev[:], core_d, "c_core", w=["corev"])
        P.dma("sp", mvec[:], mvec_d, "c_mvec", w=["mvec"])
        P.dma("sp", foxb[:], foxb_d, "c_foxb", w=["foxb"])
        P.dma("sp", pcnt[:], pcnt_d, "c_pcnt", w=["pcnt"])
        P.dma("sp", invw[:], invw_d, "c_invw", w=["invw"])
        P.op("dve", lambda e: e.reciprocal(out=pcnt[:], in_=pcnt[:]), r=["pcnt"], w=["pcnt"])
        P.op("dve", lambda e: e.memset(onesf[:], 1.0), w=["onesf"])
        P.op("dve", lambda e: e.memset(oneb[:], 1.0), w=["oneb"])
        P.op("dve", lambda e: e.memset(zerob[:], 0.0), w=["zerob"])
        banks = pg + pu + py + pn
        bk = {"i": 0}

        def bank():
            i = bk["i"] % 8
            bk["i"] += 1
            return i

        def mm(i, out, lhsT, rhs, start, stop, r):
            P.op("pe", lambda e: e.matmul(out, lhsT=lhsT, rhs=rhs, start=start, stop=stop), r=list(r), w=[("bk", i)])

        def mix_phase(l, gi, parts=("mla", "pool", "fox"), pre_normed=False, after_tile=None):
            if not pre_normed:
                rmsnorm(gi)
            tsls = [slice(t * TS, (t + 1) * TS) for t in range(NTS)]
            with ExitStack() as mx:
                u32 = sb("u32_sb", [128, 2, 16 + T], F32, mx)
                csall = sb("cs_sb", [128, NB, NH], F32, mx)
                def pool_mixer(pp):
                    poolw = sb("poolw_sb", [128, 2, 64], BF16, pp)
                    sA = sb("sA_sb", [128, 16 + TS], F32, pp)
                    sB = sb("sB_sb", [128, 16 + TS], F32, pp)
                    pl = [sb("pl%d" % i, [128, TS], BF16, pp) for i in range(2)]
                    fx = sb("fx_sb", [128, 16], F32, pp)
                    P.dma("pool", poolw[:], poolw_d[l].rearrange("(m p) d -> p m d", p=128), "poolw", w=["poolw"])
                    for m in range(2):
                        P.dma("sp", u32[:, m, 0:16], misc_out[l][0:128, 96 + m * 16:96 + (m + 1) * 16], ("halo", m), r=[("cc", "misc")], w=[("halo", m)])
                        P.op("dve", lambda e, m=m: e.tensor_scalar(out=u32[:, m, 0:16], in0=u32[:, m, 0:16], scalar1=corev[:, 1:2], scalar2=None, op0=ALU.mult),
                             r=[("halo", m), "corev"], w=[("halo", m)])
                    def pm_tile(t):
                        for m in range(2):
                            a = u32[:, m, t * TS:t * TS + 16 + TS]
                            W = 16 + TS
                            rd = [("u32", m, t), ("halo", m)] + ([("u32", m, t - 1)] if t else [])
                            P.op("dve", lambda e, a=a: e.tensor_tensor(out=sA[:, 1:W], in0=a[:, 1:W], in1=a[:, 0:W - 1], op=ALU.add), r=rd, w=["sA"])
                            plm = pl[m]
                            pk = ("pl", m)
                            def fin(src, rows, skey, a=a, m=m, t=t, plm=plm, pk=pk):
                                P.op("dve", lambda e: e.scalar_tensor_tensor(out=plm[rows, :], in0=src[rows, 16:W], scalar=invw[rows, m:m + 1], in1=a[rows, 16:W],
                                                                           op0=ALU.mult, op1=ALU.subtract), r=[skey, "invw"], w=[pk])
                                if t == 0:
                                    P.op("dve", lambda e: e.tensor_tensor(out=fx[rows, :], in0=src[rows, 16:32], in1=pcnt[rows, m, :], op=ALU.mult),
                                         r=[skey, "pcnt"], w=["fx"])
                                    P.op("dve", lambda e: e.tensor_tensor(out=plm[rows, 0:16], in0=fx[rows, :], in1=a[rows, 16:32], op=ALU.subtract),
                                         r=["fx"], w=[pk])
                            if m == 0:
                                fin(sA, slice(0, 64), "sA")
                            P.op("dve", lambda e: e.tensor_tensor(out=sB[:, 3:W], in0=sA[:, 3:W], in1=sA[:, 1:W - 2], op=ALU.add), r=["sA"], w=["sB"])
                            if m == 0:
                                fin(sB, slice(64, 128), "sB")
                            else:
                                P.op("dve", lambda e: e.tensor_tensor(out=sA[:, 7:W], in0=sB[:, 7:W], in1=sB[:, 3:W - 4], op=ALU.add), r=["sB"], w=["sA"])
                                fin(sA, slice(0, 64), "sA")
                                P.op("dve", lambda e: e.tensor_tensor(out=sB[:, 15:W], in0=sA[:, 15:W], in1=sA[:, 7:W - 8], op=ALU.add), r=["sA"], w=["sB"])
                                fin(sB, slice(64, 128), "sB")
                            for gg in range(2):
                                g = m * 2 + gg
                                rows = slice(gg * 64, gg * 64 + 64)
                                i = bank()
                                mm(i, banks[i][0:64, :], poolw[rows, m, :], plm[rows, :], True, True, ["poolw", pk])
                                P.op("dve", lambda e, i=i, g=g, rows=rows, m=m, t=t: e.tensor_scalar(out=hT[rows, 3 + m, tsls[t]], in0=banks[i][0:64, :], scalar1=mvec[0:64, l, 3 + g:4 + g],
                                                                                              scalar2=None, op0=ALU.mult),
                                     r=[("bk", i), "mvec"], w=[("mix", 3 + m, t, gg * 64), ("h", 3 + m, t)])
                    return pm_tile

                with ExitStack() as pa:
                    win = sb("win_sb", [128, KC, N_IN], BF16, pa)
                    wkrot = sb("wkrot_sb", [128, KC, 32], BF16, pa)
                    wqb = sb("wqb_sb", [128, 2, 576], BF16, pa)
                    wqbrot = sb("wqbrot_sb", [128, 2, 192], BF16, pa)
                    wkvb = sb("wkvb_sb", [128, 768], BF16, pa)
                    ropet = [sb("rope%d" % i, [128, 2, TS], F32, pa) for i in range(2)]
                    qn = [sb("qn%d" % i, [128, 2, TS], BF16, pa) for i in range(2)]
                    kvn = [sb("kvn%d" % i, [128, TS], BF16, pa) for i in range(2)]
                    sqm = [sb("sqm%d" % i, [128, TS], BF16, pa) for i in range(2)]
                    rsm = [sb("rsm%d" % i, [128, TS], F32, pa) for i in range(2)]
                    r1 = sb("r1_sb", [128, TS], F32, pa)
                    r2 = sb("r2_sb", [128, TS], F32, pa)
                    NQS, NFS = 4, 2
                    qst = [sb("qst%d" % i, [96, TS], BF16, pa) for i in range(NQS)]
                    kst = [sb("kst%d" % i, [96, TS], BF16, pa) for i in range(NQS)]
                    fqst = [sb("fqst%d" % i, [128, TS], BF16, pa) for i in range(NFS)]
                    fkst = [sb("fkst%d" % i, [128, TS], BF16, pa) for i in range(NFS)]
                    drow = [sb("drow%d" % i, [8, TS], BF16, pa) for i in range(2)]
                    vst = [sb("vst%d" % i, [128, NH, 64], BF16, pa) for i in range(4)]
                    fb = sb("fb_sb", [128, NH], F32, pa)
                    spb = [sb("sp%d" % i, [128, NH], F32, pa) for i in range(2)]
                    spacc = sb("spacc_sb", [128, NH], F32, pa)
                    misc = sb("misc_sb", [128, 128], F32, pa)
                    P.dma("pool", win[:], w_in_d[l].rearrange("(k p) n -> p k n", p=128), "win", w=[("win", c) for c in range(KC)])
                    P.dma("pool", wkrot[:], w_krot_d[l].rearrange("(k p) n -> p k n", p=128), "wkrot", w=["wkrot"])
                    P.dma("pool", wqb[:], w_qb_d[l].rearrange("(k p) n -> p k n", p=128), "wqb", w=["wqb"])
                    P.dma("pool", wqbrot[:], w_qbrot_d[l].rearrange("(k p) n -> p k n", p=128), "wqbrot", w=["wqbrot"])
                    P.dma("pool", wkvb[:], w_kvb_d[l], "wkvb", w=["wkvb"])
                    P.op("pool", lambda e: e.memset(spacc[:], 0.0), w=["spacc"])
                    WIN = [("win", c) for c in range(KC)]
                    cn = {"q": 0, "k": 0, "fq": 0, "fk": 0, "v": 0, "sp": 0}

                    def proj(i, out, col0, ncol, t, wt=None):
                        for c in range(KC):
                            lh = (win if wt is None else wt)[:, c, col0:col0 + ncol]
                            mm(i, out, lh, hT[:, c, tsls[t]], c == 0, c == KC - 1, WIN + ["wkrot", ("h", c, t)])

                    def subnorm(t, srcs, mean_scale, gcols, dst, dkeys):
                        j = t % 2
                        ib = bank()
                        for m, (i, ap) in enumerate(srcs):
                            P.op("act", lambda e, ap=ap, m=m: e.activation(out=sqm[m % 2][:], in_=ap, func=AF.Square),
                                 r=[("bk", i)], w=[("sqm", m % 2)])
                            mm(ib, banks[ib][:], ones[:], sqm[m % 2][:], m == 0, m == len(srcs) - 1, [("sqm", m % 2), "ones"])
                        P.op("act", lambda e: e.activation(out=rsm[j][:], in_=banks[ib][:], func=AF.Ln, bias=epsb[:], scale=mean_scale),
                             r=[("bk", ib), "epsb"], w=[("rsm", j)])
                        P.op("act", lambda e: e.activation(out=rsm[j][:], in_=rsm[j][:], func=AF.Exp, scale=-0.5),
                             r=[("rsm", j)], w=[("rsm", j)])
                        for m, (i, ap) in enumerate(srcs):
                            P.op("dve", lambda e, ap=ap, m=m: e.scalar_tensor_tensor(out=dst[m], in0=ap, scalar=mvec[:, l, gcols[m]:gcols[m] + 1],
                                                                                  in1=rsm[j][:], op0=ALU.mult, op1=ALU.mult),
                                 r=[("bk", i), ("rsm", j), "mvec"], w=[dkeys[m]])

                    def rope_rows(ia, ib_, rp, out_list, okeys, t1, t2):
                        P.op("dve", lambda e: e.tensor_tensor(out=t1[64:96, :], in0=banks[ia][64:96, :], in1=rp[64:96, 0, :], op=ALU.mult),
                             r=[("bk", ia), "ropet"], w=[("t1", id(t1))])
                        P.op("dve", lambda e: e.tensor_tensor(out=t2[64:96, :], in0=banks[ib_][64:96, :], in1=rp[64:96, 1, :], op=ALU.mult),
                             r=[("bk", ib_), "ropet"], w=[("t2", id(t2))])
                        for o, ok in zip(out_list, okeys):
                            P.op("dve", lambda e, o=o: e.tensor_tensor(out=o, in0=t1[64:96, :], in1=t2[64:96, :], op=ALU.add),
                                 r=[("t1", id(t1)), ("t2", id(t2))], w=[ok])

                    m8all = sb("m8all_sb", [128, NB, NH], BF16, pa)

                    def evac_v(i, dst_d, blk):
                        vt = vst[cn["v"] % 4]
                        vk = ("vst", cn["v"] % 4)
                        cn["v"] += 1
                        P.op("act", lambda e: e.activation(out=vt[:].rearrange("p h d -> p (h d)"), in_=banks[i][:, 0:NH * 64], func=AF.Copy),
                             r=[("bk", i)], w=[vk])
                        name = "foxV_in" if dst_d is foxV_in else "mlaV_in"
                        P.dma("pool", dst_d[l].rearrange("(h t) d -> t h d", h=NH)[blk * 128:(blk + 1) * 128], vt[:], vk, r=[vk], w=[(name, blk)])

                    def tile_kv(t):
                        tsl = tsls[t]
                        rp = ropet[t % 2]
                        P.dma("sp", rp[:], rope_d[:, :, tsl], ("rope", t % 2), w=["ropet"])
                        ic = bank()
                        proj(ic, banks[ic][:], O_KVA, 128, t)
                        kj = kvn[t % 2]
                        subnorm(t, [(ic, banks[ic][:])], 8.0, [2], [kj[:]], [("kvn", t % 2)])
                        for bi in range(4):
                            blk = t * 4 + bi
                            bsl = slice(blk * 128, (blk + 1) * 128)
                            i = bank()
                            for c in range(KC):
                                mm(i, banks[i][:, 0:NH], hT[:, c, bsl], win[:, c, O_FF:O_FF + NH], c == 0, c == KC - 1, WIN + [("h", c, t)])
                            sp_ = spb[cn["sp"] % 2]
                            spk = ("sp", cn["sp"] % 2)
                            cn["sp"] += 1
                            P.op("dve", lambda e, i=i: e.tensor_tensor(out=fb[:], in0=banks[i][:, 0:NH], in1=foxb[:, l, :], op=ALU.add),
                                 r=[("bk", i), "foxb"], w=["fb"])
                            P.op("act", lambda e: e.activation(out=fb[:], in_=fb[:], func=AF.Exp, scale=-1.0), r=["fb"], w=["fb"])
                            P.op("act", lambda e, sp_=sp_: e.activation(out=sp_[:], in_=fb[:], func=AF.Ln, bias=oneb[:], scale=1.0),
                                 r=["fb", "oneb"], w=[spk])
                            i = bank()
                            for c in range(KC):
                                mm(i, banks[i][:, 0:NH * 64], hT[:, c, bsl], win[:, c, O_FV:O_FV + NH * 64], c == 0, c == KC - 1, WIN + [("h", c, t)])
                            evac_v(i, foxV_in, blk)
                            i2 = bank()
                            mm(i2, banks[i2][:, 0:NH], triu[:], sp_[:], True, False, ["triu", spk])
                            mm(i2, banks[i2][:, 0:NH], onesf[:], spacc[:], False, True, ["onesf", "spacc"])
                            P.op("act", lambda e, i2=i2, blk=blk: e.activation(out=csall[:, blk, :], in_=banks[i2][:, 0:NH], func=AF.Copy),
                                 r=[("bk", i2)], w=[("cs", blk)])
                            P.op("pool", lambda e, sp_=sp_: e.tensor_tensor(out=spacc[:], in0=spacc[:], in1=sp_[:], op=ALU.add),
                                 r=[spk, "spacc"], w=["spacc"])
                            P.op("dve", lambda e, blk=blk: e.tensor_scalar(out=m8all[:, blk, :], in0=csall[:, blk, :], scalar1=-8.0, scalar2=None, op0=ALU.mult),
                                 r=[("cs", blk)], w=[("m8", blk)])
                        for bi in range(4):
                            i = bank()
                            for h in range(NH):
                                mm(i, banks[i][:, h * 64:(h + 1) * 64], kj[:, bi * 128:(bi + 1) * 128], wkvb[:, h * 128 + 64:h * 128 + 128], True, True, [("kvn", t % 2), "wkvb"])
                            evac_v(i, mlaV_in, t * 4 + bi)
                        ika, ikb = bank(), bank()
                        proj(ika, banks[ika][64:96, :], O_KR, 32, t)
                        proj(ikb, banks[ikb][64:96, :], 0, 32, t, wt=wkrot)
                        K1, K2 = ("t1", id(r1)), ("t2", id(r2))
                        P.op("dve", lambda e: e.tensor_tensor(out=r1[64:96, :], in0=banks[ika][64:96, :], in1=rp[64:96, 0, :], op=ALU.mult),
                             r=[("bk", ika), "ropet"], w=[K1])
                        P.op("dve", lambda e: e.tensor_tensor(out=r2[64:96, :], in0=banks[ikb][64:96, :], in1=rp[64:96, 1, :], op=ALU.mult),
                             r=[("bk", ikb), "ropet"], w=[K2])
                        for h in range(NH):
                            kt = kst[cn["k"] % NQS]
                            kk = ("kst", cn["k"] % NQS)
                            cn["k"] += 1
                            i = bank()
                            mm(i, banks[i][0:64, :], wkvb[:, h * 128:h * 128 + 64], kj[:], True, True, [("kvn", t % 2), "wkvb"])
                            P.op("act", lambda e, i=i, kt=kt: e.activation(out=kt[0:64, :], in_=banks[i][0:64, :], func=AF.Copy), r=[("bk", i)], w=[kk])
                            P.op("dve", lambda e, kt=kt: e.tensor_tensor(out=kt[64:96, :], in0=r1[64:96, :], in1=r2[64:96, :], op=ALU.add), r=[K1, K2], w=[kk])
                            P.dma(("sp" if h % 2 == 0 else "pool"), mlaK_in[l][h // 3][(h % 3) * 96:(h % 3 + 1) * 96, tsl], kt[:], kk, r=[kk], w=[("mlaK_in", h, t)])
                            if h % 2 == 1:
                                hp = h // 2
                                ft = fkst[cn["fk"] % NFS]
                                fk = ("fkst", cn["fk"] % NFS)
                                cn["fk"] += 1
                                i = bank()
                                proj(i, banks[i][:], O_FK + hp * 128, 128, t)
                                P.op("act", lambda e, i=i, ft=ft: e.activation(out=ft[:], in_=banks[i][:], func=AF.Copy), r=[("bk", i)], w=[fk])
                                P.dma(("sp" if hp % 2 == 0 else "pool"), foxK_in[l][hp * 128:(hp + 1) * 128, tsl], ft[:], fk, r=[fk],
                                      w=[("foxK_in", h - 1, t), ("foxK_in", h, t)])
                        for m in range(2):
                            i = bank()
                            proj(i, banks[i][:], O_POOL + m * 128, 128, t)
                            P.op("act", lambda e, i=i, m=m: e.activation(out=u32[:, m, 16 + t * TS:16 + (t + 1) * TS], in_=banks[i][:], func=AF.Copy),
                                 r=[("bk", i)], w=[("u32", m, t)])

                    def qa_norm(t):
                        ia, ib2 = bank(), bank()
                        proj(ia, banks[ia][:], O_QA, 128, t)
                        proj(ib2, banks[ib2][:], O_QA + 128, 128, t)
                        qj = qn[t % 2]
                        subnorm(t, [(ia, banks[ia][:]), (ib2, banks[ib2][:])], 4.0, [0, 1], [qj[:, 0, :], qj[:, 1, :]], [("qn", t % 2, 0), ("qn", t % 2, 1)])

                    def tile_q_mla(t):
                        tsl = tsls[t]
                        rp = ropet[t % 2]
                        P.dma("sp", rp[:], rope_d[:, :, tsl], ("rope", t % 2), w=["ropet"])
                        if t + 1 < NTS:
                            qa_norm(t + 1)
                        qj = qn[t % 2]
                        QN = [("qn", t % 2, 0), ("qn", t % 2, 1)]
                        for h in range(NH):
                            qt_ = qst[cn["q"] % NQS]
                            qk = ("qst", cn["q"] % NQS)
                            cn["q"] += 1
                            i, ir = bank(), bank()
                            for m in range(2):
                                mm(i, banks[i][0:96, :], wqb[:, m, h * 96:(h + 1) * 96], qj[:, m, :], m == 0, m == 1, QN + ["wqb"])
                            for m in range(2):
                                mm(ir, banks[ir][64:96, :], wqbrot[:, m, h * 32:(h + 1) * 32], qj[:, m, :], m == 0, m == 1, QN + ["wqbrot"])
                            P.op("act", lambda e, i=i, qt_=qt_: e.activation(out=qt_[0:64, :], in_=banks[i][0:64, :], func=AF.Copy), r=[("bk", i)], w=[qk])
                            rope_rows(i, ir, rp, [qt_[64:96, :]], [qk], r1, r2)
                            P.dma(("sp" if h % 2 == 0 else "pool"), mlaQ_d[l][h // 3][(h % 3) * 96:(h % 3 + 1) * 96, tsl], qt_[:], qk, r=[qk], w=[("mlaQ", h, t)])

                    def tile_q_fox(t):
                        tsl = tsls[t]
                        i2 = bank()
                        for bi in range(4):
                            blk = t * 4 + bi
                            mm(i2, banks[i2][0:NH, bi * 128:(bi + 1) * 128], m8all[:, blk, :], ident[:], True, True, [("m8", blk), "ident"])
                        dr = drow[t % 2]
                        dk = ("drow", t % 2)
                        P.op("act", lambda e: e.activation(out=dr[0:NH, :], in_=banks[i2][0:NH, :], func=AF.Copy), r=[("bk", i2)], w=[dk])
                        P.dma("pool", foxQ_d[l].rearrange("(h r) t -> r h t", r=65)[64, :, tsl], dr[0:NH, :], dk, r=[dk], w=[("foxQd", t)])
                        for hp in range(NH // 2):
                            fq = fqst[cn["fq"] % NFS]
                            fqk = ("fqst", cn["fq"] % NFS)
                            cn["fq"] += 1
                            i = bank()
                            proj(i, banks[i][:], O_FQ + hp * 128, 128, t)
                            P.op("act", lambda e, i=i, fq=fq: e.activation(out=fq[:], in_=banks[i][:], func=AF.Copy), r=[("bk", i)], w=[fqk])
                            for k_ in range(2):
                                h = 2 * hp + k_
                                P.dma(("sp" if k_ == 0 else "pool"), foxQ_d[l][h * 65:h * 65 + 64, tsl], fq[k_ * 64:(k_ + 1) * 64, :], fqk, r=[fqk], w=[("foxQ", h, t)])

                    for t in range(NTS):
                        tile_kv(t)
                    it = bank()
                    mm(it, banks[it][:, 0:NH], onesf[:], spacc[:], True, True, ["onesf", "spacc"])
                    for blk in range(NB):
                        P.op("dve", lambda e, blk=blk: e.tensor_tensor(out=misc[:, blk * NH:(blk + 1) * NH], in0=csall[:, blk, :], in1=banks[it][:, 0:NH], op=ALU.subtract),
                             r=[("cs", blk), ("bk", it)], w=["misc"])
                    for m in range(2):
                        P.op("dve", lambda e, m=m: e.tensor_copy(out=misc[:, 96 + m * 16:96 + (m + 1) * 16], in_=u32[:, m, T:T + 16]),
                             r=[("u32", m, NTS - 1)], w=["misc"])
                    P.dma("sp", misc_in[l], misc[:], "misc", r=["misc"], w=["misc_in"])
                    def gather(name, src, dst, rkeys):
                        P.cc(lambda g: g.collective_compute("AllGather", ALU.bypass, replica_groups=PAIRS, ins=[src], outs=[dst]),
                             ("cc", name, l), r=rkeys, w=[("cc", name)])
                    HT = [(h, t) for h in range(NH) for t in range(NTS)]
                    gather("misc", misc_in[l], misc_out[l], ["misc_in"])
                    gather("mlaK0", mlaK_in[l][0], mlaK_out[l][0], [("mlaK_in", h, t) for h, t in HT if h < 3])
                    gather("mlaK1", mlaK_in[l][1], mlaK_out[l][1], [("mlaK_in", h, t) for h, t in HT if h >= 3])
                    gather("mlaV", mlaV_in[l], mlaV_out[l], [("mlaV_in", b_) for b_ in range(NB)])
                    gather("foxK", foxK_in[l], foxK_out[l], [("foxK_in", h, t) for h, t in HT])
                    gather("foxV", foxV_in[l], foxV_out[l], [("foxV_in", b_) for b_ in range(NB)])
                    pm_tile = pool_mixer(pa) if "pool" in parts else (lambda t: None)
                    qa_norm(0)
                    for t in range(NTS):
                        tile_q_mla(t)
                    gather("mlaQ0", mlaQ_d[l][0], mlaQ_out[l][0], [("mlaQ", h, t) for h, t in HT if h < 3])
                    gather("mlaQ1", mlaQ_d[l][1], mlaQ_out[l][1], [("mlaQ", h, t) for h, t in HT if h >= 3])
                    for t in range(NTS):
                        tile_q_fox(t)
                        pm_tile(t)
                    gather("foxQ", foxQ_d[l], foxQ_out[l], [("foxQ", h, t) for h, t in HT] + [("foxQd", t) for t in range(NTS)])
                    P.fence(scratch[:, 1:2])
                with ExitStack() as pb:
                    Kh = [sb("Kh%d" % i, [96, T], BF16, pb) for i in range(2)]
                    Vh = [sb("Vh%d" % i, [128, NB, 128], BF16, pb) for i in range(2)]
                    Qh = [sb("Qh%d" % i, [96, T], BF16, pb) for i in range(2)]
                    Kc = [sb("Kc%d" % i, [96, T], BF16, pb) for i in range(2)]
                    Vc = [sb("Vc%d" % i, [128, NB, 64], BF16, pb) for i in range(2)]
                    Qc = [sb("Qc%d" % i, [96, T], BF16, pb) for i in range(2)]
                    pt = [sb("pt%d" % i, [128, TS], BF16, pb) for i in range(3)]
                    rct = sb("rct_sb", [128, TS], F32, pb)
                    comb = sb("comb_sb", [128, TS], F32, pb)
                    part = [sb("part%d" % i, [128, TS], F32, pb) for i in range(2)]
                    rst = [sb("rst%d" % i, [128, TS], F32, pb) for i in range(2)]
                    negraw = sb("negraw_sb", [128, NB, NH], F32, pb)
                    negsel = sb("negsel_sb", [128, NB, 3], F32, pb)
                    fA, fB = corev[:, 2:3], corev[:, 1:2]
                    for i in range(2):
                        P.op("pool", lambda e, i=i: e.memset(Vh[i][:, :, 64:128], 1.0), w=[("Vh", i)])
                        P.op("pool", lambda e, i=i: e.memset(Kh[i][64:65, :], 1.0), w=[("Kh", i)])
                    P.dma("sp", negraw[:].rearrange("p b h -> p (b h)"), misc_out[l][0:128, 0:NB * NH], "negraw", r=[("cc", "misc")], w=["negraw"])
                    P.op("dve", lambda e: e.tensor_scalar(out=negsel[:], in0=negraw[:, :, 0:3], scalar1=fA, scalar2=None, op0=ALU.mult), r=["negraw", "corev"], w=["negsel"])
                    P.op("dve", lambda e: e.scalar_tensor_tensor(out=negsel[:], in0=negraw[:, :, 3:6], scalar=fB, in1=negsel[:], op0=ALU.mult, op1=ALU.add),
                         r=["negraw", "corev", "negsel"], w=["negsel"])
                    ptn = {"i": 0}
                    pon = {"i": 0}
                    last_fam = {0: None, 1: None}

                    def blend(dst, ca, cb, srca, srcb, skey, rkeys, dkey, np_):
                        fa, fb = corev[0:np_, 2:3], corev[0:np_, 1:2]
                        P.dma("sp", ca, srca, (skey, "a"), r=rkeys, w=[(skey, "a")])
                        P.dma("sp", cb, srcb, (skey, "b"), r=rkeys, w=[(skey, "b")])
                        P.op("act", lambda e: e.activation(out=dst, in_=ca, func=AF.Copy, scale=fa), r=[(skey, "a"), "corev"], w=[dkey])
                        P.op("dve", lambda e: e.scalar_tensor_tensor(out=dst, in0=cb, scalar=fb, in1=dst, op0=ALU.mult, op1=ALU.add), r=[(skey, "b"), "corev", dkey], w=[dkey])

                    def fix_k_row(sl, fam):
                        if fam == "fox" and last_fam[sl] == "mla":
                            P.op("pool", lambda e: e.memset(Kh[sl][64:65, :], 1.0), r=[("Kh", sl)], w=[("Kh", sl)])
                        last_fam[sl] = fam

                    def attend(sl, fam, kdq, qt, blocks, bias_of, ipo):
                        tsl = tsls[qt]
                        sc = MLA_SCALE if fam == "mla" else 0.125
                        pend = []

                        def score(kb, diag):
                            i = bk["i"] % 6
                            bk["i"] += 1
                            q0 = max(diag, 0) * 128
                            ksl = slice(kb * 128, (kb + 1) * 128)
                            qsl = slice(tsl.start + q0, tsl.stop)
                            P.op("pe", lambda e: e.matmul(banks[i][:, q0:], lhsT=Kh[sl][0:kdq, ksl], rhs=Qh[sl][0:kdq, qsl], start=True, stop=(diag < 0)),
                                 r=[("Kh", sl), ("Qh", sl)], w=[("bk", i)])
                            if diag >= 0:
                                P.op("pe", lambda e: e.matmul(banks[i][:, q0:], lhsT=ident[:], rhs=maskb[:, diag, q0:], start=False, stop=True),
                                     r=["ident", "maskb"], w=[("bk", i)])
                            p_ = ptn["i"] % 3
                            ptn["i"] += 1
                            bias, bkeys = bias_of(kb)
                            P.op("act", lambda e: e.activation(out=pt[p_][:, q0:], in_=banks[i][:, q0:], func=AF.Exp, bias=bias, scale=sc),
                                 r=[("bk", i)] + bkeys, w=[("pt", p_)])
                            return kb, p_, q0

                        def pv(kb, p_, q0, first, last):
                            P.op("pe", lambda e: e.matmul(banks[ipo][:, q0:], lhsT=Vh[sl][:, kb, :], rhs=pt[p_][:, q0:], start=first, stop=last),
                                 r=[("Vh", sl), ("pt", p_)], w=[("bk", ipo)])

                        first_kb = blocks[0][0]
                        for kb, diag in blocks:
                            pend.append(score(kb, diag))
                            if len(pend) > 2:
                                k0, p0, c0 = pend.pop(0)
                                pv(k0, p0, c0, k0 == first_kb, False)
                        while pend:
                            k0, p0, c0 = pend.pop(0)
                            pv(k0, p0, c0, k0 == first_kb, not pend)

                    items = []
                    rect = ([("mla", j) for j in range(3)] if "mla" in parts else []) + ([("fox", j) for j in range(3)] if "fox" in parts else [])
                    heads = ([("mla", h) for h in range(NH)] if "mla" in parts else []) + ([("fox", h) for h in range(NH)] if "fox" in parts else [])
                    pn_ = {"i": 0}

                    def make_rect(fam, j, sl):
                        jj = j if fam == "mla" else 3 + j
                        kd, kdq = (96, 96) if fam == "mla" else (64, 65)
                        if fam == "mla":
                            Ka, Kb_ = mlaK_out[l][0][j * 96:(j + 1) * 96, :], mlaK_out[l][1][j * 96:(j + 1) * 96, :]
                            Qa, Qb_ = mlaQ_out[l][0][288 + j * 96:288 + (j + 1) * 96, :], mlaQ_out[l][1][288 + j * 96:288 + (j + 1) * 96, :]
                            Vv = mlaV_out[l].rearrange("(h b p) d -> p h b d", h=2 * NH, b=NB)
                            rk, rq, rv = [("cc", "mlaK0"), ("cc", "mlaK1")], [("cc", "mlaQ0"), ("cc", "mlaQ1")], [("cc", "mlaV")]
                        else:
                            Ka, Kb_ = foxK_out[l][j * 64:(j + 1) * 64, :], foxK_out[l][(j + 3) * 64:(j + 4) * 64, :]
                            Qa, Qb_ = foxQ_out[l][390 + j * 65:390 + (j + 1) * 65, :], foxQ_out[l][390 + (j + 3) * 65:390 + (j + 4) * 65, :]
                            Vv = foxV_out[l].rearrange("(h b p) d -> p h b d", h=2 * NH, b=NB)
                            rk, rq, rv = [("cc", "foxK")], [("cc", "foxQ")], [("cc", "foxV")]
                        specs = [(Kh[sl][0:kd, :], Kc[0][0:kd, :], Kc[1][0:kd, :], Ka, Kb_, "Kc", rk, ("Kh", sl), kd),
                                 (Vh[sl][:, :, 0:64], Vc[0][:], Vc[1][:], Vv[:, j], Vv[:, j + 3], "Vc", rv, ("Vh", sl), 128),
                                 (Qh[sl][0:kdq, :], Qc[0][0:kdq, :], Qc[1][0:kdq, :], Qa, Qb_, "Qc", rq, ("Qh", sl), kdq)]

                        def loads():
                            for dst, ca, cb, srca, srcb, skey, rkeys, dkey, np_ in specs:
                                P.dma("sp", ca, srca, (skey, "a"), r=rkeys, w=[(skey, "a")])
                                P.dma("sp", cb, srcb, (skey, "b"), r=rkeys, w=[(skey, "b")])

                        def prep():
                            for n_, (dst, ca, cb, srca, srcb, skey, rkeys, dkey, np_) in enumerate(specs):
                                fa, fb = corev[0:np_, 2:3], corev[0:np_, 1:2]
                                P.op("dve", lambda e, dst=dst, ca=ca, fa=fa: e.tensor_scalar(out=dst, in0=ca, scalar1=fa, scalar2=None, op0=ALU.mult), r=[(skey, "a"), "corev"], w=[dkey])
                                P.op("dve", lambda e, dst=dst, cb=cb, fb=fb: e.scalar_tensor_tensor(out=dst, in0=cb, scalar=fb, in1=dst, op0=ALU.mult, op1=ALU.add),
                                     r=[(skey, "b"), "corev", dkey], w=[dkey])
                                if n_ == 0:
                                    fix_k_row(sl, fam)

                        def run(mid):
                            for qt in range(NTS):
                                ipo = 6 + pon["i"] % 2
                                pon["i"] += 1
                                if fam == "mla":
                                    bias_of = lambda kb: (zerob[:], ["zerob"])
                                else:
                                    bias_of = lambda kb: (negsel[:, kb, j:j + 1], ["negsel"])
                                attend(sl, fam, kdq, qt, [(kb, -1) for kb in range(NB)], bias_of, ipo)
                                r_ = rst[pon["i"] % 2]
                                rkey = ("rst", pon["i"] % 2)
                                P.op("dve", lambda e, ipo=ipo, r_=r_: e.tensor_copy(out=r_[:], in_=banks[ipo][:]), r=[("bk", ipo)], w=[rkey])
                                P.dma("pool", rout_in[l][jj // 2][(jj % 2) * 128:(jj % 2 + 1) * 128, tsls[qt]], r_[:], rkey, r=[rkey], w=[("rout_in", jj, qt)])
                                if qt == 1:
                                    mid()
                            if jj % 2 == 1:
                                g_ = jj // 2
                                P.cc(lambda g: g.collective_compute("AllGather", ALU.bypass, replica_groups=PAIRS, ins=[rout_in[l][g_]], outs=[rout_out[l][g_]]),
                                     ("cc", "rout%d" % g_, l), r=[("rout_in", x, qt) for x in (jj - 1, jj) for qt in range(NTS)], w=[("cc", "rout", g_)])
                        return loads, prep, run

                    def make_tri(fam, h, sl):
                        kd, kdq = (96, 96) if fam == "mla" else (64, 65)
                        jj = (h % 3) if fam == "mla" else 3 + (h % 3)
                        slot = h // 3
                        if fam == "mla":
                            Ko, Qo, Vo = mlaK_in[l][h // 3][(h % 3) * 96:(h % 3 + 1) * 96, :], mlaQ_d[l][h // 3][(h % 3) * 96:(h % 3 + 1) * 96, :], mlaV_in[l]
                            kn, vn, qn_ = "mlaK_in", "mlaV_in", "mlaQ"
                        else:
                            Ko, Qo, Vo = foxK_in[l][h * 64:(h + 1) * 64, :], foxQ_d[l][h * 65:(h + 1) * 65, :], foxV_in[l]
                            kn, vn, qn_ = "foxK_in", "foxV_in", "foxQ"
                        chunk = (h // 2) if fam == "mla" else (5 + h // 2)
                        prow = (h % 2) * 64

                        def loads():
                            P.dma("sp", Kh[sl][0:kd, :], Ko, ("Kho", sl), r=[(kn, h, t) for t in range(NTS)], w=[("Kh", sl)])
                            fix_k_row(sl, fam)
                            P.dma("sp", Vh[sl][:, :, 0:64], Vo.rearrange("(h b p) d -> p h b d", h=NH, b=NB)[:, h], ("Vho", sl), r=[(vn, b_) for b_ in range(NB)], w=[("Vh", sl)])
                            P.dma("sp", Qh[sl][0:kdq, :], Qo, ("Qho", sl), r=[(qn_, h, t) for t in range(NTS)] + ([("foxQd", t) for t in range(NTS)] if fam == "fox" else []), w=[("Qh", sl)])

                        def prep():
                            pass

                        def run(mid):
                            for qt in range(NTS):
                                ipo = 6 + pon["i"] % 2
                                pon["i"] += 1
                                pa_ = part[pn_["i"] % 2]
                                pkey = ("part", pn_["i"] % 2)
                                pn_["i"] += 1
                                P.dma("sp", pa_[:], rout_out[l][jj // 2][slot * 256 + (jj % 2) * 128:slot * 256 + (jj % 2 + 1) * 128, tsls[qt]], pkey,
                                      r=[("cc", "rout", jj // 2)], w=[pkey])
                                if fam == "mla":
                                    bias_of = lambda kb: (zerob[:], ["zerob"])
                                else:
                                    bias_of = lambda kb: (csall[:, kb, h:h + 1], [("cs", kb)])
                                attend(sl, fam, kdq, qt, [(kb, kb - 4 * qt) for kb in range(4 * (qt + 1))], bias_of, ipo)
                                tsl = tsls[qt]
                                P.op("dve", lambda e, ipo=ipo, pa_=pa_: e.scalar_tensor_tensor(out=comb[:], in0=pa_[:], scalar=fB, in1=banks[ipo][:], op0=ALU.mult, op1=ALU.add),
                                     r=[("bk", ipo), pkey, "corev"], w=["comb"])
                                P.op("dve", lambda e: e.reciprocal(out=rct[0:64, :], in_=comb[64:128, :]), r=["comb"], w=["rct"])
                                P.op("dve", lambda e, tsl=tsl: e.tensor_tensor(out=hT[prow:prow + 64, chunk, tsl], in0=comb[0:64, :], in1=rct[0:64, :], op=ALU.mult),
                                     r=["comb", "rct"], w=[("mix", chunk, qt, prow)])
                                if qt == 1:
                                    mid()
                        return loads, prep, run

                    for n_, (fam, j) in enumerate(rect):
                        items.append(make_rect(fam, j, n_ % 2))
                    for n_, (fam, h) in enumerate(heads):
                        items.append(make_tri(fam, h, (len(rect) + n_) % 2))
                    if items:
                        items[0][0]()
                        items[0][1]()
                    for n_, (loads, prep, run) in enumerate(items):
                        nxt = items[n_ + 1] if n_ + 1 < len(items) else None
                        if nxt:
                            nxt[0]()
                        run(nxt[1] if nxt else (lambda: None))
                    P.fence(scratch[:, 2:3])
                with ExitStack() as pc:
                    wout = sb("wout_sb", [128, KC, D], BF16, pc)
                    P.dma("pool", wout[:], w_out_d[l].rearrange("(k p) n -> p k n", p=128), "wout", w=[("wout", c) for c in range(KC)])
                    zero_chunks = ([] if "mla" in parts else [0, 1, 2]) + ([] if "pool" in parts else [3, 4]) + ([] if "fox" in parts else [5, 6, 7])
                    for c in zero_chunks:
                        P.op("pool", lambda e, c=c: e.memset(hT[:, c, :], 0.0), w=[("mix", c, t, pr) for t in range(NTS) for pr in (0, 64)])
                    for t in range(NTS):
                        for dc in range(KC):
                            i = bank()
                            for c in range(KC):
                                mm(i, banks[i][:], wout[:, c, dc * 128:(dc + 1) * 128], hT[:, c, tsls[t]], c == 0, c == KC - 1, [("wout", c), ("mix", c, t, 0), ("mix", c, t, 64)])
                            P.op("dve", lambda e, i=i, dc=dc, t=t: e.tensor_tensor(out=xT[:, dc, tsls[t]], in0=banks[i][:], in1=xT[:, dc, tsls[t]], op=ALU.add),
                                 r=[("bk", i), ("x", dc, t)], w=[("x", dc, t)])
                        if after_tile is not None and t > 0:
                            after_tile(t - 1)
                    if after_tile is not None:
                        after_tile(NTS - 1)
                    P.fence(scratch[:, 3:4])

        fin = []
        if stage == "ffn1":
            ffn_phase(0, 0, 0)
        elif stage == "ffn1n":
            ffn_phase(0, 0, 0)
            rmsnorm(NG - 1, out_fp32=True)
        elif stage.startswith("mix_"):
            mix_phase(0, 1, tuple(stage[4:].split("+")))
        elif stage == "full":
            def store_tile(t):
                fin.append(P.dma("sp", outT_d.rearrange("(k p) t -> p k t", p=128)[:, :, t * TS:(t + 1) * TS], xT[:, :, t * TS:(t + 1) * TS], ("o", t),
                                 r=[("x", c, t) for c in range(KC)]))

            def final_tile(t):
                rmsnorm_tile(NG - 1, t, out_fp32=True)
                store_tile(t)

            for l in range(DEPTH):
                ffn_phase(l, 0, 3 * l, pre_normed=(l > 0), after_tile=lambda t, l=l: rmsnorm_tile(3 * l + 1, t))
                mix_phase(l, 3 * l + 1, pre_normed=True, after_tile=lambda t, l=l: rmsnorm_tile(3 * l + 2, t))
                ffn_phase(l, 1, 3 * l + 2, pre_normed=True,
                          after_tile=(lambda t, l=l: rmsnorm_tile(3 * (l + 1), t)) if l + 1 < DEPTH else final_tile)
        if not fin:
            fin = [P.dma("sp", outT_d.rearrange("(k p) t -> p k t", p=128)[:, :, t * TS:(t + 1) * TS], xT[:, :, t * TS:(t + 1) * TS], ("o", t),
                         r=[("x", c, t) for c in range(KC)]) for t in range(NTS)]
        P.emit(final_wait_ops=fin)
    return nc


def _gains(inputs):
    rows = []
    for l in range(DEPTH):
        rows += [inputs["ffn1_norm"][l], inputs["mix_norm"][l], inputs["ffn2_norm"][l]]
    rows.append(inputs["final_norm"])
    g = np.stack([np.asarray(r, np.float32) for r in rows])
    return np.ascontiguousarray(g.reshape(len(rows), KC, 128).transpose(2, 0, 1))


def _constants(half):
    c = {}
    pos = (half * T + np.arange(T)).astype(np.float32)
    inv_freq = (np.float32(10000.0) ** (-(np.arange(0, 32, 2, dtype=np.float32) / np.float32(32)))).astype(np.float32)
    ang = (pos[:, None] * inv_freq[None, :]).astype(np.float32)
    cos, sin = np.cos(ang).astype(np.float32).T, np.sin(ang).astype(np.float32).T
    rope = np.zeros((128, 2, T), np.float32)
    rope[64:80, 0], rope[80:96, 0] = cos, cos
    rope[64:80, 1], rope[80:96, 1] = -sin, sin
    c["rope"] = rope
    k = np.arange(128)[:, None, None]
    r = np.arange(4)[None, :, None]
    q = np.arange(TS)[None, None, :]
    c["maskT"] = np.where(q >= r * 128 + k, 0.0, NEG).astype(np.float32)
    c["ident"] = np.eye(128, dtype=np.float32)
    c["triu"] = np.triu(np.ones((128, 128), np.float32))
    corev = np.zeros((128, 4), np.float32)
    corev[:, 0] = 0.0 if half == 1 else NEG
    corev[:, 1] = 1.0 if half == 1 else 0.0
    corev[:, 2] = 1.0 if half == 0 else 0.0
    c["corev"] = corev
    wins = np.array([2.0, 4.0, 8.0, 16.0], np.float32)
    g_of = (np.arange(2)[None, :] * 2 + (np.arange(128)[:, None] // 64))
    w_of = wins[g_of]
    count = (half * T + np.arange(16) + 1).astype(np.float32)
    c["pcnt"] = np.minimum(count[None, None, :], w_of[:, :, None]).astype(np.float32)
    c["invw"] = (1.0 / w_of).astype(np.float32)
    return c


def _layouts(inputs):
    f = lambda k: np.ascontiguousarray(np.asarray(inputs[k], np.float32))
    m = {k: f(k) for k in ("ffn1_w_gu", "ffn1_w_down", "ffn2_w_gu", "ffn2_w_down", "w_in", "w_q_b", "w_kv_b", "w_out")}
    swap = np.concatenate([np.arange(16, 32), np.arange(0, 16)])
    m["w_in_krot"] = np.ascontiguousarray(m["w_in"][:, :, O_KR:O_KR + 32][:, :, swap])
    qb = m["w_q_b"].reshape(DEPTH, 256, NH, 96)[:, :, :, 64:96][:, :, :, swap]
    m["w_q_b_rot"] = np.ascontiguousarray(qb.reshape(DEPTH, 256, NH * 32))
    m["pool_w"] = np.ascontiguousarray(f("pool_w").reshape(DEPTH, 256, 64))
    mvec = np.zeros((128, DEPTH, 8), np.float32)
    qa, kva, psc, fbf = f("q_a_norm"), f("kv_a_norm"), f("pool_scale"), f("fox_b_f")
    for l in range(DEPTH):
        mvec[:, l, 0], mvec[:, l, 1], mvec[:, l, 2] = qa[l, 0:128], qa[l, 128:256], kva[l]
        for g in range(4):
            mvec[0:64, l, 3 + g] = psc[l, g * 64:(g + 1) * 64]
    m["mvec"] = mvec
    m["fox_b"] = np.ascontiguousarray(np.broadcast_to(fbf[None], (128, DEPTH, NH)))
    m["gains"] = _gains(inputs)
    return m


def kernel(_stage="full", **inputs):
    x = np.asarray(inputs["x"], np.float32)
    nc = build_nc(_stage)
    shared = _layouts(inputs)
    consts = [_constants(0), _constants(1)]
    in_maps = []
    for c in range(NCORES):
        b, h = c // 2, c % 2
        m = dict(shared)
        m.update(consts[h])
        m["xT"] = np.ascontiguousarray(x[b, h * T:(h + 1) * T, :].T)
        in_maps.append(m)
    res = run_bass_kernel_spmd(nc, in_maps, core_ids=list(range(NCORES)))
    out = np.empty((B, S, D), np.float32)
    for c in range(NCORES):
        b, h = c // 2, c % 2
        out[b, h * T:(h + 1) * T, :] = res.results[c]["outT"].T
    return out
```

```python
from contextlib import ExitStack
import numpy as np
import concourse.bass as bass
import concourse.mybir as mybir
from concourse.bass_utils import run_bass_kernel_spmd

F32 = mybir.dt.float32
BF16 = mybir.dt.bfloat16
AF = mybir.ActivationFunctionType
ALU = mybir.AluOpType

D = 1024
KC = D // 128
DEPTH = 2
B, S = 4, 4096
NCORES = 8
T = S // 2
TS = 512
NTS = T // TS
D_FF = 2816
FC = D_FF // 128
FGROUPS = 2
EPS = 1e-6
NB = T // 128
NH = 6
N_IN = 1830
O_QA, O_KVA, O_KR, O_POOL, O_FQ, O_FK, O_FV, O_FF = 0, 256, 384, 416, 672, 1056, 1440, 1824
NEG = -30000.0
MLA_SCALE = 1.0 / float(np.sqrt(96.0))

ENGINES = ("pe", "act", "dve", "pool", "sp")


class Prog:
    def __init__(self, nc):
        self.nc = nc
        self.ops = []
        self.last_w = {}
        self.readers = {}
        self.fence_op = None
        self.fence_start = 0

    def fence(self, scratch):
        deps = set()
        last = {}
        for i in range(self.fence_start, len(self.ops)):
            o = self.ops[i]
            if o["dma"]:
                if o["inc"] != 1:
                    deps.add(i)
            else:
                last[o["engine"]] = i
        deps.update(last.values())
        oid = self._add("dve", lambda e: e.memset(scratch, 0.0), (), ())
        self.ops[oid]["deps"].update(deps)
        self.fence_op = oid
        self.fence_start = oid
        return oid

    def _add(self, engine, fn, r, w, dma=False, semkey=None, inc=16):
        oid = len(self.ops)
        deps = set()
        if self.fence_op is not None:
            deps.add(self.fence_op)
        for k in r:
            if k in self.last_w:
                deps.add(self.last_w[k])
        for k in w:
            if k in self.last_w:
                deps.add(self.last_w[k])
            deps.update(self.readers.get(k, ()))
        for k in r:
            self.readers.setdefault(k, []).append(oid)
        for k in w:
            self.last_w[k] = oid
            self.readers[k] = []
        self.ops.append(dict(engine=engine, fn=fn, deps=deps, dma=dma, semkey=semkey, inc=inc))
        return oid

    def op(self, engine, fn, r=(), w=()):
        return self._add(engine, fn, tuple(r), tuple(w))

    def dma(self, engine, out, in_, semkey, r=(), w=()):
        return self._add(engine, lambda e: e.dma_start(out=out, in_=in_), tuple(r), tuple(w),
                         dma=True, semkey=semkey)

    def cc(self, fn, semkey, r=(), w=()):
        return self._add("pool", fn, tuple(r), tuple(w), dma=True, semkey=semkey, inc=1)

    def emit(self, final_wait_ops=()):
        nc, ops = self.nc, self.ops
        signalled = set()
        for o in ops:
            for d in o["deps"]:
                do = ops[d]
                if not do["dma"] and (o["dma"] or do["engine"] != o["engine"] or o["engine"] != "pe"):
                    signalled.add(d)
        for d in final_wait_ops:
            if not ops[d]["dma"]:
                signalled.add(d)
        eng_count = {e: 0 for e in ENGINES}
        sig_val, dma_count = {}, {}
        for i, o in enumerate(ops):
            if o["dma"]:
                k = o["semkey"]
                dma_count[k] = dma_count.get(k, 0) + o["inc"]
                sig_val[i] = dma_count[k]
            elif i in signalled:
                eng_count[o["engine"]] += 1
                sig_val[i] = eng_count[o["engine"]]
        semkeys = sorted(set(o["semkey"] for o in ops if o["dma"]), key=str)
        self.n_sems = len(semkeys) + len(ENGINES)
        with ExitStack() as st:
            esem = {e: st.enter_context(nc.semaphore("e_" + e)) for e in ENGINES}
            dsem = {k: st.enter_context(nc.semaphore("d_%d" % j)) for j, k in enumerate(semkeys)}
            block = st.enter_context(nc.Block())
            streams = {e: [i for i, o in enumerate(ops) if o["engine"] == e] for e in ENGINES}

            def run(e, eng):
                waited = {}
                for i in streams[e]:
                    o = ops[i]
                    need = {}
                    for d in o["deps"]:
                        do = ops[d]
                        if do["dma"]:
                            key, sem = ("d", do["semkey"]), dsem[do["semkey"]]
                        else:
                            if do["engine"] == e and not o["dma"] and e == "pe":
                                continue
                            key, sem = ("e", do["engine"]), esem[do["engine"]]
                        if need.get(key, (None, 0))[1] < sig_val[d]:
                            need[key] = (sem, sig_val[d])
                    for key, (sem, v) in need.items():
                        if waited.get(key, 0) >= v:
                            continue
                        waited[key] = v
                        eng.wait_ge(sem, v)
                    ins = o["fn"](eng)
                    if o["dma"]:
                        ins.then_inc(dsem[o["semkey"]], o["inc"])
                    elif i in signalled:
                        ins.then_inc(esem[e], 1)
                if e == "sp":
                    for d in final_wait_ops:
                        do = ops[d]
                        eng.wait_ge(dsem[do["semkey"]] if do["dma"] else esem[do["engine"]], sig_val[d])

            @block.tensor
            def _(eng):
                run("pe", eng)

            @block.scalar
            def _(eng):
                run("act", eng)

            @block.vector
            def _(eng):
                run("dve", eng)

            @block.gpsimd
            def _(eng):
                run("pool", eng)

            @block.sync
            def _(eng):
                run("sp", eng)


def build_nc(stage="full"):
    nc = bass.Bass("TRN2", target_bir_lowering=False)
    xT_d = nc.dram_tensor("xT", [D, T], F32, kind="ExternalInput").ap()
    outT_d = nc.dram_tensor("outT", [D, T], F32, kind="ExternalOutput").ap()
    NG = 3 * DEPTH + 1
    gains_d = nc.dram_tensor("gains", [128, NG, KC], F32, kind="ExternalInput").ap()
    wgu_d = [nc.dram_tensor("ffn%d_w_gu" % i, [DEPTH, D, 2 * D_FF], F32, kind="ExternalInput").ap() for i in (1, 2)]
    wdn_d = [nc.dram_tensor("ffn%d_w_down" % i, [DEPTH, D_FF, D], F32, kind="ExternalInput").ap() for i in (1, 2)]
    FG = FC // FGROUPS
    NW = 3
    din = lambda n, shp: nc.dram_tensor(n, shp, F32, kind="ExternalInput").ap()
    w_in_d = din("w_in", [DEPTH, D, N_IN])
    w_krot_d = din("w_in_krot", [DEPTH, D, 32])
    w_qb_d = din("w_q_b", [DEPTH, 256, 576])
    w_qbrot_d = din("w_q_b_rot", [DEPTH, 256, 192])
    w_kvb_d = din("w_kv_b", [DEPTH, 128, 768])
    poolw_d = din("pool_w", [DEPTH, 256, 64])
    w_out_d = din("w_out", [DEPTH, D, D])
    mvec_d = din("mvec", [128, DEPTH, 8])
    foxb_d = din("fox_b", [128, DEPTH, NH])
    rope_d = din("rope", [128, 2, T])
    mask_d = din("maskT", [128, 4, TS])
    ident_d = din("ident", [128, 128])
    triu_d = din("triu", [128, 128])
    core_d = din("corev", [128, 4])
    pcnt_d = din("pcnt", [128, 2, 16])
    invw_d = din("invw", [128, 2])
    dsc = lambda n, shp, dt: nc.dram_tensor(n, shp, dt, kind="Internal").ap()
    mlaK_in = [[dsc("mlaK_in%d_%d" % (l, g), [3 * 96, T], BF16) for g in range(2)] for l in range(DEPTH)]
    mlaK_out = [[dsc("mlaK_out%d_%d" % (l, g), [2 * 3 * 96, T], BF16) for g in range(2)] for l in range(DEPTH)]
    foxK_in = [dsc("foxK_in%d" % l, [NH * 64, T], BF16) for l in range(DEPTH)]
    foxK_out = [dsc("foxK_out%d" % l, [2 * NH * 64, T], BF16) for l in range(DEPTH)]
    mlaV_in = [dsc("mlaV_in%d" % l, [NH * T, 64], BF16) for l in range(DEPTH)]
    mlaV_out = [dsc("mlaV_out%d" % l, [2 * NH * T, 64], BF16) for l in range(DEPTH)]
    foxV_in = [dsc("foxV_in%d" % l, [NH * T, 64], BF16) for l in range(DEPTH)]
    foxV_out = [dsc("foxV_out%d" % l, [2 * NH * T, 64], BF16) for l in range(DEPTH)]
    misc_in = [dsc("misc_in%d" % l, [128, 128], F32) for l in range(DEPTH)]
    misc_out = [dsc("misc_out%d" % l, [256, 128], F32) for l in range(DEPTH)]
    mlaQ_d = [[dsc("mlaQ%d_%d" % (l, g), [3 * 96, T], BF16) for g in range(2)] for l in range(DEPTH)]
    mlaQ_out = [[dsc("mlaQ_out%d_%d" % (l, g), [2 * 3 * 96, T], BF16) for g in range(2)] for l in range(DEPTH)]
    foxQ_d = [dsc("foxQ%d" % l, [NH * 65, T], BF16) for l in range(DEPTH)]
    foxQ_out = [dsc("foxQ_out%d" % l, [2 * NH * 65, T], BF16) for l in range(DEPTH)]
    rout_in = [[dsc("rout_in%d_%d" % (l, g), [256, T], F32) for g in range(3)] for l in range(DEPTH)]
    rout_out = [[dsc("rout_out%d_%d" % (l, g), [512, T], F32) for g in range(3)] for l in range(DEPTH)]
    PAIRS = [[0, 1], [2, 3], [4, 5], [6, 7]]

    with ExitStack() as st:
        uniq = {"n": 0}

        def sb(name, shape, dt, stack=st):
            uniq["n"] += 1
            return stack.enter_context(nc.sbuf_tensor("%s_%d" % (name, uniq["n"]), shape, dt))

        def ps(name):
            return st.enter_context(nc.psum_tensor(name, [128, TS], F32))

        P = Prog(nc)
        xT = sb("xT_sb", [128, KC, T], F32)
        hT = sb("hT_sb", [128, KC, T], BF16)
        gains = sb("gains_sb", [128, NG, KC], F32)
        ones = sb("ones_sb", [128, 128], BF16)
        scratch = sb("scratch_sb", [128, 8], F32)
        epsb = sb("epsb_sb", [128, 1], F32)
        sq = [sb("sq%d" % i, [128, TS], BF16) for i in range(2)]
        rstd = [sb("rstd%d" % i, [128, TS], F32) for i in range(2)]
        pg = [ps("pg%d" % i) for i in range(2)]
        pu = [ps("pu%d" % i) for i in range(2)]
        py = [ps("py%d" % i) for i in range(2)]
        pn = [ps("pn%d" % i) for i in range(2)]
        cnt = {"n": 0, "w": 0, "g": 0, "y": 0}

        for t in range(NTS):
            P.dma("sp", xT[:, :, t * TS:(t + 1) * TS], xT_d.rearrange("(k p) t -> p k t", p=128)[:, :, t * TS:(t + 1) * TS], ("x", t),
                  w=[("x", c, t) for c in range(KC)])
        P.dma("sp", gains[:], gains_d, "gains", w=["gains"])
        P.op("dve", lambda e: e.memset(ones[:], 1.0 / D), w=["ones"])
        P.op("dve", lambda e: e.memset(epsb[:], EPS), w=["epsb"])

        def rmsnorm(gi, out_fp32=None):
            for t in range(NTS):
                tsl = slice(t * TS, (t + 1) * TS)
                j = cnt["n"] % 2
                cnt["n"] += 1
                for c in range(KC):
                    q = sq[c % 2]
                    P.op("act", lambda e, q=q, c=c, tsl=tsl: e.activation(out=q[:], in_=xT[:, c, tsl], func=AF.Square),
                         r=[("x", c, t)], w=[("sq", c % 2)])
                    P.op("pe", lambda e, q=q, c=c, j=j: e.matmul(pn[j][:], lhsT=ones[:], rhs=q[:], start=(c == 0), stop=(c == KC - 1)),
                         r=[("sq", c % 2), "ones"], w=[("pn", j)])
                P.op("act", lambda e, j=j: e.activation(out=rstd[j][:], in_=pn[j][:], func=AF.Ln, bias=epsb[:], scale=1.0),
                     r=[("pn", j), "epsb"], w=[("rstd", j)])
                P.op("act", lambda e, j=j: e.activation(out=rstd[j][:], in_=rstd[j][:], func=AF.Exp, scale=-0.5),
                     r=[("rstd", j)], w=[("rstd", j)])
                for c in range(KC):
                    if out_fp32 is None:
                        P.op("dve", lambda e, c=c, j=j, tsl=tsl: e.scalar_tensor_tensor(
                            out=hT[:, c, tsl], in0=xT[:, c, tsl], scalar=gains[:, gi, c:c + 1], in1=rstd[j][:], op0=ALU.mult, op1=ALU.mult),
                            r=[("x", c, t), ("rstd", j), "gains"], w=[("h", c, t)])
                    else:
                        P.op("dve", lambda e, c=c, j=j, tsl=tsl: e.scalar_tensor_tensor(
                            out=xT[:, c, tsl], in0=xT[:, c, tsl], scalar=gains[:, gi, c:c + 1], in1=rstd[j][:], op0=ALU.mult, op1=ALU.mult),
                            r=[("x", c, t), ("rstd", j), "gains"], w=[("x", c, t)])

        def ffn(l, which, bufs):
            act, sil, wgu, wdn = bufs
            wg_v = wgu_d[which][l].rearrange("(k p) n -> p k n", p=128)
            wd_d = wdn_d[which][l]
            for fg in range(FGROUPS):
                slot_of = {}

                def load_gu(fi):
                    f = fg * FG + fi
                    s = cnt["w"] % NW
                    cnt["w"] += 1
                    slot_of[fi] = s
                    P.dma("pool", wgu[s][:, :, 0:128], wg_v[:, :, f * 128:(f + 1) * 128], ("wg", s), w=[("wgu", s)])
                    P.dma("pool", wgu[s][:, :, 128:256], wg_v[:, :, D_FF + f * 128:D_FF + (f + 1) * 128], ("wg", s), w=[("wgu", s)])

                for fi in range(NW):
                    load_gu(fi)
                P.dma("pool", wdn[:], wd_d[fg * FG * 128:(fg + 1) * FG * 128, :].rearrange("(f p) d -> p f d", p=128), "wd",
                      w=[("wdn", fi) for fi in range(FG)])
                for fi in range(FG):
                    f = fg * FG + fi
                    if fi + NW - 1 < FG and fi > 0:
                        load_gu(fi + NW - 1)
                    s = slot_of[fi]
                    for t in range(NTS):
                        tsl = slice(t * TS, (t + 1) * TS)
                        j = cnt["g"] % 2
                        cnt["g"] += 1
                        for c in range(KC):
                            P.op("pe", lambda e, c=c, s=s, j=j, tsl=tsl: e.matmul(pg[j][:], lhsT=wgu[s][:, c, 0:128], rhs=hT[:, c, tsl], start=(c == 0), stop=(c == KC - 1)),
                                 r=[("wgu", s), ("h", c, t)], w=[("pg", j)])
                        for c in range(KC):
                            P.op("pe", lambda e, c=c, s=s, j=j, tsl=tsl: e.matmul(pu[j][:], lhsT=wgu[s][:, c, 128:256], rhs=hT[:, c, tsl], start=(c == 0), stop=(c == KC - 1)),
                                 r=[("wgu", s), ("h", c, t)], w=[("pu", j)])
                        P.op("act", lambda e, j=j: e.activation(out=sil[j][:], in_=pg[j][:], func=AF.Silu),
                             r=[("pg", j)], w=[("sil", j)])
                        P.op("dve", lambda e, j=j, fi=fi, tsl=tsl: e.tensor_tensor(out=act[:, fi, tsl], in0=pu[j][:], in1=sil[j][:], op=ALU.mult),
                             r=[("pu", j), ("sil", j)], w=[("act", fi, t)])
                for t in range(NTS):
                    tsl = slice(t * TS, (t + 1) * TS)
                    for dc in range(KC):
                        j = cnt["y"] % 2
                        cnt["y"] += 1
                        for fi in range(FG):
                            P.op("pe", lambda e, fi=fi, dc=dc, j=j, tsl=tsl: e.matmul(py[j][:], lhsT=wdn[:, fi, dc * 128:(dc + 1) * 128], rhs=act[:, fi, tsl], start=(fi == 0), stop=(fi == FG - 1)),
                                 r=[("wdn", fi), ("act", fi, t)], w=[("py", j)])
                        P.op("dve", lambda e, dc=dc, j=j, tsl=tsl: e.scalar_tensor_tensor(
                            out=xT[:, dc, tsl], in0=py[j][:], scalar=0.5, in1=xT[:, dc, tsl], op0=ALU.mult, op1=ALU.add),
                            r=[("py", j), ("x", dc, t)], w=[("x", dc, t)])

        def ffn_phase(l, which, gi):
            with ExitStack() as ph:
                act = sb("act_sb", [128, FG, T], BF16, ph)
                sil = [sb("sil%d" % i, [128, TS], F32, ph) for i in range(2)]
                wgu = [sb("wgu%d" % i, [128, KC, 256], BF16, ph) for i in range(NW)]
                wdn = sb("wdn_sb", [128, FG, D], BF16, ph)
                rmsnorm(gi)
                ffn(l, which, (act, sil, wgu, wdn))
                P.fence(scratch[:, 0:1])

        ident = sb("ident_sb", [128, 128], BF16)
        triu = sb("triu_sb", [128, 128], F32)
        onesf = sb("onesf_sb", [128, 128], F32)
        maskb = sb("mask_sb", [128, 4, TS], BF16)
        corev = sb("corev_sb", [128, 4], F32)
        mvec = sb("mvec_sb", [128, DEPTH, 8], F32)
        foxb = sb("foxb_sb", [128, DEPTH, NH], F32)
        pcnt = sb("pcnt_sb", [128, 2, 16], F32)
        invw = sb("invw_sb", [128, 2], F32)
        oneb = sb("oneb_sb", [128, 1], F32)
        zerob = sb("zerob_sb", [128, 1], F32)
        P.dma("pool", ident[:], ident_d, "c_ident", w=["ident"])
        P.dma("pool", maskb[:], mask_d, "c_mask", w=["maskb"])
        P.dma("sp", triu[:], triu_d, "c_triu", w=["triu"])
        P.dma("sp", corev[:], core_d, "c_core", w=["corev"])
        P.dma("sp", mvec[:], mvec_d, "c_mvec", w=["mvec"])
        P.dma("sp", foxb[:], foxb_d, "c_foxb", w=["foxb"])
        P.dma("sp", pcnt[:], pcnt_d, "c_pcnt", w=["pcnt"])
        P.dma("sp", invw[:], invw_d, "c_invw", w=["invw"])
        P.op("dve", lambda e: e.reciprocal(out=pcnt[:], in_=pcnt[:]), r=["pcnt"], w=["pcnt"])
        P.op("dve", lambda e: e.memset(onesf[:], 1.0), w=["onesf"])
        P.op("dve", lambda e: e.memset(oneb[:], 1.0), w=["oneb"])
        P.op("dve", lambda e: e.memset(zerob[:], 0.0), w=["zerob"])
        banks = pg + pu + py + pn
        bk = {"i": 0}

        def bank():
            i = bk["i"] % 8
            bk["i"] += 1
            return i

        def mm(i, out, lhsT, rhs, start, stop, r):
            P.op("pe", lambda e: e.matmul(out, lhsT=lhsT, rhs=rhs, start=start, stop=stop), r=list(r), w=[("bk", i)])

        def mix_phase(l, gi, parts=("mla", "pool", "fox")):
            rmsnorm(gi)
            tsls = [slice(t * TS, (t + 1) * TS) for t in range(NTS)]
            with ExitStack() as mx:
                u32 = sb("u32_sb", [128, 2, 16 + T], F32, mx)
                csall = sb("cs_sb", [128, NB, NH], F32, mx)
                def pool_mixer(pp):
                    poolw = sb("poolw_sb", [128, 2, 64], BF16, pp)
                    sA = sb("sA_sb", [128, 16 + TS], F32, pp)
                    sB = sb("sB_sb", [128, 16 + TS], F32, pp)
                    pl = [sb("pl%d" % i, [128, TS], BF16, pp) for i in range(2)]
                    fx = sb("fx_sb", [128, 16], F32, pp)
                    P.dma("pool", poolw[:], poolw_d[l].rearrange("(m p) d -> p m d", p=128), "poolw", w=["poolw"])
                    for m in range(2):
                        P.dma("sp", u32[:, m, 0:16], misc_out[l][0:128, 96 + m * 16:96 + (m + 1) * 16], ("halo", m), r=[("cc", "misc")], w=[("halo", m)])
                        P.op("dve", lambda e, m=m: e.tensor_scalar(out=u32[:, m, 0:16], in0=u32[:, m, 0:16], scalar1=corev[:, 1:2], scalar2=None, op0=ALU.mult),
                             r=[("halo", m), "corev"], w=[("halo", m)])
                    pending = []

                    def pm_tile(t):
                        for m in range(2):
                            a = u32[:, m, t * TS:t * TS + 16 + TS]
                            W = 16 + TS
                            rd = [("u32", m, t), ("halo", m)] + ([("u32", m, t - 1)] if t else [])
                            P.op("dve", lambda e, a=a: e.tensor_tensor(out=sA[:, 1:W], in0=a[:, 1:W], in1=a[:, 0:W - 1], op=ALU.add), r=rd, w=["sA"])
                            plm = pl[m]
                            pk = ("pl", m)
                            def fin(src, rows, skey, a=a, m=m, t=t, plm=plm, pk=pk):
                                P.op("dve", lambda e: e.scalar_tensor_tensor(out=plm[rows, :], in0=src[rows, 16:W], scalar=invw[rows, m:m + 1], in1=a[rows, 16:W],
                                                                           op0=ALU.mult, op1=ALU.subtract), r=[skey, "invw"], w=[pk])
                                if t == 0:
                                    P.op("dve", lambda e: e.tensor_tensor(out=fx[rows, :], in0=src[rows, 16:32], in1=pcnt[rows, m, :], op=ALU.mult),
                                         r=[skey, "pcnt"], w=["fx"])
                                    P.op("dve", lambda e: e.tensor_tensor(out=plm[rows, 0:16], in0=fx[rows, :], in1=a[rows, 16:32], op=ALU.subtract),
                                         r=["fx"], w=[pk])
                            if m == 0:
                                fin(sA, slice(0, 64), "sA")
                            P.op("dve", lambda e: e.tensor_tensor(out=sB[:, 3:W], in0=sA[:, 3:W], in1=sA[:, 1:W - 2], op=ALU.add), r=["sA"], w=["sB"])
                            if m == 0:
                                fin(sB, slice(64, 128), "sB")
                            else:
                                P.op("dve", lambda e: e.tensor_tensor(out=sA[:, 7:W], in0=sB[:, 7:W], in1=sB[:, 3:W - 4], op=ALU.add), r=["sB"], w=["sA"])
                                fin(sA, slice(0, 64), "sA")
                                P.op("dve", lambda e: e.tensor_tensor(out=sB[:, 15:W], in0=sA[:, 15:W], in1=sA[:, 7:W - 8], op=ALU.add), r=["sA"], w=["sB"])
                                fin(sB, slice(64, 128), "sB")
                            def tail(m=m, t=t, plm=plm, pk=pk):
                                for gg in range(2):
                                    g = m * 2 + gg
                                    rows = slice(gg * 64, gg * 64 + 64)
                                    i = bank()
                                    mm(i, banks[i][0:64, :], poolw[rows, m, :], plm[rows, :], True, True, ["poolw", pk])
                                    P.op("dve", lambda e, i=i, g=g, rows=rows, m=m, t=t: e.tensor_scalar(out=hT[rows, 3 + m, tsls[t]], in0=banks[i][0:64, :], scalar1=mvec[0:64, l, 3 + g:4 + g],
                                                                                                  scalar2=None, op0=ALU.mult),
                                         r=[("bk", i), "mvec"], w=[("mix", 3 + m, t, gg * 64), ("h", 3 + m, t)])
                            if pending:
                                pending.pop(0)()
                            pending.append(tail)

                    def pm_flush():
                        while pending:
                            pending.pop(0)()
                    return pm_tile, pm_flush

                with ExitStack() as pa:
                    win = sb("win_sb", [128, KC, N_IN], BF16, pa)
                    wkrot = sb("wkrot_sb", [128, KC, 32], BF16, pa)
                    wqb = sb("wqb_sb", [128, 2, 576], BF16, pa)
                    wqbrot = sb("wqbrot_sb", [128, 2, 192], BF16, pa)
                    wkvb = sb("wkvb_sb", [128, 768], BF16, pa)
                    ropet = [sb("rope%d" % i, [128, 2, TS], F32, pa) for i in range(2)]
                    qn = [sb("qn%d" % i, [128, 2, TS], BF16, pa) for i in range(2)]
                    kvn = [sb("kvn%d" % i, [128, TS], BF16, pa) for i in range(2)]
                    sqm = [sb("sqm%d" % i, [128, TS], BF16, pa) for i in range(2)]
                    rsm = [sb("rsm%d" % i, [128, TS], F32, pa) for i in range(2)]
                    r1 = sb("r1_sb", [128, TS], F32, pa)
                    r2 = sb("r2_sb", [128, TS], F32, pa)
                    NQS, NFS = 4, 2
                    qst = [sb("qst%d" % i, [96, TS], BF16, pa) for i in range(NQS)]
                    kst = [sb("kst%d" % i, [96, TS], BF16, pa) for i in range(NQS)]
                    fqst = [sb("fqst%d" % i, [128, TS], BF16, pa) for i in range(NFS)]
                    fkst = [sb("fkst%d" % i, [128, TS], BF16, pa) for i in range(NFS)]
                    drow = [sb("drow%d" % i, [8, TS], BF16, pa) for i in range(2)]
                    vst = [sb("vst%d" % i, [128, NH, 64], BF16, pa) for i in range(4)]
                    fb = sb("fb_sb", [128, NH], F32, pa)
                    spb = [sb("sp%d" % i, [128, NH], F32, pa) for i in range(2)]
                    spacc = sb("spacc_sb", [128, NH], F32, pa)
                    misc = sb("misc_sb", [128, 128], F32, pa)
                    P.dma("pool", win[:], w_in_d[l].rearrange("(k p) n -> p k n", p=128), "win", w=[("win", c) for c in range(KC)])
                    P.dma("pool", wkrot[:], w_krot_d[l].rearrange("(k p) n -> p k n", p=128), "wkrot", w=["wkrot"])
                    P.dma("pool", wqb[:], w_qb_d[l].rearrange("(k p) n -> p k n", p=128), "wqb", w=["wqb"])
                    P.dma("pool", wqbrot[:], w_qbrot_d[l].rearrange("(k p) n -> p k n", p=128), "wqbrot", w=["wqbrot"])
                    P.dma("pool", wkvb[:], w_kvb_d[l], "wkvb", w=["wkvb"])
                    P.op("pool", lambda e: e.memset(spacc[:], 0.0), w=["spacc"])
                    WIN = [("win", c) for c in range(KC)]
                    cn = {"q": 0, "k": 0, "fq": 0, "fk": 0, "v": 0, "sp": 0}

                    def proj(i, out, col0, ncol, t, wt=None):
                        for c in range(KC):
                            lh = (win if wt is None else wt)[:, c, col0:col0 + ncol]
                            mm(i, out, lh, hT[:, c, tsls[t]], c == 0, c == KC - 1, WIN + ["wkrot", ("h", c, t)])

                    def subnorm(t, srcs, mean_scale, gcols, dst, dkeys):
                        j = t % 2
                        ib = bank()
                        for m, (i, ap) in enumerate(srcs):
                            P.op("act", lambda e, ap=ap, m=m: e.activation(out=sqm[m % 2][:], in_=ap, func=AF.Square),
                                 r=[("bk", i)], w=[("sqm", m % 2)])
                            mm(ib, banks[ib][:], ones[:], sqm[m % 2][:], m == 0, m == len(srcs) - 1, [("sqm", m % 2), "ones"])
                        P.op("act", lambda e: e.activation(out=rsm[j][:], in_=banks[ib][:], func=AF.Ln, bias=epsb[:], scale=mean_scale),
                             r=[("bk", ib), "epsb"], w=[("rsm", j)])
                        P.op("act", lambda e: e.activation(out=rsm[j][:], in_=rsm[j][:], func=AF.Exp, scale=-0.5),
                             r=[("rsm", j)], w=[("rsm", j)])
                        for m, (i, ap) in enumerate(srcs):
                            P.op("dve", lambda e, ap=ap, m=m: e.scalar_tensor_tensor(out=dst[m], in0=ap, scalar=mvec[:, l, gcols[m]:gcols[m] + 1],
                                                                                  in1=rsm[j][:], op0=ALU.mult, op1=ALU.mult),
                                 r=[("bk", i), ("rsm", j), "mvec"], w=[dkeys[m]])

                    def rope_rows(ia, ib_, rp, out_list, okeys, t1, t2):
                        P.op("dve", lambda e: e.tensor_tensor(out=t1[64:96, :], in0=banks[ia][64:96, :], in1=rp[64:96, 0, :], op=ALU.mult),
                             r=[("bk", ia), "ropet"], w=[("t1", id(t1))])
                        P.op("dve", lambda e: e.tensor_tensor(out=t2[64:96, :], in0=banks[ib_][64:96, :], in1=rp[64:96, 1, :], op=ALU.mult),
                             r=[("bk", ib_), "ropet"], w=[("t2", id(t2))])
                        for o, ok in zip(out_list, okeys):
                            P.op("dve", lambda e, o=o: e.tensor_tensor(out=o, in0=t1[64:96, :], in1=t2[64:96, :], op=ALU.add),
                                 r=[("t1", id(t1)), ("t2", id(t2))], w=[ok])

                    m8all = sb("m8all_sb", [128, NB, NH], BF16, pa)

                    def evac_v(i, dst_d, blk):
                        vt = vst[cn["v"] % 4]
                        vk = ("vst", cn["v"] % 4)
                        cn["v"] += 1
                        P.op("act", lambda e: e.activation(out=vt[:].rearrange("p h d -> p (h d)"), in_=banks[i][:, 0:NH * 64], func=AF.Copy),
                             r=[("bk", i)], w=[vk])
                        name = "foxV_in" if dst_d is foxV_in else "mlaV_in"
                        P.dma("pool", dst_d[l].rearrange("(h t) d -> t h d", h=NH)[blk * 128:(blk + 1) * 128], vt[:], vk, r=[vk], w=[(name, blk)])

                    def tile_kv(t):
                        tsl = tsls[t]
                        rp = ropet[t % 2]
                        P.dma("sp", rp[:], rope_d[:, :, tsl], ("rope", t % 2), w=["ropet"])
                        ic = bank()
                        proj(ic, banks[ic][:], O_KVA, 128, t)
                        kj = kvn[t % 2]
                        subnorm(t, [(ic, banks[ic][:])], 8.0, [2], [kj[:]], [("kvn", t % 2)])
                        for bi in range(4):
                            blk = t * 4 + bi
                            bsl = slice(blk * 128, (blk + 1) * 128)
                            i = bank()
                            for c in range(KC):
                                mm(i, banks[i][:, 0:NH], hT[:, c, bsl], win[:, c, O_FF:O_FF + NH], c == 0, c == KC - 1, WIN + [("h", c, t)])
                            sp_ = spb[cn["sp"] % 2]
                            spk = ("sp", cn["sp"] % 2)
                            cn["sp"] += 1
                            P.op("dve", lambda e, i=i: e.tensor_tensor(out=fb[:], in0=banks[i][:, 0:NH], in1=foxb[:, l, :], op=ALU.add),
                                 r=[("bk", i), "foxb"], w=["fb"])
                            P.op("act", lambda e: e.activation(out=fb[:], in_=fb[:], func=AF.Exp, scale=-1.0), r=["fb"], w=["fb"])
                            P.op("act", lambda e, sp_=sp_: e.activation(out=sp_[:], in_=fb[:], func=AF.Ln, bias=oneb[:], scale=1.0),
                                 r=["fb", "oneb"], w=[spk])
                            i = bank()
                            for c in range(KC):
                                mm(i, banks[i][:, 0:NH * 64], hT[:, c, bsl], win[:, c, O_FV:O_FV + NH * 64], c == 0, c == KC - 1, WIN + [("h", c, t)])
                            evac_v(i, foxV_in, blk)
                            i2 = bank()
                            mm(i2, banks[i2][:, 0:NH], triu[:], sp_[:], True, False, ["triu", spk])
                            mm(i2, banks[i2][:, 0:NH], onesf[:], spacc[:], False, True, ["onesf", "spacc"])
                            P.op("act", lambda e, i2=i2, blk=blk: e.activation(out=csall[:, blk, :], in_=banks[i2][:, 0:NH], func=AF.Copy),
                                 r=[("bk", i2)], w=[("cs", blk)])
                            P.op("pool", lambda e, sp_=sp_: e.tensor_tensor(out=spacc[:], in0=spacc[:], in1=sp_[:], op=ALU.add),
                                 r=[spk, "spacc"], w=["spacc"])
                            P.op("dve", lambda e, blk=blk: e.tensor_scalar(out=m8all[:, blk, :], in0=csall[:, blk, :], scalar1=-8.0, scalar2=None, op0=ALU.mult),
                                 r=[("cs", blk)], w=[("m8", blk)])
                        for bi in range(4):
                            i = bank()
                            for h in range(NH):
                                mm(i, banks[i][:, h * 64:(h + 1) * 64], kj[:, bi * 128:(bi + 1) * 128], wkvb[:, h * 128 + 64:h * 128 + 128], True, True, [("kvn", t % 2), "wkvb"])
                            evac_v(i, mlaV_in, t * 4 + bi)
                        ika, ikb = bank(), bank()
                        proj(ika, banks[ika][64:96, :], O_KR, 32, t)
                        proj(ikb, banks[ikb][64:96, :], 0, 32, t, wt=wkrot)
                        K1, K2 = ("t1", id(r1)), ("t2", id(r2))
                        P.op("dve", lambda e: e.tensor_tensor(out=r1[64:96, :], in0=banks[ika][64:96, :], in1=rp[64:96, 0, :], op=ALU.mult),
                             r=[("bk", ika), "ropet"], w=[K1])
                        P.op("dve", lambda e: e.tensor_tensor(out=r2[64:96, :], in0=banks[ikb][64:96, :], in1=rp[64:96, 1, :], op=ALU.mult),
                             r=[("bk", ikb), "ropet"], w=[K2])
                        for h in range(NH):
                            kt = kst[cn["k"] % NQS]
                            kk = ("kst", cn["k"] % NQS)
                            cn["k"] += 1
                            i = bank()
                            mm(i, banks[i][0:64, :], wkvb[:, h * 128:h * 128 + 64], kj[:], True, True, [("kvn", t % 2), "wkvb"])
                            P.op("act", lambda e, i=i, kt=kt: e.activation(out=kt[0:64, :], in_=banks[i][0:64, :], func=AF.Copy), r=[("bk", i)], w=[kk])
                            P.op("dve", lambda e, kt=kt: e.tensor_tensor(out=kt[64:96, :], in0=r1[64:96, :], in1=r2[64:96, :], op=ALU.add), r=[K1, K2], w=[kk])
                            P.dma(("sp" if h % 2 == 0 else "pool"), mlaK_in[l][h // 3][(h % 3) * 96:(h % 3 + 1) * 96, tsl], kt[:], kk, r=[kk], w=[("mlaK_in", h, t)])
                            if h % 2 == 1:
                                hp = h // 2
                                ft = fkst[cn["fk"] % NFS]
                                fk = ("fkst", cn["fk"] % NFS)
                                cn["fk"] += 1
                                i = bank()
                                proj(i, banks[i][:], O_FK + hp * 128, 128, t)
                                P.op("act", lambda e, i=i, ft=ft: e.activation(out=ft[:], in_=banks[i][:], func=AF.Copy), r=[("bk", i)], w=[fk])
                                P.dma(("sp" if hp % 2 == 0 else "pool"), foxK_in[l][hp * 128:(hp + 1) * 128, tsl], ft[:], fk, r=[fk],
                                      w=[("foxK_in", h - 1, t), ("foxK_in", h, t)])
                        for m in range(2):
                            i = bank()
                            proj(i, banks[i][:], O_POOL + m * 128, 128, t)
                            P.op("act", lambda e, i=i, m=m: e.activation(out=u32[:, m, 16 + t * TS:16 + (t + 1) * TS], in_=banks[i][:], func=AF.Copy),
                                 r=[("bk", i)], w=[("u32", m, t)])

                    def qa_norm(t):
                        ia, ib2 = bank(), bank()
                        proj(ia, banks[ia][:], O_QA, 128, t)
                        proj(ib2, banks[ib2][:], O_QA + 128, 128, t)
                        qj = qn[t % 2]
                        subnorm(t, [(ia, banks[ia][:]), (ib2, banks[ib2][:])], 4.0, [0, 1], [qj[:, 0, :], qj[:, 1, :]], [("qn", t % 2, 0), ("qn", t % 2, 1)])

                    def tile_q_mla(t):
                        tsl = tsls[t]
                        rp = ropet[t % 2]
                        P.dma("sp", rp[:], rope_d[:, :, tsl], ("rope", t % 2), w=["ropet"])
                        if t + 1 < NTS:
                            qa_norm(t + 1)
                        qj = qn[t % 2]
                        QN = [("qn", t % 2, 0), ("qn", t % 2, 1)]
                        for h in range(NH):
                            qt_ = qst[cn["q"] % NQS]
                            qk = ("qst", cn["q"] % NQS)
                            cn["q"] += 1
                            i, ir = bank(), bank()
                            for m in range(2):
                                mm(i, banks[i][0:96, :], wqb[:, m, h * 96:(h + 1) * 96], qj[:, m, :], m == 0, m == 1, QN + ["wqb"])
                            for m in range(2):
                                mm(ir, banks[ir][64:96, :], wqbrot[:, m, h * 32:(h + 1) * 32], qj[:, m, :], m == 0, m == 1, QN + ["wqbrot"])
                            P.op("act", lambda e, i=i, qt_=qt_: e.activation(out=qt_[0:64, :], in_=banks[i][0:64, :], func=AF.Copy), r=[("bk", i)], w=[qk])
                            rope_rows(i, ir, rp, [qt_[64:96, :]], [qk], r1, r2)
                            P.dma(("sp" if h % 2 == 0 else "pool"), mlaQ_d[l][h // 3][(h % 3) * 96:(h % 3 + 1) * 96, tsl], qt_[:], qk, r=[qk], w=[("mlaQ", h, t)])

                    def tile_q_fox(t):
                        tsl = tsls[t]
                        i2 = bank()
                        for bi in range(4):
                            blk = t * 4 + bi
                            mm(i2, banks[i2][0:NH, bi * 128:(bi + 1) * 128], m8all[:, blk, :], ident[:], True, True, [("m8", blk), "ident"])
                        dr = drow[t % 2]
                        dk = ("drow", t % 2)
                        P.op("act", lambda e: e.activation(out=dr[0:NH, :], in_=banks[i2][0:NH, :], func=AF.Copy), r=[("bk", i2)], w=[dk])
                        P.dma("pool", foxQ_d[l].rearrange("(h r) t -> r h t", r=65)[64, :, tsl], dr[0:NH, :], dk, r=[dk], w=[("foxQd", t)])
                        for hp in range(NH // 2):
                            fq = fqst[cn["fq"] % NFS]
                            fqk = ("fqst", cn["fq"] % NFS)
                            cn["fq"] += 1
                            i = bank()
                            proj(i, banks[i][:], O_FQ + hp * 128, 128, t)
                            P.op("act", lambda e, i=i, fq=fq: e.activation(out=fq[:], in_=banks[i][:], func=AF.Copy), r=[("bk", i)], w=[fqk])
                            for k_ in range(2):
                                h = 2 * hp + k_
                                P.dma(("sp" if k_ == 0 else "pool"), foxQ_d[l][h * 65:h * 65 + 64, tsl], fq[k_ * 64:(k_ + 1) * 64, :], fqk, r=[fqk], w=[("foxQ", h, t)])

                    for t in range(NTS):
                        tile_kv(t)
                    it = bank()
                    mm(it, banks[it][:, 0:NH], onesf[:], spacc[:], True, True, ["onesf", "spacc"])
                    for blk in range(NB):
                        P.op("dve", lambda e, blk=blk: e.tensor_tensor(out=misc[:, blk * NH:(blk + 1) * NH], in0=csall[:, blk, :], in1=banks[it][:, 0:NH], op=ALU.subtract),
                             r=[("cs", blk), ("bk", it)], w=["misc"])
                    for m in range(2):
                        P.op("dve", lambda e, m=m: e.tensor_copy(out=misc[:, 96 + m * 16:96 + (m + 1) * 16], in_=u32[:, m, T:T + 16]),
                             r=[("u32", m, NTS - 1)], w=["misc"])
                    P.dma("sp", misc_in[l], misc[:], "misc", r=["misc"], w=["misc_in"])
                    def gather(name, src, dst, rkeys):
                        P.cc(lambda g: g.collective_compute("AllGather", ALU.bypass, replica_groups=PAIRS, ins=[src], outs=[dst]),
                             ("cc", name, l), r=rkeys, w=[("cc", name)])
                    HT = [(h, t) for h in range(NH) for t in range(NTS)]
                    gather("misc", misc_in[l], misc_out[l], ["misc_in"])
                    gather("mlaK0", mlaK_in[l][0], mlaK_out[l][0], [("mlaK_in", h, t) for h, t in HT if h < 3])
                    gather("mlaK1", mlaK_in[l][1], mlaK_out[l][1], [("mlaK_in", h, t) for h, t in HT if h >= 3])
                    gather("mlaV", mlaV_in[l], mlaV_out[l], [("mlaV_in", b_) for b_ in range(NB)])
                    gather("foxK", foxK_in[l], foxK_out[l], [("foxK_in", h, t) for h, t in HT])
                    gather("foxV", foxV_in[l], foxV_out[l], [("foxV_in", b_) for b_ in range(NB)])
                    pm_tile, pm_flush = pool_mixer(pa) if "pool" in parts else ((lambda t: None), (lambda: None))
                    qa_norm(0)
                    for t in range(NTS):
                        tile_q_mla(t)
                    gather("mlaQ0", mlaQ_d[l][0], mlaQ_out[l][0], [("mlaQ", h, t) for h, t in HT if h < 3])
                    gather("mlaQ1", mlaQ_d[l][1], mlaQ_out[l][1], [("mlaQ", h, t) for h, t in HT if h >= 3])
                    for t in range(NTS):
                        tile_q_fox(t)
                        pm_tile(t)
                    pm_flush()
                    gather("foxQ", foxQ_d[l], foxQ_out[l], [("foxQ", h, t) for h, t in HT] + [("foxQd", t) for t in range(NTS)])
                    P.fence(scratch[:, 1:2])
                with ExitStack() as pb:
                    Kh = [sb("Kh%d" % i, [96, T], BF16, pb) for i in range(2)]
                    Vh = [sb("Vh%d" % i, [128, NB, 128], BF16, pb) for i in range(2)]
                    Qh = [sb("Qh%d" % i, [96, T], BF16, pb) for i in range(2)]
                    Kc = [sb("Kc%d" % i, [96, T], BF16, pb) for i in range(2)]
                    Vc = [sb("Vc%d" % i, [128, NB, 64], BF16, pb) for i in range(2)]
                    Qc = [sb("Qc%d" % i, [96, T], BF16, pb) for i in range(2)]
                    pt = [sb("pt%d" % i, [128, TS], BF16, pb) for i in range(3)]
                    rct = sb("rct_sb", [128, TS], F32, pb)
                    comb = sb("comb_sb", [128, TS], F32, pb)
                    part = [sb("part%d" % i, [128, TS], F32, pb) for i in range(2)]
                    rst = [sb("rst%d" % i, [128, TS], F32, pb) for i in range(2)]
                    negraw = sb("negraw_sb", [128, NB, NH], F32, pb)
                    negsel = sb("negsel_sb", [128, NB, 3], F32, pb)
                    fA, fB = corev[:, 2:3], corev[:, 1:2]
                    for i in range(2):
                        P.op("pool", lambda e, i=i: e.memset(Vh[i][:, :, 64:128], 1.0), w=[("Vh", i)])
                        P.op("pool", lambda e, i=i: e.memset(Kh[i][64:65, :], 1.0), w=[("Kh", i)])
                    P.dma("sp", negraw[:].rearrange("p b h -> p (b h)"), misc_out[l][0:128, 0:NB * NH], "negraw", r=[("cc", "misc")], w=["negraw"])
                    P.op("dve", lambda e: e.tensor_scalar(out=negsel[:], in0=negraw[:, :, 0:3], scalar1=fA, scalar2=None, op0=ALU.mult), r=["negraw", "corev"], w=["negsel"])
                    P.op("dve", lambda e: e.scalar_tensor_tensor(out=negsel[:], in0=negraw[:, :, 3:6], scalar=fB, in1=negsel[:], op0=ALU.mult, op1=ALU.add),
                         r=["negraw", "corev", "negsel"], w=["negsel"])
                    ptn = {"i": 0}
                    pon = {"i": 0}
                    last_fam = {0: None, 1: None}

                    def blend(dst, ca, cb, srca, srcb, skey, rkeys, dkey, np_):
                        fa, fb = corev[0:np_, 2:3], corev[0:np_, 1:2]
                        P.dma("sp", ca, srca, (skey, "a"), r=rkeys, w=[(skey, "a")])
                        P.dma("sp", cb, srcb, (skey, "b"), r=rkeys, w=[(skey, "b")])
                        P.op("act", lambda e: e.activation(out=dst, in_=ca, func=AF.Copy, scale=fa), r=[(skey, "a"), "corev"], w=[dkey])
                        P.op("dve", lambda e: e.scalar_tensor_tensor(out=dst, in0=cb, scalar=fb, in1=dst, op0=ALU.mult, op1=ALU.add), r=[(skey, "b"), "corev", dkey], w=[dkey])

                    def fix_k_row(sl, fam):
                        if fam == "fox" and last_fam[sl] == "mla":
                            P.op("pool", lambda e: e.memset(Kh[sl][64:65, :], 1.0), r=[("Kh", sl)], w=[("Kh", sl)])
                        last_fam[sl] = fam

                    def attend(sl, fam, kdq, qt, blocks, bias_of, ipo):
                        tsl = tsls[qt]
                        sc = MLA_SCALE if fam == "mla" else 0.125
                        pend = []

                        def score(kb, diag):
                            i = bk["i"] % 6
                            bk["i"] += 1
                            q0 = max(diag, 0) * 128
                            ksl = slice(kb * 128, (kb + 1) * 128)
                            qsl = slice(tsl.start + q0, tsl.stop)
                            P.op("pe", lambda e: e.matmul(banks[i][:, q0:], lhsT=Kh[sl][0:kdq, ksl], rhs=Qh[sl][0:kdq, qsl], start=True, stop=(diag < 0)),
                                 r=[("Kh", sl), ("Qh", sl)], w=[("bk", i)])
                            if diag >= 0:
                                P.op("pe", lambda e: e.matmul(banks[i][:, q0:], lhsT=ident[:], rhs=maskb[:, diag, q0:], start=False, stop=True),
                                     r=["ident", "maskb"], w=[("bk", i)])
                            p_ = ptn["i"] % 3
                            ptn["i"] += 1
                            bias, bkeys = bias_of(kb)
                            P.op("act", lambda e: e.activation(out=pt[p_][:, q0:], in_=banks[i][:, q0:], func=AF.Exp, bias=bias, scale=sc),
                                 r=[("bk", i)] + bkeys, w=[("pt", p_)])
                            return kb, p_, q0

                        def pv(kb, p_, q0, first, last):
                            P.op("pe", lambda e: e.matmul(banks[ipo][:, q0:], lhsT=Vh[sl][:, kb, :], rhs=pt[p_][:, q0:], start=first, stop=last),
                                 r=[("Vh", sl), ("pt", p_)], w=[("bk", ipo)])

                        first_kb = blocks[0][0]
                        for kb, diag in blocks:
                            pend.append(score(kb, diag))
                            if len(pend) > 2:
                                k0, p0, c0 = pend.pop(0)
                                pv(k0, p0, c0, k0 == first_kb, False)
                        while pend:
                            k0, p0, c0 = pend.pop(0)
                            pv(k0, p0, c0, k0 == first_kb, not pend)

                    items = []
                    rect = ([("mla", j) for j in range(3)] if "mla" in parts else []) + ([("fox", j) for j in range(3)] if "fox" in parts else [])
                    heads = ([("mla", h) for h in range(NH)] if "mla" in parts else []) + ([("fox", h) for h in range(NH)] if "fox" in parts else [])
                    pn_ = {"i": 0}

                    def make_rect(fam, j, sl):
                        jj = j if fam == "mla" else 3 + j
                        kd, kdq = (96, 96) if fam == "mla" else (64, 65)
                        if fam == "mla":
                            Ka, Kb_ = mlaK_out[l][0][j * 96:(j + 1) * 96, :], mlaK_out[l][1][j * 96:(j + 1) * 96, :]
                            Qa, Qb_ = mlaQ_out[l][0][288 + j * 96:288 + (j + 1) * 96, :], mlaQ_out[l][1][288 + j * 96:288 + (j + 1) * 96, :]
                            Vv = mlaV_out[l].rearrange("(h b p) d -> p h b d", h=2 * NH, b=NB)
                            rk, rq, rv = [("cc", "mlaK0"), ("cc", "mlaK1")], [("cc", "mlaQ0"), ("cc", "mlaQ1")], [("cc", "mlaV")]
                        else:
                            Ka, Kb_ = foxK_out[l][j * 64:(j + 1) * 64, :], foxK_out[l][(j + 3) * 64:(j + 4) * 64, :]
                            Qa, Qb_ = foxQ_out[l][390 + j * 65:390 + (j + 1) * 65, :], foxQ_out[l][390 + (j + 3) * 65:390 + (j + 4) * 65, :]
                            Vv = foxV_out[l].rearrange("(h b p) d -> p h b d", h=2 * NH, b=NB)
                            rk, rq, rv = [("cc", "foxK")], [("cc", "foxQ")], [("cc", "foxV")]
                        specs = [(Kh[sl][0:kd, :], Kc[0][0:kd, :], Kc[1][0:kd, :], Ka, Kb_, "Kc", rk, ("Kh", sl), kd),
                                 (Vh[sl][:, :, 0:64], Vc[0][:], Vc[1][:], Vv[:, j], Vv[:, j + 3], "Vc", rv, ("Vh", sl), 128),
                                 (Qh[sl][0:kdq, :], Qc[0][0:kdq, :], Qc[1][0:kdq, :], Qa, Qb_, "Qc", rq, ("Qh", sl), kdq)]

                        def loads():
                            for dst, ca, cb, srca, srcb, skey, rkeys, dkey, np_ in specs:
                                P.dma("sp", ca, srca, (skey, "a"), r=rkeys, w=[(skey, "a")])
                                P.dma("sp", cb, srcb, (skey, "b"), r=rkeys, w=[(skey, "b")])

                        def prep():
                            for n_, (dst, ca, cb, srca, srcb, skey, rkeys, dkey, np_) in enumerate(specs):
                                fa, fb = corev[0:np_, 2:3], corev[0:np_, 1:2]
                                P.op("dve", lambda e, dst=dst, ca=ca, fa=fa: e.tensor_scalar(out=dst, in0=ca, scalar1=fa, scalar2=None, op0=ALU.mult), r=[(skey, "a"), "corev"], w=[dkey])
                                P.op("dve", lambda e, dst=dst, cb=cb, fb=fb: e.scalar_tensor_tensor(out=dst, in0=cb, scalar=fb, in1=dst, op0=ALU.mult, op1=ALU.add),
                                     r=[(skey, "b"), "corev", dkey], w=[dkey])
                                if n_ == 0:
                                    fix_k_row(sl, fam)

                        def run(mid):
                            for qt in range(NTS):
                                ipo = 6 + pon["i"] % 2
                                pon["i"] += 1
                                if fam == "mla":
                                    bias_of = lambda kb: (zerob[:], ["zerob"])
                                else:
                                    bias_of = lambda kb: (negsel[:, kb, j:j + 1], ["negsel"])
                                attend(sl, fam, kdq, qt, [(kb, -1) for kb in range(NB)], bias_of, ipo)
                                r_ = rst[pon["i"] % 2]
                                rkey = ("rst", pon["i"] % 2)
                                P.op("dve", lambda e, ipo=ipo, r_=r_: e.tensor_copy(out=r_[:], in_=banks[ipo][:]), r=[("bk", ipo)], w=[rkey])
                                P.dma("pool", rout_in[l][jj // 2][(jj % 2) * 128:(jj % 2 + 1) * 128, tsls[qt]], r_[:], rkey, r=[rkey], w=[("rout_in", jj, qt)])
                                if qt == 1:
                                    mid()
                            if jj % 2 == 1:
                                g_ = jj // 2
                                P.cc(lambda g: g.collective_compute("AllGather", ALU.bypass, replica_groups=PAIRS, ins=[rout_in[l][g_]], outs=[rout_out[l][g_]]),
                                     ("cc", "rout%d" % g_, l), r=[("rout_in", x, qt) for x in (jj - 1, jj) for qt in range(NTS)], w=[("cc", "rout", g_)])
                        return loads, prep, run

                    def make_tri(fam, h, sl):
                        kd, kdq = (96, 96) if fam == "mla" else (64, 65)
                        jj = (h % 3) if fam == "mla" else 3 + (h % 3)
                        slot = h // 3
                        if fam == "mla":
                            Ko, Qo, Vo = mlaK_in[l][h // 3][(h % 3) * 96:(h % 3 + 1) * 96, :], mlaQ_d[l][h // 3][(h % 3) * 96:(h % 3 + 1) * 96, :], mlaV_in[l]
                            kn, vn, qn_ = "mlaK_in", "mlaV_in", "mlaQ"
                        else:
                            Ko, Qo, Vo = foxK_in[l][h * 64:(h + 1) * 64, :], foxQ_d[l][h * 65:(h + 1) * 65, :], foxV_in[l]
                            kn, vn, qn_ = "foxK_in", "foxV_in", "foxQ"
                        chunk = (h // 2) if fam == "mla" else (5 + h // 2)
                        prow = (h % 2) * 64

                        def loads():
                            P.dma("sp", Kh[sl][0:kd, :], Ko, ("Kho", sl), r=[(kn, h, t) for t in range(NTS)], w=[("Kh", sl)])
                            fix_k_row(sl, fam)
                            P.dma("sp", Vh[sl][:, :, 0:64], Vo.rearrange("(h b p) d -> p h b d", h=NH, b=NB)[:, h], ("Vho", sl), r=[(vn, b_) for b_ in range(NB)], w=[("Vh", sl)])
                            P.dma("sp", Qh[sl][0:kdq, :], Qo, ("Qho", sl), r=[(qn_, h, t) for t in range(NTS)] + ([("foxQd", t) for t in range(NTS)] if fam == "fox" else []), w=[("Qh", sl)])

                        def prep():
                            pass

                        def run(mid):
                            for qt in range(NTS):
                                ipo = 6 + pon["i"] % 2
                                pon["i"] += 1
                                pa_ = part[pn_["i"] % 2]
                                pkey = ("part", pn_["i"] % 2)
                                pn_["i"] += 1
                                P.dma("sp", pa_[:], rout_out[l][jj // 2][slot * 256 + (jj % 2) * 128:slot * 256 + (jj % 2 + 1) * 128, tsls[qt]], pkey,
                                      r=[("cc", "rout", jj // 2)], w=[pkey])
                                if fam == "mla":
                                    bias_of = lambda kb: (zerob[:], ["zerob"])
                                else:
                                    bias_of = lambda kb: (csall[:, kb, h:h + 1], [("cs", kb)])
                                attend(sl, fam, kdq, qt, [(kb, kb - 4 * qt) for kb in range(4 * (qt + 1))], bias_of, ipo)
                                tsl = tsls[qt]
                                P.op("dve", lambda e, ipo=ipo, pa_=pa_: e.scalar_tensor_tensor(out=comb[:], in0=pa_[:], scalar=fB, in1=banks[ipo][:], op0=ALU.mult, op1=ALU.add),
                                     r=[("bk", ipo), pkey, "corev"], w=["comb"])
                                P.op("dve", lambda e: e.reciprocal(out=rct[0:64, :], in_=comb[64:128, :]), r=["comb"], w=["rct"])
                                P.op("dve", lambda e, tsl=tsl: e.tensor_tensor(out=hT[prow:prow + 64, chunk, tsl], in0=comb[0:64, :], in1=rct[0:64, :], op=ALU.mult),
                                     r=["comb", "rct"], w=[("mix", chunk, qt, prow)])
                                if qt == 1:
                                    mid()
                        return loads, prep, run

                    for n_, (fam, j) in enumerate(rect):
                        items.append(make_rect(fam, j, n_ % 2))
                    for n_, (fam, h) in enumerate(heads):
                        items.append(make_tri(fam, h, (len(rect) + n_) % 2))
                    if items:
                        items[0][0]()
                        items[0][1]()
                    for n_, (loads, prep, run) in enumerate(items):
                        nxt = items[n_ + 1] if n_ + 1 < len(items) else None
                        if nxt:
                            nxt[0]()
                        run(nxt[1] if nxt else (lambda: None))
                    P.fence(scratch[:, 2:3])
                with ExitStack() as pc:
                    wout = sb("wout_sb", [128, KC, D], BF16, pc)
                    P.dma("pool", wout[:], w_out_d[l].rearrange("(k p) n -> p k n", p=128), "wout", w=[("wout", c) for c in range(KC)])
                    zero_chunks = ([] if "mla" in parts else [0, 1, 2]) + ([] if "pool" in parts else [3, 4]) + ([] if "fox" in parts else [5, 6, 7])
                    for c in zero_chunks:
                        P.op("pool", lambda e, c=c: e.memset(hT[:, c, :], 0.0), w=[("mix", c, t, pr) for t in range(NTS) for pr in (0, 64)])
                    for t in range(NTS):
                        for dc in range(KC):
                            i = bank()
                            for c in range(KC):
                                mm(i, banks[i][:], wout[:, c, dc * 128:(dc + 1) * 128], hT[:, c, tsls[t]], c == 0, c == KC - 1, [("wout", c), ("mix", c, t, 0), ("mix", c, t, 64)])
                            P.op("dve", lambda e, i=i, dc=dc, t=t: e.tensor_tensor(out=xT[:, dc, tsls[t]], in0=banks[i][:], in1=xT[:, dc, tsls[t]], op=ALU.add),
                                 r=[("bk", i), ("x", dc, t)], w=[("x", dc, t)])
                    P.fence(scratch[:, 3:4])

        if stage == "ffn1":
            ffn_phase(0, 0, 0)
        elif stage == "ffn1n":
            ffn_phase(0, 0, 0)
            rmsnorm(NG - 1, out_fp32=True)
        elif stage.startswith("mix_"):
            mix_phase(0, 1, tuple(stage[4:].split("+")))
        elif stage == "full":
            for l in range(DEPTH):
                ffn_phase(l, 0, 3 * l)
                mix_phase(l, 3 * l + 1)
                ffn_phase(l, 1, 3 * l + 2)
            rmsnorm(NG - 1, out_fp32=True)
        fin = [P.dma("sp", outT_d.rearrange("(k p) t -> p k t", p=128)[:, :, t * TS:(t + 1) * TS], xT[:, :, t * TS:(t + 1) * TS], ("o", t),
                     r=[("x", c, t) for c in range(KC)]) for t in range(NTS)]
        P.emit(final_wait_ops=fin)
    return nc


def _gains(inputs):
    rows = []
    for l in range(DEPTH):
        rows += [inputs["ffn1_norm"][l], inputs["mix_norm"][l], inputs["ffn2_norm"][l]]
    rows.append(inputs["final_norm"])
    g = np.stack([np.asarray(r, np.float32) for r in rows])
    return np.ascontiguousarray(g.reshape(len(rows), KC, 128).transpose(2, 0, 1))


def _constants(half):
    c = {}
    pos = (half * T + np.arange(T)).astype(np.float32)
    inv_freq = (np.float32(10000.0) ** (-(np.arange(0, 32, 2, dtype=np.float32) / np.float32(32)))).astype(np.float32)
    ang = (pos[:, None] * inv_freq[None, :]).astype(np.float32)
    cos, sin = np.cos(ang).astype(np.float32).T, np.sin(ang).astype(np.float32).T
    rope = np.zeros((128, 2, T), np.float32)
    rope[64:80, 0], rope[80:96, 0] = cos, cos
    rope[64:80, 1], rope[80:96, 1] = -sin, sin
    c["rope"] = rope
    k = np.arange(128)[:, None, None]
    r = np.arange(4)[None, :, None]
    q = np.arange(TS)[None, None, :]
    c["maskT"] = np.where(q >= r * 128 + k, 0.0, NEG).astype(np.float32)
    c["ident"] = np.eye(128, dtype=np.float32)
    c["triu"] = np.triu(np.ones((128, 128), np.float32))
    corev = np.zeros((128, 4), np.float32)
    corev[:, 0] = 0.0 if half == 1 else NEG
    corev[:, 1] = 1.0 if half == 1 else 0.0
    corev[:, 2] = 1.0 if half == 0 else 0.0
    c["corev"] = corev
    wins = np.array([2.0, 4.0, 8.0, 16.0], np.float32)
    g_of = (np.arange(2)[None, :] * 2 + (np.arange(128)[:, None] // 64))
    w_of = wins[g_of]
    count = (half * T + np.arange(16) + 1).astype(np.float32)
    c["pcnt"] = np.minimum(count[None, None, :], w_of[:, :, None]).astype(np.float32)
    c["invw"] = (1.0 / w_of).astype(np.float32)
    return c


def _layouts(inputs):
    f = lambda k: np.ascontiguousarray(np.asarray(inputs[k], np.float32))
    m = {k: f(k) for k in ("ffn1_w_gu", "ffn1_w_down", "ffn2_w_gu", "ffn2_w_down", "w_in", "w_q_b", "w_kv_b", "w_out")}
    swap = np.concatenate([np.arange(16, 32), np.arange(0, 16)])
    m["w_in_krot"] = np.ascontiguousarray(m["w_in"][:, :, O_KR:O_KR + 32][:, :, swap])
    qb = m["w_q_b"].reshape(DEPTH, 256, NH, 96)[:, :, :, 64:96][:, :, :, swap]
    m["w_q_b_rot"] = np.ascontiguousarray(qb.reshape(DEPTH, 256, NH * 32))
    m["pool_w"] = np.ascontiguousarray(f("pool_w").reshape(DEPTH, 256, 64))
    mvec = np.zeros((128, DEPTH, 8), np.float32)
    qa, kva, psc, fbf = f("q_a_norm"), f("kv_a_norm"), f("pool_scale"), f("fox_b_f")
    for l in range(DEPTH):
        mvec[:, l, 0], mvec[:, l, 1], mvec[:, l, 2] = qa[l, 0:128], qa[l, 128:256], kva[l]
        for g in range(4):
            mvec[0:64, l, 3 + g] = psc[l, g * 64:(g + 1) * 64]
    m["mvec"] = mvec
    m["fox_b"] = np.ascontiguousarray(np.broadcast_to(fbf[None], (128, DEPTH, NH)))
    m["gains"] = _gains(inputs)
    return m


def kernel(_stage="full", **inputs):
    x = np.asarray(inputs["x"], np.float32)
    nc = build_nc(_stage)
    shared = _layouts(inputs)
    consts = [_constants(0), _constants(1)]
    in_maps = []
    for c in range(NCORES):
        b, h = c // 2, c % 2
        m = dict(shared)
        m.update(consts[h])
        m["xT"] = np.ascontiguousarray(x[b, h * T:(h + 1) * T, :].T)
        in_maps.append(m)
    res = run_bass_kernel_spmd(nc, in_maps, core_ids=list(range(NCORES)))
    out = np.empty((B, S, D), np.float32)
    for c in range(NCORES):
        b, h = c // 2, c % 2
        out[b, h * T:(h + 1) * T, :] = res.results[c]["outT"].T
    return out
```

```python
from contextlib import ExitStack
import numpy as np
import concourse.bass as bass
import concourse.mybir as mybir
from concourse.bass_utils import run_bass_kernel_spmd

F32 = mybir.dt.float32
BF16 = mybir.dt.bfloat16
AF = mybir.ActivationFunctionType
ALU = mybir.AluOpType

D = 1024
KC = D // 128
DEPTH = 2
B, S = 4, 4096
NCORES = 8
T = S // 2
TS = 512
NTS = T // TS
D_FF = 2816
FC = D_FF // 128
FGROUPS = 2
EPS = 1e-6
NB = T // 128
NH = 6
N_IN = 1830
O_QA, O_KVA, O_KR, O_POOL, O_FQ, O_FK, O_FV, O_FF = 0, 256, 384, 416, 672, 1056, 1440, 1824
NEG = -30000.0
MLA_SCALE = 1.0 / float(np.sqrt(96.0))

ENGINES = ("pe", "act", "dve", "pool", "sp")


class Prog:
    def __init__(self, nc):
        self.nc = nc
        self.ops = []
        self.last_w = {}
        self.readers = {}
        self.fence_op = None
        self.fence_start = 0

    def fence(self, scratch):
        deps = set()
        last = {}
        for i in range(self.fence_start, len(self.ops)):
            o = self.ops[i]
            if o["dma"]:
                if o["inc"] != 1:
                    deps.add(i)
            else:
                last[o["engine"]] = i
        deps.update(last.values())
        oid = self._add("dve", lambda e: e.memset(scratch, 0.0), (), ())
        self.ops[oid]["deps"].update(deps)
        self.fence_op = oid
        self.fence_start = oid
        return oid

    def _add(self, engine, fn, r, w, dma=False, semkey=None, inc=16):
        oid = len(self.ops)
        deps = set()
        if self.fence_op is not None:
            deps.add(self.fence_op)
        for k in r:
            if k in self.last_w:
                deps.add(self.last_w[k])
        for k in w:
            if k in self.last_w:
                deps.add(self.last_w[k])
            deps.update(self.readers.get(k, ()))
        for k in r:
            self.readers.setdefault(k, []).append(oid)
        for k in w:
            self.last_w[k] = oid
            self.readers[k] = []
        self.ops.append(dict(engine=engine, fn=fn, deps=deps, dma=dma, semkey=semkey, inc=inc))
        return oid

    def op(self, engine, fn, r=(), w=()):
        return self._add(engine, fn, tuple(r), tuple(w))

    def dma(self, engine, out, in_, semkey, r=(), w=()):
        return self._add(engine, lambda e: e.dma_start(out=out, in_=in_), tuple(r), tuple(w),
                         dma=True, semkey=semkey)

    def cc(self, fn, semkey, r=(), w=()):
        return self._add("pool", fn, tuple(r), tuple(w), dma=True, semkey=semkey, inc=1)

    def emit(self, final_wait_ops=()):
        nc, ops = self.nc, self.ops
        signalled = set()
        for o in ops:
            for d in o["deps"]:
                do = ops[d]
                if not do["dma"] and (o["dma"] or do["engine"] != o["engine"] or o["engine"] != "pe"):
                    signalled.add(d)
        for d in final_wait_ops:
            if not ops[d]["dma"]:
                signalled.add(d)
        eng_count = {e: 0 for e in ENGINES}
        sig_val, dma_count = {}, {}
        for i, o in enumerate(ops):
            if o["dma"]:
                k = o["semkey"]
                dma_count[k] = dma_count.get(k, 0) + o["inc"]
                sig_val[i] = dma_count[k]
            elif i in signalled:
                eng_count[o["engine"]] += 1
                sig_val[i] = eng_count[o["engine"]]
        semkeys = sorted(set(o["semkey"] for o in ops if o["dma"]), key=str)
        self.n_sems = len(semkeys) + len(ENGINES)
        with ExitStack() as st:
            esem = {e: st.enter_context(nc.semaphore("e_" + e)) for e in ENGINES}
            dsem = {k: st.enter_context(nc.semaphore("d_%d" % j)) for j, k in enumerate(semkeys)}
            block = st.enter_context(nc.Block())
            streams = {e: [i for i, o in enumerate(ops) if o["engine"] == e] for e in ENGINES}

            def run(e, eng):
                waited = {}
                for i in streams[e]:
                    o = ops[i]
                    need = {}
                    for d in o["deps"]:
                        do = ops[d]
                        if do["dma"]:
                            key, sem = ("d", do["semkey"]), dsem[do["semkey"]]
                        else:
                            if do["engine"] == e and not o["dma"] and e == "pe":
                                continue
                            key, sem = ("e", do["engine"]), esem[do["engine"]]
                        if need.get(key, (None, 0))[1] < sig_val[d]:
                            need[key] = (sem, sig_val[d])
                    for key, (sem, v) in need.items():
                        if waited.get(key, 0) >= v:
                            continue
                        waited[key] = v
                        eng.wait_ge(sem, v)
                    ins = o["fn"](eng)
                    if o["dma"]:
                        ins.then_inc(dsem[o["semkey"]], o["inc"])
                    elif i in signalled:
                        ins.then_inc(esem[e], 1)
                if e == "sp":
                    for d in final_wait_ops:
                        do = ops[d]
                        eng.wait_ge(dsem[do["semkey"]] if do["dma"] else esem[do["engine"]], sig_val[d])

            @block.tensor
            def _(eng):
                run("pe", eng)

            @block.scalar
            def _(eng):
                run("act", eng)

            @block.vector
            def _(eng):
                run("dve", eng)

            @block.gpsimd
            def _(eng):
                run("pool", eng)

            @block.sync
            def _(eng):
                run("sp", eng)


def build_nc(stage="full"):
    nc = bass.Bass("TRN2", target_bir_lowering=False)
    xT_d = nc.dram_tensor("xT", [D, T], F32, kind="ExternalInput").ap()
    outT_d = nc.dram_tensor("outT", [D, T], F32, kind="ExternalOutput").ap()
    NG = 3 * DEPTH + 1
    gains_d = nc.dram_tensor("gains", [128, NG, KC], F32, kind="ExternalInput").ap()
    wgu_d = [nc.dram_tensor("ffn%d_w_gu" % i, [DEPTH, D, 2 * D_FF], F32, kind="ExternalInput").ap() for i in (1, 2)]
    wdn_d = [nc.dram_tensor("ffn%d_w_down" % i, [DEPTH, D_FF, D], F32, kind="ExternalInput").ap() for i in (1, 2)]
    FG = FC // FGROUPS
    NW = 3
    din = lambda n, shp: nc.dram_tensor(n, shp, F32, kind="ExternalInput").ap()
    w_in_d = din("w_in", [DEPTH, D, N_IN])
    w_krot_d = din("w_in_krot", [DEPTH, D, 32])
    w_qb_d = din("w_q_b", [DEPTH, 256, 576])
    w_qbrot_d = din("w_q_b_rot", [DEPTH, 256, 192])
    w_kvb_d = din("w_kv_b", [DEPTH, 128, 768])
    poolw_d = din("pool_w", [DEPTH, 256, 64])
    w_out_d = din("w_out", [DEPTH, D, D])
    mvec_d = din("mvec", [128, DEPTH, 8])
    foxb_d = din("fox_b", [128, DEPTH, NH])
    rope_d = din("rope", [128, 2, T])
    mask_d = din("maskT", [128, 4, TS])
    ident_d = din("ident", [128, 128])
    triu_d = din("triu", [128, 128])
    core_d = din("corev", [128, 4])
    pcnt_d = din("pcnt", [128, 2, 16])
    invw_d = din("invw", [128, 2])
    dsc = lambda n, shp, dt: nc.dram_tensor(n, shp, dt, kind="Internal").ap()
    mlaK_in = [[dsc("mlaK_in%d_%d" % (l, g), [3 * 96, T], BF16) for g in range(2)] for l in range(DEPTH)]
    mlaK_out = [[dsc("mlaK_out%d_%d" % (l, g), [2 * 3 * 96, T], BF16) for g in range(2)] for l in range(DEPTH)]
    foxK_in = [dsc("foxK_in%d" % l, [NH * 64, T], BF16) for l in range(DEPTH)]
    foxK_out = [dsc("foxK_out%d" % l, [2 * NH * 64, T], BF16) for l in range(DEPTH)]
    mlaV_in = [dsc("mlaV_in%d" % l, [NH * T, 64], BF16) for l in range(DEPTH)]
    mlaV_out = [dsc("mlaV_out%d" % l, [2 * NH * T, 64], BF16) for l in range(DEPTH)]
    foxV_in = [dsc("foxV_in%d" % l, [NH * T, 64], BF16) for l in range(DEPTH)]
    foxV_out = [dsc("foxV_out%d" % l, [2 * NH * T, 64], BF16) for l in range(DEPTH)]
    misc_in = [dsc("misc_in%d" % l, [128, 128], F32) for l in range(DEPTH)]
    misc_out = [dsc("misc_out%d" % l, [256, 128], F32) for l in range(DEPTH)]
    mlaQ_d = [[dsc("mlaQ%d_%d" % (l, g), [3 * 96, T], BF16) for g in range(2)] for l in range(DEPTH)]
    mlaQ_out = [[dsc("mlaQ_out%d_%d" % (l, g), [2 * 3 * 96, T], BF16) for g in range(2)] for l in range(DEPTH)]
    foxQ_d = [dsc("foxQ%d" % l, [NH * 65, T], BF16) for l in range(DEPTH)]
    foxQ_out = [dsc("foxQ_out%d" % l, [2 * NH * 65, T], BF16) for l in range(DEPTH)]
    rout_in = [[dsc("rout_in%d_%d" % (l, g), [256, T], F32) for g in range(3)] for l in range(DEPTH)]
    rout_out = [[dsc("rout_out%d_%d" % (l, g), [512, T], F32) for g in range(3)] for l in range(DEPTH)]
    PAIRS = [[0, 1], [2, 3], [4, 5], [6, 7]]

    with ExitStack() as st:
        uniq = {"n": 0}

        def sb(name, shape, dt, stack=st):
            uniq["n"] += 1
            return stack.enter_context(nc.sbuf_tensor("%s_%d" % (name, uniq["n"]), shape, dt))

        def ps(name):
            return st.enter_context(nc.psum_tensor(name, [128, TS], F32))

        P = Prog(nc)
        xT = sb("xT_sb", [128, KC, T], F32)
        hT = sb("hT_sb", [128, KC, T], BF16)
        gains = sb("gains_sb", [128, NG, KC], F32)
        ones = sb("ones_sb", [128, 128], BF16)
        scratch = sb("scratch_sb", [128, 8], F32)
        epsb = sb("epsb_sb", [128, 1], F32)
        sq = [sb("sq%d" % i, [128, TS], BF16) for i in range(2)]
        rstd = [sb("rstd%d" % i, [128, TS], F32) for i in range(2)]
        pg = [ps("pg%d" % i) for i in range(2)]
        pu = [ps("pu%d" % i) for i in range(2)]
        py = [ps("py%d" % i) for i in range(2)]
        pn = [ps("pn%d" % i) for i in range(2)]
        cnt = {"n": 0, "w": 0, "g": 0, "y": 0}

        for t in range(NTS):
            P.dma("sp", xT[:, :, t * TS:(t + 1) * TS], xT_d.rearrange("(k p) t -> p k t", p=128)[:, :, t * TS:(t + 1) * TS], ("x", t),
                  w=[("x", c, t) for c in range(KC)])
        P.dma("sp", gains[:], gains_d, "gains", w=["gains"])
        P.op("dve", lambda e: e.memset(ones[:], 1.0 / D), w=["ones"])
        P.op("dve", lambda e: e.memset(epsb[:], EPS), w=["epsb"])

        def rmsnorm(gi, out_fp32=None):
            for t in range(NTS):
                tsl = slice(t * TS, (t + 1) * TS)
                j = cnt["n"] % 2
                cnt["n"] += 1
                for c in range(KC):
                    q = sq[c % 2]
                    P.op("act", lambda e, q=q, c=c, tsl=tsl: e.activation(out=q[:], in_=xT[:, c, tsl], func=AF.Square),
                         r=[("x", c, t)], w=[("sq", c % 2)])
                    P.op("pe", lambda e, q=q, c=c, j=j: e.matmul(pn[j][:], lhsT=ones[:], rhs=q[:], start=(c == 0), stop=(c == KC - 1)),
                         r=[("sq", c % 2), "ones"], w=[("pn", j)])
                P.op("act", lambda e, j=j: e.activation(out=rstd[j][:], in_=pn[j][:], func=AF.Ln, bias=epsb[:], scale=1.0),
                     r=[("pn", j), "epsb"], w=[("rstd", j)])
                P.op("act", lambda e, j=j: e.activation(out=rstd[j][:], in_=rstd[j][:], func=AF.Exp, scale=-0.5),
                     r=[("rstd", j)], w=[("rstd", j)])
                for c in range(KC):
                    if out_fp32 is None:
                        P.op("dve", lambda e, c=c, j=j, tsl=tsl: e.scalar_tensor_tensor(
                            out=hT[:, c, tsl], in0=xT[:, c, tsl], scalar=gains[:, gi, c:c + 1], in1=rstd[j][:], op0=ALU.mult, op1=ALU.mult),
                            r=[("x", c, t), ("rstd", j), "gains"], w=[("h", c, t)])
                    else:
                        P.op("dve", lambda e, c=c, j=j, tsl=tsl: e.scalar_tensor_tensor(
                            out=xT[:, c, tsl], in0=xT[:, c, tsl], scalar=gains[:, gi, c:c + 1], in1=rstd[j][:], op0=ALU.mult, op1=ALU.mult),
                            r=[("x", c, t), ("rstd", j), "gains"], w=[("x", c, t)])

        def ffn(l, which, bufs):
            act, sil, wgu, wdn = bufs
            wg_v = wgu_d[which][l].rearrange("(k p) n -> p k n", p=128)
            wd_d = wdn_d[which][l]
            for fg in range(FGROUPS):
                slot_of = {}

                def load_gu(fi):
                    f = fg * FG + fi
                    s = cnt["w"] % NW
                    cnt["w"] += 1
                    slot_of[fi] = s
                    P.dma("pool", wgu[s][:, :, 0:128], wg_v[:, :, f * 128:(f + 1) * 128], ("wg", s), w=[("wgu", s)])
                    P.dma("pool", wgu[s][:, :, 128:256], wg_v[:, :, D_FF + f * 128:D_FF + (f + 1) * 128], ("wg", s), w=[("wgu", s)])

                for fi in range(NW):
                    load_gu(fi)
                P.dma("pool", wdn[:], wd_d[fg * FG * 128:(fg + 1) * FG * 128, :].rearrange("(f p) d -> p f d", p=128), "wd",
                      w=[("wdn", fi) for fi in range(FG)])
                for fi in range(FG):
                    f = fg * FG + fi
                    if fi + NW - 1 < FG and fi > 0:
                        load_gu(fi + NW - 1)
                    s = slot_of[fi]
                    for t in range(NTS):
                        tsl = slice(t * TS, (t + 1) * TS)
                        j = cnt["g"] % 2
                        cnt["g"] += 1
                        for c in range(KC):
                            P.op("pe", lambda e, c=c, s=s, j=j, tsl=tsl: e.matmul(pg[j][:], lhsT=wgu[s][:, c, 0:128], rhs=hT[:, c, tsl], start=(c == 0), stop=(c == KC - 1)),
                                 r=[("wgu", s), ("h", c, t)], w=[("pg", j)])
                        for c in range(KC):
                            P.op("pe", lambda e, c=c, s=s, j=j, tsl=tsl: e.matmul(pu[j][:], lhsT=wgu[s][:, c, 128:256], rhs=hT[:, c, tsl], start=(c == 0), stop=(c == KC - 1)),
                                 r=[("wgu", s), ("h", c, t)], w=[("pu", j)])
                        P.op("act", lambda e, j=j: e.activation(out=sil[j][:], in_=pg[j][:], func=AF.Silu),
                             r=[("pg", j)], w=[("sil", j)])
                        P.op("dve", lambda e, j=j, fi=fi, tsl=tsl: e.tensor_tensor(out=act[:, fi, tsl], in0=pu[j][:], in1=sil[j][:], op=ALU.mult),
                             r=[("pu", j), ("sil", j)], w=[("act", fi, t)])
                for t in range(NTS):
                    tsl = slice(t * TS, (t + 1) * TS)
                    for dc in range(KC):
                        j = cnt["y"] % 2
                        cnt["y"] += 1
                        for fi in range(FG):
                            P.op("pe", lambda e, fi=fi, dc=dc, j=j, tsl=tsl: e.matmul(py[j][:], lhsT=wdn[:, fi, dc * 128:(dc + 1) * 128], rhs=act[:, fi, tsl], start=(fi == 0), stop=(fi == FG - 1)),
                                 r=[("wdn", fi), ("act", fi, t)], w=[("py", j)])
                        P.op("dve", lambda e, dc=dc, j=j, tsl=tsl: e.scalar_tensor_tensor(
                            out=xT[:, dc, tsl], in0=py[j][:], scalar=0.5, in1=xT[:, dc, tsl], op0=ALU.mult, op1=ALU.add),
                            r=[("py", j), ("x", dc, t)], w=[("x", dc, t)])

        def ffn_phase(l, which, gi):
            with ExitStack() as ph:
                act = sb("act_sb", [128, FG, T], BF16, ph)
                sil = [sb("sil%d" % i, [128, TS], F32, ph) for i in range(2)]
                wgu = [sb("wgu%d" % i, [128, KC, 256], BF16, ph) for i in range(NW)]
                wdn = sb("wdn_sb", [128, FG, D], BF16, ph)
                rmsnorm(gi)
                ffn(l, which, (act, sil, wgu, wdn))
                P.fence(scratch[:, 0:1])

        ident = sb("ident_sb", [128, 128], BF16)
        triu = sb("triu_sb", [128, 128], F32)
        onesf = sb("onesf_sb", [128, 128], F32)
        maskb = sb("mask_sb", [128, 4, TS], BF16)
        corev = sb("corev_sb", [128, 4], F32)
        mvec = sb("mvec_sb", [128, DEPTH, 8], F32)
        foxb = sb("foxb_sb", [128, DEPTH, NH], F32)
        pcnt = sb("pcnt_sb", [128, 2, 16], F32)
        invw = sb("invw_sb", [128, 2], F32)
        oneb = sb("oneb_sb", [128, 1], F32)
        zerob = sb("zerob_sb", [128, 1], F32)
        P.dma("pool", ident[:], ident_d, "c_ident", w=["ident"])
        P.dma("pool", maskb[:], mask_d, "c_mask", w=["maskb"])
        P.dma("sp", triu[:], triu_d, "c_triu", w=["triu"])
        P.dma("sp", corev[:], core_d, "c_core", w=["corev"])
        P.dma("sp", mvec[:], mvec_d, "c_mvec", w=["mvec"])
        P.dma("sp", foxb[:], foxb_d, "c_foxb", w=["foxb"])
        P.dma("sp", pcnt[:], pcnt_d, "c_pcnt", w=["pcnt"])
        P.dma("sp", invw[:], invw_d, "c_invw", w=["invw"])
        P.op("dve", lambda e: e.reciprocal(out=pcnt[:], in_=pcnt[:]), r=["pcnt"], w=["pcnt"])
        P.op("dve", lambda e: e.memset(onesf[:], 1.0), w=["onesf"])
        P.op("dve", lambda e: e.memset(oneb[:], 1.0), w=["oneb"])
        P.op("dve", lambda e: e.memset(zerob[:], 0.0), w=["zerob"])
        banks = pg + pu + py + pn
        bk = {"i": 0}

        def bank():
            i = bk["i"] % 8
            bk["i"] += 1
            return i

        def mm(i, out, lhsT, rhs, start, stop, r):
            P.op("pe", lambda e: e.matmul(out, lhsT=lhsT, rhs=rhs, start=start, stop=stop), r=list(r), w=[("bk", i)])

        def mix_phase(l, gi, parts=("mla", "pool", "fox")):
            rmsnorm(gi)
            tsls = [slice(t * TS, (t + 1) * TS) for t in range(NTS)]
            with ExitStack() as mx:
                u32 = sb("u32_sb", [128, 2, 16 + T], F32, mx)
                csall = sb("cs_sb", [128, NB, NH], F32, mx)
                def pool_mixer(pp):
                    poolw = sb("poolw_sb", [128, 2, 64], BF16, pp)
                    sA = sb("sA_sb", [128, 16 + TS], F32, pp)
                    sB = sb("sB_sb", [128, 16 + TS], F32, pp)
                    pl = [sb("pl%d" % i, [128, TS], BF16, pp) for i in range(2)]
                    fx = sb("fx_sb", [128, 16], F32, pp)
                    P.dma("pool", poolw[:], poolw_d[l].rearrange("(m p) d -> p m d", p=128), "poolw", w=["poolw"])
                    for m in range(2):
                        P.dma("sp", u32[:, m, 0:16], misc_out[l][0:128, 96 + m * 16:96 + (m + 1) * 16], ("halo", m), r=[("cc", "misc")], w=[("halo", m)])
                        P.op("dve", lambda e, m=m: e.tensor_scalar(out=u32[:, m, 0:16], in0=u32[:, m, 0:16], scalar1=corev[:, 1:2], scalar2=None, op0=ALU.mult),
                             r=[("halo", m), "corev"], w=[("halo", m)])
                    pending = []

                    def pm_tile(t):
                        for m in range(2):
                            a = u32[:, m, t * TS:t * TS + 16 + TS]
                            W = 16 + TS
                            rd = [("u32", m, t), ("halo", m)] + ([("u32", m, t - 1)] if t else [])
                            P.op("dve", lambda e, a=a: e.tensor_tensor(out=sA[:, 1:W], in0=a[:, 1:W], in1=a[:, 0:W - 1], op=ALU.add), r=rd, w=["sA"])
                            plm = pl[m]
                            pk = ("pl", m)
                            def fin(src, rows, skey, a=a, m=m, t=t, plm=plm, pk=pk):
                                P.op("dve", lambda e: e.scalar_tensor_tensor(out=plm[rows, :], in0=src[rows, 16:W], scalar=invw[rows, m:m + 1], in1=a[rows, 16:W],
                                                                           op0=ALU.mult, op1=ALU.subtract), r=[skey, "invw"], w=[pk])
                                if t == 0:
                                    P.op("dve", lambda e: e.tensor_tensor(out=fx[rows, :], in0=src[rows, 16:32], in1=pcnt[rows, m, :], op=ALU.mult),
                                         r=[skey, "pcnt"], w=["fx"])
                                    P.op("dve", lambda e: e.tensor_tensor(out=plm[rows, 0:16], in0=fx[rows, :], in1=a[rows, 16:32], op=ALU.subtract),
                                         r=["fx"], w=[pk])
                            if m == 0:
                                fin(sA, slice(0, 64), "sA")
                            P.op("dve", lambda e: e.tensor_tensor(out=sB[:, 3:W], in0=sA[:, 3:W], in1=sA[:, 1:W - 2], op=ALU.add), r=["sA"], w=["sB"])
                            if m == 0:
                                fin(sB, slice(64, 128), "sB")
                            else:
                                P.op("dve", lambda e: e.tensor_tensor(out=sA[:, 7:W], in0=sB[:, 7:W], in1=sB[:, 3:W - 4], op=ALU.add), r=["sB"], w=["sA"])
                                fin(sA, slice(0, 64), "sA")
                                P.op("dve", lambda e: e.tensor_tensor(out=sB[:, 15:W], in0=sA[:, 15:W], in1=sA[:, 7:W - 8], op=ALU.add), r=["sA"], w=["sB"])
                                fin(sB, slice(64, 128), "sB")
                            def tail(m=m, t=t, plm=plm, pk=pk):
                                for gg in range(2):
                                    g = m * 2 + gg
                                    rows = slice(gg * 64, gg * 64 + 64)
                                    i = bank()
                                    mm(i, banks[i][0:64, :], poolw[rows, m, :], plm[rows, :], True, True, ["poolw", pk])
                                    P.op("dve", lambda e, i=i, g=g, rows=rows, m=m, t=t: e.tensor_scalar(out=hT[rows, 3 + m, tsls[t]], in0=banks[i][0:64, :], scalar1=mvec[0:64, l, 3 + g:4 + g],
                                                                                                  scalar2=None, op0=ALU.mult),
                                         r=[("bk", i), "mvec"], w=[("mix", 3 + m, t, gg * 64), ("h", 3 + m, t)])
                            if pending:
                                pending.pop(0)()
                            pending.append(tail)

                    def pm_flush():
                        while pending:
                            pending.pop(0)()
                    return pm_tile, pm_flush

                with ExitStack() as pa:
                    win = sb("win_sb", [128, KC, N_IN], BF16, pa)
                    wkrot = sb("wkrot_sb", [128, KC, 32], BF16, pa)
                    wqb = sb("wqb_sb", [128, 2, 576], BF16, pa)
                    wqbrot = sb("wqbrot_sb", [128, 2, 192], BF16, pa)
                    wkvb = sb("wkvb_sb", [128, 768], BF16, pa)
                    ropet = [sb("rope%d" % i, [128, 2, TS], F32, pa) for i in range(2)]
                    qn = [sb("qn%d" % i, [128, 2, TS], BF16, pa) for i in range(2)]
                    kvn = [sb("kvn%d" % i, [128, TS], BF16, pa) for i in range(2)]
                    sqm = [sb("sqm%d" % i, [128, TS], BF16, pa) for i in range(2)]
                    rsm = [sb("rsm%d" % i, [128, TS], F32, pa) for i in range(2)]
                    r1 = sb("r1_sb", [128, TS], F32, pa)
                    r2 = sb("r2_sb", [128, TS], F32, pa)
                    NQS, NFS = 4, 2
                    qst = [sb("qst%d" % i, [96, TS], BF16, pa) for i in range(NQS)]
                    kst = [sb("kst%d" % i, [96, TS], BF16, pa) for i in range(NQS)]
                    fqst = [sb("fqst%d" % i, [128, TS], BF16, pa) for i in range(NFS)]
                    fkst = [sb("fkst%d" % i, [128, TS], BF16, pa) for i in range(NFS)]
                    drow = [sb("drow%d" % i, [8, TS], BF16, pa) for i in range(2)]
                    vst = [sb("vst%d" % i, [128, NH, 64], BF16, pa) for i in range(4)]
                    fb = sb("fb_sb", [128, NH], F32, pa)
                    spb = [sb("sp%d" % i, [128, NH], F32, pa) for i in range(2)]
                    spacc = sb("spacc_sb", [128, NH], F32, pa)
                    misc = sb("misc_sb", [128, 128], F32, pa)
                    P.dma("pool", win[:], w_in_d[l].rearrange("(k p) n -> p k n", p=128), "win", w=[("win", c) for c in range(KC)])
                    P.dma("pool", wkrot[:], w_krot_d[l].rearrange("(k p) n -> p k n", p=128), "wkrot", w=["wkrot"])
                    P.dma("pool", wqb[:], w_qb_d[l].rearrange("(k p) n -> p k n", p=128), "wqb", w=["wqb"])
                    P.dma("pool", wqbrot[:], w_qbrot_d[l].rearrange("(k p) n -> p k n", p=128), "wqbrot", w=["wqbrot"])
                    P.dma("pool", wkvb[:], w_kvb_d[l], "wkvb", w=["wkvb"])
                    P.op("pool", lambda e: e.memset(spacc[:], 0.0), w=["spacc"])
                    WIN = [("win", c) for c in range(KC)]
                    cn = {"q": 0, "k": 0, "fq": 0, "fk": 0, "v": 0, "sp": 0}

                    def proj(i, out, col0, ncol, t, wt=None):
                        for c in range(KC):
                            lh = (win if wt is None else wt)[:, c, col0:col0 + ncol]
                            mm(i, out, lh, hT[:, c, tsls[t]], c == 0, c == KC - 1, WIN + ["wkrot", ("h", c, t)])

                    def subnorm(t, srcs, mean_scale, gcols, dst, dkeys):
                        j = t % 2
                        ib = bank()
                        for m, (i, ap) in enumerate(srcs):
                            P.op("act", lambda e, ap=ap, m=m: e.activation(out=sqm[m % 2][:], in_=ap, func=AF.Square),
                                 r=[("bk", i)], w=[("sqm", m % 2)])
                            mm(ib, banks[ib][:], ones[:], sqm[m % 2][:], m == 0, m == len(srcs) - 1, [("sqm", m % 2), "ones"])
                        P.op("act", lambda e: e.activation(out=rsm[j][:], in_=banks[ib][:], func=AF.Ln, bias=epsb[:], scale=mean_scale),
                             r=[("bk", ib), "epsb"], w=[("rsm", j)])
                        P.op("act", lambda e: e.activation(out=rsm[j][:], in_=rsm[j][:], func=AF.Exp, scale=-0.5),
                             r=[("rsm", j)], w=[("rsm", j)])
                        for m, (i, ap) in enumerate(srcs):
                            P.op("dve", lambda e, ap=ap, m=m: e.scalar_tensor_tensor(out=dst[m], in0=ap, scalar=mvec[:, l, gcols[m]:gcols[m] + 1],
                                                                                  in1=rsm[j][:], op0=ALU.mult, op1=ALU.mult),
                                 r=[("bk", i), ("rsm", j), "mvec"], w=[dkeys[m]])

                    def rope_rows(ia, ib_, rp, out_list, okeys, t1, t2):
                        P.op("dve", lambda e: e.tensor_tensor(out=t1[64:96, :], in0=banks[ia][64:96, :], in1=rp[64:96, 0, :], op=ALU.mult),
                             r=[("bk", ia), "ropet"], w=[("t1", id(t1))])
                        P.op("dve", lambda e: e.tensor_tensor(out=t2[64:96, :], in0=banks[ib_][64:96, :], in1=rp[64:96, 1, :], op=ALU.mult),
                             r=[("bk", ib_), "ropet"], w=[("t2", id(t2))])
                        for o, ok in zip(out_list, okeys):
                            P.op("dve", lambda e, o=o: e.tensor_tensor(out=o, in0=t1[64:96, :], in1=t2[64:96, :], op=ALU.add),
                                 r=[("t1", id(t1)), ("t2", id(t2))], w=[ok])

                    m8all = sb("m8all_sb", [128, NB, NH], BF16, pa)

                    def evac_v(i, dst_d, blk):
                        vt = vst[cn["v"] % 4]
                        vk = ("vst", cn["v"] % 4)
                        cn["v"] += 1
                        P.op("act", lambda e: e.activation(out=vt[:].rearrange("p h d -> p (h d)"), in_=banks[i][:, 0:NH * 64], func=AF.Copy),
                             r=[("bk", i)], w=[vk])
                        name = "foxV_in" if dst_d is foxV_in else "mlaV_in"
                        P.dma("pool", dst_d[l].rearrange("(h t) d -> t h d", h=NH)[blk * 128:(blk + 1) * 128], vt[:], vk, r=[vk], w=[(name, blk)])

                    def tile_kv(t):
                        tsl = tsls[t]
                        rp = ropet[t % 2]
                        P.dma("sp", rp[:], rope_d[:, :, tsl], ("rope", t % 2), w=["ropet"])
                        ic = bank()
                        proj(ic, banks[ic][:], O_KVA, 128, t)
                        kj = kvn[t % 2]
                        subnorm(t, [(ic, banks[ic][:])], 8.0, [2], [kj[:]], [("kvn", t % 2)])
                        for bi in range(4):
                            blk = t * 4 + bi
                            bsl = slice(blk * 128, (blk + 1) * 128)
                            i = bank()
                            for c in range(KC):
                                mm(i, banks[i][:, 0:NH], hT[:, c, bsl], win[:, c, O_FF:O_FF + NH], c == 0, c == KC - 1, WIN + [("h", c, t)])
                            sp_ = spb[cn["sp"] % 2]
                            spk = ("sp", cn["sp"] % 2)
                            cn["sp"] += 1
                            P.op("dve", lambda e, i=i: e.tensor_tensor(out=fb[:], in0=banks[i][:, 0:NH], in1=foxb[:, l, :], op=ALU.add),
                                 r=[("bk", i), "foxb"], w=["fb"])
                            P.op("act", lambda e: e.activation(out=fb[:], in_=fb[:], func=AF.Exp, scale=-1.0), r=["fb"], w=["fb"])
                            P.op("act", lambda e, sp_=sp_: e.activation(out=sp_[:], in_=fb[:], func=AF.Ln, bias=oneb[:], scale=1.0),
                                 r=["fb", "oneb"], w=[spk])
                            i = bank()
                            for c in range(KC):
                                mm(i, banks[i][:, 0:NH * 64], hT[:, c, bsl], win[:, c, O_FV:O_FV + NH * 64], c == 0, c == KC - 1, WIN + [("h", c, t)])
                            evac_v(i, foxV_in, blk)
                            i2 = bank()
                            mm(i2, banks[i2][:, 0:NH], triu[:], sp_[:], True, False, ["triu", spk])
                            mm(i2, banks[i2][:, 0:NH], onesf[:], spacc[:], False, True, ["onesf", "spacc"])
                            P.op("act", lambda e, i2=i2, blk=blk: e.activation(out=csall[:, blk, :], in_=banks[i2][:, 0:NH], func=AF.Copy),
                                 r=[("bk", i2)], w=[("cs", blk)])
                            P.op("pool", lambda e, sp_=sp_: e.tensor_tensor(out=spacc[:], in0=spacc[:], in1=sp_[:], op=ALU.add),
                                 r=[spk, "spacc"], w=["spacc"])
                            P.op("dve", lambda e, blk=blk: e.tensor_scalar(out=m8all[:, blk, :], in0=csall[:, blk, :], scalar1=-8.0, scalar2=None, op0=ALU.mult),
                                 r=[("cs", blk)], w=[("m8", blk)])
                        for bi in range(4):
                            i = bank()
                            for h in range(NH):
                                mm(i, banks[i][:, h * 64:(h + 1) * 64], kj[:, bi * 128:(bi + 1) * 128], wkvb[:, h * 128 + 64:h * 128 + 128], True, True, [("kvn", t % 2), "wkvb"])
                            evac_v(i, mlaV_in, t * 4 + bi)
                        ika, ikb = bank(), bank()
                        proj(ika, banks[ika][64:96, :], O_KR, 32, t)
                        proj(ikb, banks[ikb][64:96, :], 0, 32, t, wt=wkrot)
                        K1, K2 = ("t1", id(r1)), ("t2", id(r2))
                        P.op("dve", lambda e: e.tensor_tensor(out=r1[64:96, :], in0=banks[ika][64:96, :], in1=rp[64:96, 0, :], op=ALU.mult),
                             r=[("bk", ika), "ropet"], w=[K1])
                        P.op("dve", lambda e: e.tensor_tensor(out=r2[64:96, :], in0=banks[ikb][64:96, :], in1=rp[64:96, 1, :], op=ALU.mult),
                             r=[("bk", ikb), "ropet"], w=[K2])
                        for h in range(NH):
                            kt = kst[cn["k"] % NQS]
                            kk = ("kst", cn["k"] % NQS)
                            cn["k"] += 1
                            i = bank()
                            mm(i, banks[i][0:64, :], wkvb[:, h * 128:h * 128 + 64], kj[:], True, True, [("kvn", t % 2), "wkvb"])
                            P.op("act", lambda e, i=i, kt=kt: e.activation(out=kt[0:64, :], in_=banks[i][0:64, :], func=AF.Copy), r=[("bk", i)], w=[kk])
                            P.op("dve", lambda e, kt=kt: e.tensor_tensor(out=kt[64:96, :], in0=r1[64:96, :], in1=r2[64:96, :], op=ALU.add), r=[K1, K2], w=[kk])
                            P.dma(("sp" if h % 2 == 0 else "pool"), mlaK_in[l][h // 3][(h % 3) * 96:(h % 3 + 1) * 96, tsl], kt[:], kk, r=[kk], w=[("mlaK_in", h, t)])
                            if h % 2 == 1:
                                hp = h // 2
                                ft = fkst[cn["fk"] % NFS]
                                fk = ("fkst", cn["fk"] % NFS)
                                cn["fk"] += 1
                                i = bank()
                                proj(i, banks[i][:], O_FK + hp * 128, 128, t)
                                P.op("act", lambda e, i=i, ft=ft: e.activation(out=ft[:], in_=banks[i][:], func=AF.Copy), r=[("bk", i)], w=[fk])
                                P.dma(("sp" if hp % 2 == 0 else "pool"), foxK_in[l][hp * 128:(hp + 1) * 128, tsl], ft[:], fk, r=[fk],
                                      w=[("foxK_in", h - 1, t), ("foxK_in", h, t)])
                        for m in range(2):
                            i = bank()
                            proj(i, banks[i][:], O_POOL + m * 128, 128, t)
                            P.op("act", lambda e, i=i, m=m: e.activation(out=u32[:, m, 16 + t * TS:16 + (t + 1) * TS], in_=banks[i][:], func=AF.Copy),
                                 r=[("bk", i)], w=[("u32", m, t)])

                    def qa_norm(t):
                        ia, ib2 = bank(), bank()
                        proj(ia, banks[ia][:], O_QA, 128, t)
                        proj(ib2, banks[ib2][:], O_QA + 128, 128, t)
                        qj = qn[t % 2]
                        subnorm(t, [(ia, banks[ia][:]), (ib2, banks[ib2][:])], 4.0, [0, 1], [qj[:, 0, :], qj[:, 1, :]], [("qn", t % 2, 0), ("qn", t % 2, 1)])

                    def tile_q_mla(t):
                        tsl = tsls[t]
                        rp = ropet[t % 2]
                        P.dma("sp", rp[:], rope_d[:, :, tsl], ("rope", t % 2), w=["ropet"])
                        if t + 1 < NTS:
                            qa_norm(t + 1)
                        qj = qn[t % 2]
                        QN = [("qn", t % 2, 0), ("qn", t % 2, 1)]
                        for h in range(NH):
                            qt_ = qst[cn["q"] % NQS]
                            qk = ("qst", cn["q"] % NQS)
                            cn["q"] += 1
                            i, ir = bank(), bank()
                            for m in range(2):
                                mm(i, banks[i][0:96, :], wqb[:, m, h * 96:(h + 1) * 96], qj[:, m, :], m == 0, m == 1, QN + ["wqb"])
                            for m in range(2):
                                mm(ir, banks[ir][64:96, :], wqbrot[:, m, h * 32:(h + 1) * 32], qj[:, m, :], m == 0, m == 1, QN + ["wqbrot"])
                            P.op("act", lambda e, i=i, qt_=qt_: e.activation(out=qt_[0:64, :], in_=banks[i][0:64, :], func=AF.Copy), r=[("bk", i)], w=[qk])
                            rope_rows(i, ir, rp, [qt_[64:96, :]], [qk], r1, r2)
                            P.dma(("sp" if h % 2 == 0 else "pool"), mlaQ_d[l][h // 3][(h % 3) * 96:(h % 3 + 1) * 96, tsl], qt_[:], qk, r=[qk], w=[("mlaQ", h, t)])

                    def tile_q_fox(t):
                        tsl = tsls[t]
                        i2 = bank()
                        for bi in range(4):
                            blk = t * 4 + bi
                            mm(i2, banks[i2][0:NH, bi * 128:(bi + 1) * 128], m8all[:, blk, :], ident[:], True, True, [("m8", blk), "ident"])
                        dr = drow[t % 2]
                        dk = ("drow", t % 2)
                        P.op("act", lambda e: e.activation(out=dr[0:NH, :], in_=banks[i2][0:NH, :], func=AF.Copy), r=[("bk", i2)], w=[dk])
                        P.dma("pool", foxQ_d[l].rearrange("(h r) t -> r h t", r=65)[64, :, tsl], dr[0:NH, :], dk, r=[dk], w=[("foxQd", t)])
                        for hp in range(NH // 2):
                            fq = fqst[cn["fq"] % NFS]
                            fqk = ("fqst", cn["fq"] % NFS)
                            cn["fq"] += 1
                            i = bank()
                            proj(i, banks[i][:], O_FQ + hp * 128, 128, t)
                            P.op("act", lambda e, i=i, fq=fq: e.activation(out=fq[:], in_=banks[i][:], func=AF.Copy), r=[("bk", i)], w=[fqk])
                            for k_ in range(2):
                                h = 2 * hp + k_
                                P.dma(("sp" if k_ == 0 else "pool"), foxQ_d[l][h * 65:h * 65 + 64, tsl], fq[k_ * 64:(k_ + 1) * 64, :], fqk, r=[fqk], w=[("foxQ", h, t)])

                    for t in range(NTS):
                        tile_kv(t)
                    it = bank()
                    mm(it, banks[it][:, 0:NH], onesf[:], spacc[:], True, True, ["onesf", "spacc"])
                    for blk in range(NB):
                        P.op("dve", lambda e, blk=blk: e.tensor_tensor(out=misc[:, blk * NH:(blk + 1) * NH], in0=csall[:, blk, :], in1=banks[it][:, 0:NH], op=ALU.subtract),
                             r=[("cs", blk), ("bk", it)], w=["misc"])
                    for m in range(2):
                        P.op("dve", lambda e, m=m: e.tensor_copy(out=misc[:, 96 + m * 16:96 + (m + 1) * 16], in_=u32[:, m, T:T + 16]),
                             r=[("u32", m, NTS - 1)], w=["misc"])
                    P.dma("sp", misc_in[l], misc[:], "misc", r=["misc"], w=["misc_in"])
                    def gather(name, src, dst, rkeys):
                        P.cc(lambda g: g.collective_compute("AllGather", ALU.bypass, replica_groups=PAIRS, ins=[src], outs=[dst]),
                             ("cc", name, l), r=rkeys, w=[("cc", name)])
                    HT = [(h, t) for h in range(NH) for t in range(NTS)]
                    gather("misc", misc_in[l], misc_out[l], ["misc_in"])
                    gather("mlaK0", mlaK_in[l][0], mlaK_out[l][0], [("mlaK_in", h, t) for h, t in HT if h < 3])
                    gather("mlaK1", mlaK_in[l][1], mlaK_out[l][1], [("mlaK_in", h, t) for h, t in HT if h >= 3])
                    gather("mlaV", mlaV_in[l], mlaV_out[l], [("mlaV_in", b_) for b_ in range(NB)])
                    gather("foxK", foxK_in[l], foxK_out[l], [("foxK_in", h, t) for h, t in HT])
                    gather("foxV", foxV_in[l], foxV_out[l], [("foxV_in", b_) for b_ in range(NB)])
                    pm_tile, pm_flush = pool_mixer(pa) if "pool" in parts else ((lambda t: None), (lambda: None))
                    qa_norm(0)
                    for t in range(NTS):
                        tile_q_mla(t)
                    gather("mlaQ0", mlaQ_d[l][0], mlaQ_out[l][0], [("mlaQ", h, t) for h, t in HT if h < 3])
                    gather("mlaQ1", mlaQ_d[l][1], mlaQ_out[l][1], [("mlaQ", h, t) for h, t in HT if h >= 3])
                    for t in range(NTS):
                        tile_q_fox(t)
                        pm_tile(t)
                    pm_flush()
                    gather("foxQ", foxQ_d[l], foxQ_out[l], [("foxQ", h, t) for h, t in HT] + [("foxQd", t) for t in range(NTS)])
                    P.fence(scratch[:, 1:2])
                with ExitStack() as pb:
                    Kh = [sb("Kh%d" % i, [96, T], BF16, pb) for i in range(2)]
                    Vh = [sb("Vh%d" % i, [128, NB, 128], BF16, pb) for i in range(2)]
                    Qh = [sb("Qh%d" % i, [96, T], BF16, pb) for i in range(2)]
                    Kc = [sb("Kc%d" % i, [96, T], BF16, pb) for i in range(2)]
                    Vc = [sb("Vc%d" % i, [128, NB, 64], BF16, pb) for i in range(2)]
                    Qc = [sb("Qc%d" % i, [96, T], BF16, pb) for i in range(2)]
                    pt = [sb("pt%d" % i, [128, TS], BF16, pb) for i in range(3)]
                    rct = sb("rct_sb", [128, TS], F32, pb)
                    comb = sb("comb_sb", [128, TS], F32, pb)
                    part = [sb("part%d" % i, [128, TS], F32, pb) for i in range(2)]
                    rst = [sb("rst%d" % i, [128, TS], F32, pb) for i in range(2)]
                    negraw = sb("negraw_sb", [128, NB, NH], F32, pb)
                    negsel = sb("negsel_sb", [128, NB, 3], F32, pb)
                    fA, fB = corev[:, 2:3], corev[:, 1:2]
                    for i in range(2):
                        P.op("pool", lambda e, i=i: e.memset(Vh[i][:, :, 64:128], 1.0), w=[("Vh", i)])
                        P.op("pool", lambda e, i=i: e.memset(Kh[i][64:65, :], 1.0), w=[("Kh", i)])
                    P.dma("sp", negraw[:].rearrange("p b h -> p (b h)"), misc_out[l][0:128, 0:NB * NH], "negraw", r=[("cc", "misc")], w=["negraw"])
                    P.op("dve", lambda e: e.tensor_scalar(out=negsel[:], in0=negraw[:, :, 0:3], scalar1=fA, scalar2=None, op0=ALU.mult), r=["negraw", "corev"], w=["negsel"])
                    P.op("dve", lambda e: e.scalar_tensor_tensor(out=negsel[:], in0=negraw[:, :, 3:6], scalar=fB, in1=negsel[:], op0=ALU.mult, op1=ALU.add),
                         r=["negraw", "corev", "negsel"], w=["negsel"])
                    ptn = {"i": 0}
                    pon = {"i": 0}
                    last_fam = {0: None, 1: None}

                    def blend(dst, ca, cb, srca, srcb, skey, rkeys, dkey, np_):
                        fa, fb = corev[0:np_, 2:3], corev[0:np_, 1:2]
                        P.dma("sp", ca, srca, (skey, "a"), r=rkeys, w=[(skey, "a")])
                        P.dma("sp", cb, srcb, (skey, "b"), r=rkeys, w=[(skey, "b")])
                        P.op("act", lambda e: e.activation(out=dst, in_=ca, func=AF.Copy, scale=fa), r=[(skey, "a"), "corev"], w=[dkey])
                        P.op("dve", lambda e: e.scalar_tensor_tensor(out=dst, in0=cb, scalar=fb, in1=dst, op0=ALU.mult, op1=ALU.add), r=[(skey, "b"), "corev", dkey], w=[dkey])

                    def fix_k_row(sl, fam):
                        if fam == "fox" and last_fam[sl] == "mla":
                            P.op("pool", lambda e: e.memset(Kh[sl][64:65, :], 1.0), r=[("Kh", sl)], w=[("Kh", sl)])
                        last_fam[sl] = fam

                    gp = []

                    def gp_drain(keep):
                        while gp:
                            if gp[0][0] == "post" or sum(1 for tag, _ in gp if tag == "pv") > keep:
                                gp.pop(0)[1]()
                            else:
                                break

                    def attend(sl, fam, kdq, qt, blocks, bias_of, ipo):
                        tsl = tsls[qt]
                        sc = MLA_SCALE if fam == "mla" else 0.125

                        def score(kb, diag):
                            i = bk["i"] % 6
                            bk["i"] += 1
                            q0 = max(diag, 0) * 128
                            ksl = slice(kb * 128, (kb + 1) * 128)
                            qsl = slice(tsl.start + q0, tsl.stop)
                            P.op("pe", lambda e: e.matmul(banks[i][:, q0:], lhsT=Kh[sl][0:kdq, ksl], rhs=Qh[sl][0:kdq, qsl], start=True, stop=(diag < 0)),
                                 r=[("Kh", sl), ("Qh", sl)], w=[("bk", i)])
                            if diag >= 0:
                                P.op("pe", lambda e: e.matmul(banks[i][:, q0:], lhsT=ident[:], rhs=maskb[:, diag, q0:], start=False, stop=True),
                                     r=["ident", "maskb"], w=[("bk", i)])
                            p_ = ptn["i"] % 3
                            ptn["i"] += 1
                            bias, bkeys = bias_of(kb)
                            P.op("act", lambda e: e.activation(out=pt[p_][:, q0:], in_=banks[i][:, q0:], func=AF.Exp, bias=bias, scale=sc),
                                 r=[("bk", i)] + bkeys, w=[("pt", p_)])
                            return kb, p_, q0

                        def pv(kb, p_, q0, first, last):
                            P.op("pe", lambda e: e.matmul(banks[ipo][:, q0:], lhsT=Vh[sl][:, kb, :], rhs=pt[p_][:, q0:], start=first, stop=last),
                                 r=[("Vh", sl), ("pt", p_)], w=[("bk", ipo)])

                        for n_, (kb, diag) in enumerate(blocks):
                            k0, p0, c0 = score(kb, diag)
                            gp.append(("pv", lambda k0=k0, p0=p0, c0=c0, f_=(n_ == 0), l_=(n_ == len(blocks) - 1): pv(k0, p0, c0, f_, l_)))
                            gp_drain(2)

                    items = []
                    rect = ([("mla", j) for j in range(3)] if "mla" in parts else []) + ([("fox", j) for j in range(3)] if "fox" in parts else [])
                    heads = ([("mla", h) for h in range(NH)] if "mla" in parts else []) + ([("fox", h) for h in range(NH)] if "fox" in parts else [])
                    pn_ = {"i": 0}

                    def make_rect(fam, j, sl):
                        jj = j if fam == "mla" else 3 + j
                        kd, kdq = (96, 96) if fam == "mla" else (64, 65)
                        if fam == "mla":
                            Ka, Kb_ = mlaK_out[l][0][j * 96:(j + 1) * 96, :], mlaK_out[l][1][j * 96:(j + 1) * 96, :]
                            Qa, Qb_ = mlaQ_out[l][0][288 + j * 96:288 + (j + 1) * 96, :], mlaQ_out[l][1][288 + j * 96:288 + (j + 1) * 96, :]
                            Vv = mlaV_out[l].rearrange("(h b p) d -> p h b d", h=2 * NH, b=NB)
                            rk, rq, rv = [("cc", "mlaK0"), ("cc", "mlaK1")], [("cc", "mlaQ0"), ("cc", "mlaQ1")], [("cc", "mlaV")]
                        else:
                            Ka, Kb_ = foxK_out[l][j * 64:(j + 1) * 64, :], foxK_out[l][(j + 3) * 64:(j + 4) * 64, :]
                            Qa, Qb_ = foxQ_out[l][390 + j * 65:390 + (j + 1) * 65, :], foxQ_out[l][390 + (j + 3) * 65:390 + (j + 4) * 65, :]
                            Vv = foxV_out[l].rearrange("(h b p) d -> p h b d", h=2 * NH, b=NB)
                            rk, rq, rv = [("cc", "foxK")], [("cc", "foxQ")], [("cc", "foxV")]
                        specs = [(Kh[sl][0:kd, :], Kc[0][0:kd, :], Kc[1][0:kd, :], Ka, Kb_, "Kc", rk, ("Kh", sl), kd),
                                 (Vh[sl][:, :, 0:64], Vc[0][:], Vc[1][:], Vv[:, j], Vv[:, j + 3], "Vc", rv, ("Vh", sl), 128),
                                 (Qh[sl][0:kdq, :], Qc[0][0:kdq, :], Qc[1][0:kdq, :], Qa, Qb_, "Qc", rq, ("Qh", sl), kdq)]

                        def loads():
                            for dst, ca, cb, srca, srcb, skey, rkeys, dkey, np_ in specs:
                                P.dma("sp", ca, srca, (skey, "a"), r=rkeys, w=[(skey, "a")])
                                P.dma("sp", cb, srcb, (skey, "b"), r=rkeys, w=[(skey, "b")])

                        def prep():
                            for n_, (dst, ca, cb, srca, srcb, skey, rkeys, dkey, np_) in enumerate(specs):
                                fa, fb = corev[0:np_, 2:3], corev[0:np_, 1:2]
                                P.op("dve", lambda e, dst=dst, ca=ca, fa=fa: e.tensor_scalar(out=dst, in0=ca, scalar1=fa, scalar2=None, op0=ALU.mult), r=[(skey, "a"), "corev"], w=[dkey])
                                P.op("dve", lambda e, dst=dst, cb=cb, fb=fb: e.scalar_tensor_tensor(out=dst, in0=cb, scalar=fb, in1=dst, op0=ALU.mult, op1=ALU.add),
                                     r=[(skey, "b"), "corev", dkey], w=[dkey])
                                if n_ == 0:
                                    fix_k_row(sl, fam)

                        def run(mid):
                            for qt in range(NTS):
                                ipo = 6 + pon["i"] % 2
                                pon["i"] += 1
                                if fam == "mla":
                                    bias_of = lambda kb: (zerob[:], ["zerob"])
                                else:
                                    bias_of = lambda kb: (negsel[:, kb, j:j + 1], ["negsel"])
                                attend(sl, fam, kdq, qt, [(kb, -1) for kb in range(NB)], bias_of, ipo)
                                r_ = rst[pon["i"] % 2]
                                rkey = ("rst", pon["i"] % 2)

                                def post(ipo=ipo, r_=r_, rkey=rkey, qt=qt):
                                    P.op("dve", lambda e: e.tensor_copy(out=r_[:], in_=banks[ipo][:]), r=[("bk", ipo)], w=[rkey])
                                    P.dma("pool", rout_in[l][jj // 2][(jj % 2) * 128:(jj % 2 + 1) * 128, tsls[qt]], r_[:], rkey, r=[rkey], w=[("rout_in", jj, qt)])
                                gp.append(("post", post))
                                if qt == 1:
                                    mid()
                            gp_drain(0)
                            if jj % 2 == 1:
                                g_ = jj // 2
                                P.cc(lambda g: g.collective_compute("AllGather", ALU.bypass, replica_groups=PAIRS, ins=[rout_in[l][g_]], outs=[rout_out[l][g_]]),
                                     ("cc", "rout%d" % g_, l), r=[("rout_in", x, qt) for x in (jj - 1, jj) for qt in range(NTS)], w=[("cc", "rout", g_)])
                        return loads, prep, run

                    def make_tri(fam, h, sl):
                        kd, kdq = (96, 96) if fam == "mla" else (64, 65)
                        jj = (h % 3) if fam == "mla" else 3 + (h % 3)
                        slot = h // 3
                        if fam == "mla":
                            Ko, Qo, Vo = mlaK_in[l][h // 3][(h % 3) * 96:(h % 3 + 1) * 96, :], mlaQ_d[l][h // 3][(h % 3) * 96:(h % 3 + 1) * 96, :], mlaV_in[l]
                            kn, vn, qn_ = "mlaK_in", "mlaV_in", "mlaQ"
                        else:
                            Ko, Qo, Vo = foxK_in[l][h * 64:(h + 1) * 64, :], foxQ_d[l][h * 65:(h + 1) * 65, :], foxV_in[l]
                            kn, vn, qn_ = "foxK_in", "foxV_in", "foxQ"
                        chunk = (h // 2) if fam == "mla" else (5 + h // 2)
                        prow = (h % 2) * 64

                        def loads():
                            P.dma("sp", Kh[sl][0:kd, :], Ko, ("Kho", sl), r=[(kn, h, t) for t in range(NTS)], w=[("Kh", sl)])
                            fix_k_row(sl, fam)
                            P.dma("sp", Vh[sl][:, :, 0:64], Vo.rearrange("(h b p) d -> p h b d", h=NH, b=NB)[:, h], ("Vho", sl), r=[(vn, b_) for b_ in range(NB)], w=[("Vh", sl)])
                            P.dma("sp", Qh[sl][0:kdq, :], Qo, ("Qho", sl), r=[(qn_, h, t) for t in range(NTS)] + ([("foxQd", t) for t in range(NTS)] if fam == "fox" else []), w=[("Qh", sl)])

                        def prep():
                            pass

                        def run(mid):
                            for qt in range(NTS):
                                ipo = 6 + pon["i"] % 2
                                pon["i"] += 1
                                pa_ = part[pn_["i"] % 2]
                                pkey = ("part", pn_["i"] % 2)
                                pn_["i"] += 1
                                P.dma("sp", pa_[:], rout_out[l][jj // 2][slot * 256 + (jj % 2) * 128:slot * 256 + (jj % 2 + 1) * 128, tsls[qt]], pkey,
                                      r=[("cc", "rout", jj // 2)], w=[pkey])
                                if fam == "mla":
                                    bias_of = lambda kb: (zerob[:], ["zerob"])
                                else:
                                    bias_of = lambda kb: (csall[:, kb, h:h + 1], [("cs", kb)])
                                attend(sl, fam, kdq, qt, [(kb, kb - 4 * qt) for kb in range(4 * (qt + 1))], bias_of, ipo)

                                def post(ipo=ipo, pa_=pa_, pkey=pkey, qt=qt, tsl=tsls[qt]):
                                    P.op("dve", lambda e: e.scalar_tensor_tensor(out=comb[:], in0=pa_[:], scalar=fB, in1=banks[ipo][:], op0=ALU.mult, op1=ALU.add),
                                         r=[("bk", ipo), pkey, "corev"], w=["comb"])
                                    P.op("dve", lambda e: e.reciprocal(out=rct[0:64, :], in_=comb[64:128, :]), r=["comb"], w=["rct"])
                                    P.op("dve", lambda e: e.tensor_tensor(out=hT[prow:prow + 64, chunk, tsl], in0=comb[0:64, :], in1=rct[0:64, :], op=ALU.mult),
                                         r=["comb", "rct"], w=[("mix", chunk, qt, prow)])
                                gp.append(("post", post))
                                if qt == 1:
                                    mid()
                            gp_drain(0)
                        return loads, prep, run

                    for n_, (fam, j) in enumerate(rect):
                        items.append(make_rect(fam, j, n_ % 2))
                    for n_, (fam, h) in enumerate(heads):
                        items.append(make_tri(fam, h, (len(rect) + n_) % 2))
                    if items:
                        items[0][0]()
                        items[0][1]()
                    for n_, (loads, prep, run) in enumerate(items):
                        nxt = items[n_ + 1] if n_ + 1 < len(items) else None
                        if nxt:
                            nxt[0]()
                        run(nxt[1] if nxt else (lambda: None))
                    P.fence(scratch[:, 2:3])
                with ExitStack() as pc:
                    wout = sb("wout_sb", [128, KC, D], BF16, pc)
                    P.dma("pool", wout[:], w_out_d[l].rearrange("(k p) n -> p k n", p=128), "wout", w=[("wout", c) for c in range(KC)])
                    zero_chunks = ([] if "mla" in parts else [0, 1, 2]) + ([] if "pool" in parts else [3, 4]) + ([] if "fox" in parts else [5, 6, 7])
                    for c in zero_chunks:
                        P.op("pool", lambda e, c=c: e.memset(hT[:, c, :], 0.0), w=[("mix", c, t, pr) for t in range(NTS) for pr in (0, 64)])
                    for t in range(NTS):
                        for dc in range(KC):
                            i = bank()
                            for c in range(KC):
                                mm(i, banks[i][:], wout[:, c, dc * 128:(dc + 1) * 128], hT[:, c, tsls[t]], c == 0, c == KC - 1, [("wout", c), ("mix", c, t, 0), ("mix", c, t, 64)])
                            P.op("dve", lambda e, i=i, dc=dc, t=t: e.tensor_tensor(out=xT[:, dc, tsls[t]], in0=banks[i][:], in1=xT[:, dc, tsls[t]], op=ALU.add),
                                 r=[("bk", i), ("x", dc, t)], w=[("x", dc, t)])
                    P.fence(scratch[:, 3:4])

        if stage == "ffn1":
            ffn_phase(0, 0, 0)
        elif stage == "ffn1n":
            ffn_phase(0, 0, 0)
            rmsnorm(NG - 1, out_fp32=True)
        elif stage.startswith("mix_"):
            mix_phase(0, 1, tuple(stage[4:].split("+")))
        elif stage == "full":
            for l in range(DEPTH):
                ffn_phase(l, 0, 3 * l)
                mix_phase(l, 3 * l + 1)
                ffn_phase(l, 1, 3 * l + 2)
            rmsnorm(NG - 1, out_fp32=True)
        fin = [P.dma("sp", outT_d.rearrange("(k p) t -> p k t", p=128)[:, :, t * TS:(t + 1) * TS], xT[:, :, t * TS:(t + 1) * TS], ("o", t),
                     r=[("x", c, t) for c in range(KC)]) for t in range(NTS)]
        P.emit(final_wait_ops=fin)
    return nc


def _gains(inputs):
    rows = []
    for l in range(DEPTH):
        rows += [inputs["ffn1_norm"][l], inputs["mix_norm"][l], inputs["ffn2_norm"][l]]
    rows.append(inputs["final_norm"])
    g = np.stack([np.asarray(r, np.float32) for r in rows])
    return np.ascontiguousarray(g.reshape(len(rows), KC, 128).transpose(2, 0, 1))


def _constants(half):
    c = {}
    pos = (half * T + np.arange(T)).astype(np.float32)
    inv_freq = (np.float32(10000.0) ** (-(np.arange(0, 32, 2, dtype=np.float32) / np.float32(32)))).astype(np.float32)
    ang = (pos[:, None] * inv_freq[None, :]).astype(np.float32)
    cos, sin = np.cos(ang).astype(np.float32).T, np.sin(ang).astype(np.float32).T
    rope = np.zeros((128, 2, T), np.float32)
    rope[64:80, 0], rope[80:96, 0] = cos, cos
    rope[64:80, 1], rope[80:96, 1] = -sin, sin
    c["rope"] = rope
    k = np.arange(128)[:, None, None]
    r = np.arange(4)[None, :, None]
    q = np.arange(TS)[None, None, :]
    c["maskT"] = np.where(q >= r * 128 + k, 0.0, NEG).astype(np.float32)
    c["ident"] = np.eye(128, dtype=np.float32)
    c["triu"] = np.triu(np.ones((128, 128), np.float32))
    corev = np.zeros((128, 4), np.float32)
    corev[:, 0] = 0.0 if half == 1 else NEG
    corev[:, 1] = 1.0 if half == 1 else 0.0
    corev[:, 2] = 1.0 if half == 0 else 0.0
    c["corev"] = corev
    wins = np.array([2.0, 4.0, 8.0, 16.0], np.float32)
    g_of = (np.arange(2)[None, :] * 2 + (np.arange(128)[:, None] // 64))
    w_of = wins[g_of]
    count = (half * T + np.arange(16) + 1).astype(np.float32)
    c["pcnt"] = np.minimum(count[None, None, :], w_of[:, :, None]).astype(np.float32)
    c["invw"] = (1.0 / w_of).astype(np.float32)
    return c


def _layouts(inputs):
    f = lambda k: np.ascontiguousarray(np.asarray(inputs[k], np.float32))
    m = {k: f(k) for k in ("ffn1_w_gu", "ffn1_w_down", "ffn2_w_gu", "ffn2_w_down", "w_in", "w_q_b", "w_kv_b", "w_out")}
    swap = np.concatenate([np.arange(16, 32), np.arange(0, 16)])
    m["w_in_krot"] = np.ascontiguousarray(m["w_in"][:, :, O_KR:O_KR + 32][:, :, swap])
    qb = m["w_q_b"].reshape(DEPTH, 256, NH, 96)[:, :, :, 64:96][:, :, :, swap]
    m["w_q_b_rot"] = np.ascontiguousarray(qb.reshape(DEPTH, 256, NH * 32))
    m["pool_w"] = np.ascontiguousarray(f("pool_w").reshape(DEPTH, 256, 64))
    mvec = np.zeros((128, DEPTH, 8), np.float32)
    qa, kva, psc, fbf = f("q_a_norm"), f("kv_a_norm"), f("pool_scale"), f("fox_b_f")
    for l in range(DEPTH):
        mvec[:, l, 0], mvec[:, l, 1], mvec[:, l, 2] = qa[l, 0:128], qa[l, 128:256], kva[l]
        for g in range(4):
            mvec[0:64, l, 3 + g] = psc[l, g * 64:(g + 1) * 64]
    m["mvec"] = mvec
    m["fox_b"] = np.ascontiguousarray(np.broadcast_to(fbf[None], (128, DEPTH, NH)))
    m["gains"] = _gains(inputs)
    return m


def kernel(_stage="full", **inputs):
    x = np.asarray(inputs["x"], np.float32)
    nc = build_nc(_stage)
    shared = _layouts(inputs)
    consts = [_constants(0), _constants(1)]
    in_maps = []
    for c in range(NCORES):
        b, h = c // 2, c % 2
        m = dict(shared)
        m.update(consts[h])
        m["xT"] = np.ascontiguousarray(x[b, h * T:(h + 1) * T, :].T)
        in_maps.append(m)
    res = run_bass_kernel_spmd(nc, in_maps, core_ids=list(range(NCORES)))
    out = np.empty((B, S, D), np.float32)
    for c in range(NCORES):
        b, h = c // 2, c % 2
        out[b, h * T:(h + 1) * T, :] = res.results[c]["outT"].T
    return out
```
